# Optimizing a Trainium2 kernel written in Bass

```python
import jax, jax.numpy as jnp
from jax import lax
import numpy as np

D_MODEL = 1024
BATCH = 4
SEQ = 4096
DEPTH = 2

GRID_W = 64
Q_BLOCK = 128
HEAD_DIM = 64
EPS = 1e-6
ROPE_THETA = 10000.0

A_HEADS = 8
A_KV_HEADS = 2
A_GROUPS = A_HEADS // A_KV_HEADS
A_Q = A_HEADS * HEAD_DIM
A_KV = A_KV_HEADS * HEAD_DIM
A_OUT = A_Q

B_HEADS = 8
B_NOPE = 64
B_ROPE = 32
B_V = 64
B_Q_LORA = 256
B_KV_LORA = 128
B_OUT = B_HEADS * B_V

C_HEADS = 16
C_KV_HEADS = 4
C_GROUPS = C_HEADS // C_KV_HEADS
C_Q = C_HEADS * HEAD_DIM
C_KV = C_KV_HEADS * HEAD_DIM
WINDOW = 128

EVEN_IN = A_Q + 2 * A_KV + A_OUT + B_Q_LORA + B_KV_LORA + B_ROPE + B_OUT
EVEN_MIX = A_OUT + B_OUT
ODD_IN = C_Q + 2 * C_KV + C_Q
ODD_MIX = C_Q
N_EVEN = (DEPTH + 1) // 2
N_ODD = DEPTH // 2

kernel_name = "hybrid_gqa_mla_swa_adaln_encoder"


def rms_norm(x, g):
    xf = x.astype(jnp.float32)
    y = xf * lax.rsqrt(jnp.mean(xf * xf, axis=-1, keepdims=True) + EPS)
    return (y * g.astype(jnp.float32)).astype(x.dtype)


def rope_cos_sin(pos, dim):
    inv = ROPE_THETA ** (-jnp.arange(0, dim, 2, dtype=jnp.float32) / dim)
    ang = pos.astype(jnp.float32)[:, None] * inv[None, :]
    return jnp.cos(ang), jnp.sin(ang)


def apply_rope(x, cos, sin):
    half = x.shape[-1] // 2
    x1, x2 = x[..., :half], x[..., half:]
    cos = cos.astype(x.dtype)
    sin = sin.astype(x.dtype)
    return jnp.concatenate([x1 * cos - x2 * sin, x1 * sin + x2 * cos], axis=-1)


def axial_rope(x, cos_r, sin_r, cos_c, sin_c):
    half = HEAD_DIM // 2
    xr = apply_rope(x[..., :half], cos_r[:, None, :], sin_r[:, None, :])
    xc = apply_rope(x[..., half:], cos_c[:, None, :], sin_c[:, None, :])
    return jnp.concatenate([xr, xc], axis=-1)


def to_blocks(t):
    b, s = t.shape[:2]
    return jnp.moveaxis(t.reshape((b, s // Q_BLOCK, Q_BLOCK) + t.shape[2:]), 1, 0)


def from_blocks(t):
    nb, b = t.shape[:2]
    return jnp.moveaxis(t, 0, 1).reshape((b, nb * Q_BLOCK) + t.shape[3:])


def dense_gqa_attention(q, k, v):
    b, s = q.shape[:2]
    scale = HEAD_DIM ** -0.5

    def block(qi):
        sc = jnp.einsum('bqkgd,bskd->bkgqs', qi, k).astype(jnp.float32) * scale
        p = jax.nn.softmax(sc, axis=-1).astype(v.dtype)
        return jnp.einsum('bkgqs,bskd->bqkgd', p, v)

    o = from_blocks(lax.map(block, to_blocks(q)))
    return o.reshape(b, s, A_HEADS * HEAD_DIM)


def mla_attention(q_lat, q_rope, c_kv, k_rope, w_uv):
    b, s = q_lat.shape[:2]
    scale = (B_NOPE + B_ROPE) ** -0.5

    def block(args):
        ql, qr = args
        sc = (jnp.einsum('bqhc,bsc->bhqs', ql, c_kv)
              + jnp.einsum('bqhr,bsr->bhqs', qr, k_rope)).astype(jnp.float32) * scale
        p = jax.nn.softmax(sc, axis=-1).astype(c_kv.dtype)
        return jnp.einsum('bhqs,bsc->bqhc', p, c_kv)

    o_lat = from_blocks(lax.map(block, (to_blocks(q_lat), to_blocks(q_rope))))
    o = jnp.einsum('bshc,chd->bshd', o_lat, w_uv)
    return o.reshape(b, s, B_HEADS * B_V)


def windowed_sink_attention(q, k, v, sink, slopes):
    b, s = q.shape[:2]
    nb = s // Q_BLOCK
    scale = HEAD_DIM ** -0.5
    qb = q.reshape(b, nb, Q_BLOCK, C_KV_HEADS, C_GROUPS, HEAD_DIM)

    def neighbours(t):
        tb = t.reshape(b, nb, Q_BLOCK, C_KV_HEADS, HEAD_DIM)
        tp = jnp.pad(tb, ((0, 0), (1, 1), (0, 0), (0, 0), (0, 0)))
        return jnp.concatenate([tp[:, :-2], tp[:, 1:-1], tp[:, 2:]], axis=2)

    kb, vb = neighbours(k), neighbours(v)
    rel = jnp.arange(3 * Q_BLOCK)[None, :] - Q_BLOCK - jnp.arange(Q_BLOCK)[:, None]
    key_pos = (jnp.arange(nb)[:, None] - 1) * Q_BLOCK + jnp.arange(3 * Q_BLOCK)[None, :]
    valid = (jnp.abs(rel) <= WINDOW)[None] & ((key_pos >= 0) & (key_pos < s))[:, None, :]
    bias = -slopes.reshape(C_KV_HEADS, C_GROUPS)[:, :, None, None] * jnp.abs(rel).astype(jnp.float32)
    sc = jnp.einsum('bnqkgd,bnskd->bnkgqs', qb, kb).astype(jnp.float32) * scale + bias
    sc = jnp.where(valid[None, :, None, None], sc, -jnp.inf)
    sink_l = sink.astype(jnp.float32).reshape(C_KV_HEADS, C_GROUPS)[None, None, :, :, None, None]
    m = jnp.maximum(jnp.max(sc, axis=-1, keepdims=True), sink_l)
    p = jnp.exp(sc - m)
    denom = jnp.sum(p, axis=-1, keepdims=True) + jnp.exp(sink_l - m)
    o = jnp.einsum('bnkgqs,bnskd->bnqkgd', (p / denom).astype(v.dtype), vb)
    return o.reshape(b, s, C_HEADS * HEAD_DIM)


def even_mixer(h, w_in, q_norm_a, k_norm_a, q_lora_norm, kv_lora_norm, w_uq, w_uk, w_uv, w_out,
               cos_r, sin_r, cos_c, sin_c, cos_t, sin_t):
    b, s, _ = h.shape
    proj = h @ w_in
    splits = list(np.cumsum([A_Q, A_KV, A_KV, A_OUT, B_Q_LORA, B_KV_LORA, B_ROPE]))
    qa, ka, va, ga, cq, ckv, kr, gb = jnp.split(proj, splits, axis=-1)
    qa = axial_rope(rms_norm(qa.reshape(b, s, A_HEADS, HEAD_DIM), q_norm_a), cos_r, sin_r, cos_c, sin_c)
    ka = axial_rope(rms_norm(ka.reshape(b, s, A_KV_HEADS, HEAD_DIM), k_norm_a), cos_r, sin_r, cos_c, sin_c)
    oa = dense_gqa_attention(qa.reshape(b, s, A_KV_HEADS, A_GROUPS, HEAD_DIM), ka,
                             va.reshape(b, s, A_KV_HEADS, HEAD_DIM))
    oa = oa * jax.nn.silu(ga)
    cq = rms_norm(cq, q_lora_norm)
    ckv = rms_norm(ckv, kv_lora_norm)
    qb = (cq @ w_uq).reshape(b, s, B_HEADS, B_NOPE + B_ROPE)
    q_nope, q_rope = qb[..., :B_NOPE], qb[..., B_NOPE:]
    q_rope = apply_rope(q_rope, cos_t[:, None, :], sin_t[:, None, :])
    k_rope = apply_rope(kr, cos_t, sin_t)
    q_lat = jnp.einsum('bshd,chd->bshc', q_nope, w_uk)
    ob = mla_attention(q_lat, q_rope, ckv, k_rope, w_uv) * jax.nn.silu(gb)
    return jnp.concatenate([oa, ob], axis=-1) @ w_out


def odd_mixer(h, w_in, sink, w_out, slopes):
    b, s, _ = h.shape
    proj = h @ w_in
    qc, kc, vc, gc = jnp.split(proj, [C_Q, C_Q + C_KV, C_Q + 2 * C_KV], axis=-1)
    oc = windowed_sink_attention(qc.reshape(b, s, C_KV_HEADS, C_GROUPS, HEAD_DIM),
                                 kc.reshape(b, s, C_KV_HEADS, HEAD_DIM),
                                 vc.reshape(b, s, C_KV_HEADS, HEAD_DIM), sink, slopes)
    return (oc * jax.nn.silu(gc)) @ w_out


def setup_inputs(seed: int = 0) -> dict:
    key = jax.random.key(seed)
    ks = jax.random.split(key, 20)
    f32 = jnp.float32
    nrm = lambda k, shape, s: jax.random.normal(k, shape, f32) * s
    gain = lambda k, shape: 1.0 + 0.02 * jax.random.normal(k, shape, f32)
    return {
        "x": nrm(ks[0], (BATCH, SEQ, D_MODEL), 1.0),
        "c": nrm(ks[1], (BATCH, D_MODEL), 1.0),
        "norm_w": gain(ks[2], (DEPTH, D_MODEL)),
        "ada_w": nrm(ks[3], (DEPTH, D_MODEL, 3 * D_MODEL), 0.02),
        "ada_b": nrm(ks[4], (DEPTH, 3 * D_MODEL), 0.02),
        "even_w_in": nrm(ks[5], (N_EVEN, D_MODEL, EVEN_IN), D_MODEL ** -0.5),
        "a_q_norm": gain(ks[6], (N_EVEN, HEAD_DIM)),
        "a_k_norm": gain(ks[7], (N_EVEN, HEAD_DIM)),
        "b_q_lora_norm": gain(ks[8], (N_EVEN, B_Q_LORA)),
        "b_kv_lora_norm": gain(ks[9], (N_EVEN, B_KV_LORA)),
        "b_w_uq": nrm(ks[10], (N_EVEN, B_Q_LORA, B_HEADS * (B_NOPE + B_ROPE)), B_Q_LORA ** -0.5),
        "b_w_uk": nrm(ks[11], (N_EVEN, B_KV_LORA, B_HEADS, B_NOPE), B_KV_LORA ** -0.5),
        "b_w_uv": nrm(ks[12], (N_EVEN, B_KV_LORA, B_HEADS, B_V), B_KV_LORA ** -0.5),
        "even_w_out": nrm(ks[13], (N_EVEN, EVEN_MIX, D_MODEL), EVEN_MIX ** -0.5),
        "odd_w_in": nrm(ks[14], (N_ODD, D_MODEL, ODD_IN), D_MODEL ** -0.5),
        "c_sink": nrm(ks[15], (N_ODD, C_HEADS), 0.5),
        "odd_w_out": nrm(ks[16], (N_ODD, ODD_MIX, D_MODEL), ODD_MIX ** -0.5),
        "final_norm": gain(ks[17], (D_MODEL,)),
    }


def reference(x, c, norm_w, ada_w, ada_b, even_w_in, a_q_norm, a_k_norm, b_q_lora_norm,
              b_kv_lora_norm, b_w_uq, b_w_uk, b_w_uv, even_w_out, odd_w_in, c_sink, odd_w_out,
              final_norm):
    s = x.shape[1]
    rows = s // GRID_W
    row = jnp.repeat(jnp.arange(rows), GRID_W)
    col = jnp.tile(jnp.arange(GRID_W), rows)
    tok = jnp.arange(s)
    cos_r, sin_r = rope_cos_sin(row, HEAD_DIM // 2)
    cos_c, sin_c = rope_cos_sin(col, HEAD_DIM // 2)
    cos_t, sin_t = rope_cos_sin(tok, B_ROPE)
    slopes = 2.0 ** (-8.0 * jnp.arange(1, C_HEADS + 1, dtype=jnp.float32) / C_HEADS)
    c_act = jax.nn.silu(c)
    for layer in range(DEPTH):
        mod = c_act @ ada_w[layer] + ada_b[layer]
        shift, scale, gate = jnp.split(mod, 3, axis=-1)
        h = rms_norm(x, norm_w[layer]) * (1.0 + scale[:, None, :]) + shift[:, None, :]
        if layer % 2 == 0:
            i = layer // 2
            y = even_mixer(h, even_w_in[i], a_q_norm[i], a_k_norm[i], b_q_lora_norm[i],
                           b_kv_lora_norm[i], b_w_uq[i], b_w_uk[i], b_w_uv[i], even_w_out[i],
                           cos_r, sin_r, cos_c, sin_c, cos_t, sin_t)
        else:
            i = layer // 2
            y = odd_mixer(h, odd_w_in[i], c_sink[i], odd_w_out[i], slopes)
        x = x + gate[:, None, :] * y
    return rms_norm(x, final_norm)
```

```python
import contextlib
import numpy as np
import concourse.bass as bass
import concourse.mybir as mybir
from concourse.bass_utils import run_bass_kernel_spmd

F32 = mybir.dt.float32
BF16 = mybir.dt.bfloat16
AF = mybir.ActivationFunctionType
ALU = mybir.AluOpType
AX = mybir.AxisListType

D = 1024
SEQ = 4096
NT = 32
NQ = 17
TQ = NQ * 128
TO = 2048
EPS = 1e-6
SCALE_A = 64 ** -0.5
SCALE_B = 96 ** -0.5
SCALE_C = 64 ** -0.5
BIGD = 1.0e5


class _Ins:
    __slots__ = ("eng", "idx", "fn", "deps", "dma", "signal", "sem", "val", "tag", "rw")

    def __init__(self, eng, idx, fn, deps, dma):
        self.eng, self.idx, self.fn, self.deps, self.dma = eng, idx, fn, deps, dma
        self.signal = dma
        self.sem = None
        self.val = 0


class Sched:
    ENGS = ["pe", "act", "dve", "pool", "sp"]
    NDS = 8

    def __init__(self):
        self.ins = {e: [] for e in self.ENGS}
        self.last_w = {}
        self.readers = {}
        self.bar = set()
        self.bar_done = {e: True for e in self.ENGS}
        self.tag = ""
        self.names = {}

    def barrier(self):
        deps = set()
        for e in self.ENGS:
            lst = self.ins[e]
            last_c = None
            nd = 0
            for i in range(len(lst) - 1, -1, -1):
                t = lst[i]
                if t.dma:
                    if nd < self.NDS:
                        deps.add((e, i))
                        nd += 1
                elif last_c is None:
                    last_c = i
                    deps.add((e, i))
                if nd >= self.NDS and last_c is not None:
                    break
        for (e, i) in deps:
            self.ins[e][i].signal = True
        self.bar = deps
        self.bar_done = {e: False for e in self.ENGS}
        self.last_w = {}
        self.readers = {}

    def add(self, eng, fn, reads=(), writes=(), dma=False):
        lst = self.ins[eng]
        idx = len(lst)
        deps = set()
        for k in list(reads) + list(writes):
            w = self.last_w.get(k)
            if w is not None:
                deps.add(w)
        for k in writes:
            for rk, i in self.readers.get(k, {}).items():
                e = rk if isinstance(rk, str) else rk[1]
                deps.add((e, i))
        final = set()
        for (e, i) in deps:
            t = self.ins[e][i]
            if e == eng and eng == "pe" and not t.dma and not dma:
                continue
            t.signal = True
            final.add((e, i))
        if not self.bar_done[eng]:
            final |= {d for d in self.bar if not (d[0] == eng and eng == "pe" and not self.ins[d[0]][d[1]].dma)}
            self.bar_done[eng] = True
        ins = _Ins(eng, idx, fn, final, dma)
        ins.tag = self.tag
        ins.rw = (tuple(reads), tuple(writes))
        lst.append(ins)
        for k in writes:
            self.last_w[k] = (eng, idx)
            self.readers[k] = {}
        for k in reads:
            d = self.readers.setdefault(k, {})
            if dma:
                d[("dma", eng, idx)] = idx
            else:
                d[eng] = max(d.get(eng, -1), idx)
        return ins

    def emit(self, nc, final_wait_keys=()):
        with contextlib.ExitStack() as st:
            csem = {e: st.enter_context(nc.semaphore("c_" + e)) for e in self.ENGS if e != "sp"}
            dsem = {e: [st.enter_context(nc.semaphore(f"d_{e}{j}")) for j in range(self.NDS)]
                    for e in ("sp", "pool", "act")}
            for e in self.ENGS:
                cc = 0
                dc = 0
                for t in self.ins[e]:
                    if t.dma:
                        t.sem = dsem[e][dc % self.NDS]
                        t.val = 16 * (dc // self.NDS + 1)
                        dc += 1
                    elif t.signal:
                        cc += 1
                        t.sem = csem[e]
                        t.val = cc
            final_deps = set()
            for k in final_wait_keys:
                w = self.last_w.get(k)
                if w is not None:
                    final_deps.add(w)
            block = st.enter_context(nc.Block())

            def run(engh, e):
                known = {}

                def waits_raw(sem, v):
                    key = id(sem)
                    if known.get(key, 0) >= v:
                        return
                    engh.wait_ge(sem, v)
                    known[key] = v

                def waits(deps):
                    need = {}
                    for (de, di) in deps:
                        d = self.ins[de][di]
                        key = id(d.sem)
                        if known.get(key, 0) >= d.val:
                            continue
                        if key not in need or need[key][1] < d.val:
                            need[key] = (d.sem, d.val)
                    for key, (s, v) in need.items():
                        engh.wait_ge(s, v)
                        known[key] = v

                for t in self.ins[e]:
                    waits(t.deps)
                    if t.dma and t.val > 16:
                        waits_raw(t.sem, t.val - 16)
                    r = t.fn(engh)
                    try:
                        self.names[r.ins.name] = (t.eng, t.idx, t.tag, t.rw)
                    except Exception:
                        pass
                    if t.signal:
                        r.then_inc(t.sem, 16 if t.dma else 1)
                if e == "sp":
                    waits(final_deps)

            @block.tensor
            def _(h):
                run(h, "pe")

            @block.scalar
            def _(h):
                run(h, "act")

            @block.vector
            def _(h):
                run(h, "dve")

            @block.gpsimd
            def _(h):
                run(h, "pool")

            @block.sync
            def _(h):
                run(h, "sp")


class _Stop(Exception):
    pass


def build_program(debug=False, stage=99):
    nc = bass.Bass("TRN2", target_bir_lowering=False)
    S = Sched()

    def din(name, shape):
        return nc.dram_tensor(name, list(shape), F32, kind="ExternalInput").ap()

    x_d = din("x_loc", [SEQ, D])
    c_d = din("c_col", [128, 8])
    adaw_d = din("ada_w", [2, 12, 128, 8 * 256])
    adab_d = din("ada_b", [2, 3 * D])
    normw_d = din("norm_w", [2, D])
    fnorm_d = din("final_norm", [D])
    ident_d = din("ident", [128, 128])
    wk0_d = din("wk0", [128, 8 * 672])
    wA_d = din("wA", [4, 128, 8 * 256])
    wgb_d = din("wgb", [4, 128, 8 * 128])
    wuq_d = din("wuq", [128, 2 * 768])
    wuk_d = din("wuk", [128, 512])
    wuv_d = din("wuv", [128, 512])
    wout0_d = din("wout0", [128, 8 * D])
    aqn_d = din("aqn", [64])
    akn_d = din("akn", [64])
    bqn_d = din("bqn", [256])
    bkvn_d = din("bkvn", [128])
    tab_d = din("tab", [NT, 128, 192])
    tabqa_d = din("tabqa", [128, NQ * 128])
    tabqb_d = din("tabqb", [128, NQ * 64])
    wkv1_d = din("wkv1", [128, 8 * 512])
    wqg1_d = din("wqg1", [8, 128, 8 * 256])
    wout1_d = din("wout1", [128, 8 * D])
    sink_d = din("sink", [16])
    dtab_d = din("dtab", [128, 384])
    y_d = nc.dram_tensor("y", [TO, D], F32, kind="ExternalOutput").ap()
    dbg_d = None
    if debug:
        dbg_d = nc.dram_tensor("dbg", [TQ, D], F32, kind="ExternalOutput").ap()
        dbg2_d = nc.dram_tensor("dbg2", [128, 8 * TQ], BF16, kind="ExternalOutput").ap()
        dbg3_d = nc.dram_tensor("dbg3", [128, 3 * SEQ + NT * 192 + 2 * TQ], BF16, kind="ExternalOutput").ap()
        dbg4_d = nc.dram_tensor("dbg4", [128, 3 * D], F32, kind="ExternalOutput").ap()

    def mm(out, lhsT, rhs, start, stop, r, w, skip=False):
        if skip:
            S.add("pe", lambda e: e.matmul(out, lhsT=lhsT, rhs=rhs, start=start, stop=stop, skip_group_check=True), r, w)
        else:
            S.add("pe", lambda e: e.matmul(out, lhsT=lhsT, rhs=rhs, start=start, stop=stop), r, w)

    def tp(out, in_, r, w):
        S.add("pe", lambda e: e.transpose(out, in_, ident[:]), list(r) + ["ident"], w)

    def act(out, in_, func, r, w, bias=None, scale=None, accum=None):
        kw = {}
        if bias is not None:
            kw["bias"] = bias
        if scale is not None:
            kw["scale"] = scale
        if accum is not None:
            kw["accum_out"] = accum
        S.add("act", lambda e: e.activation(out=out, in_=in_, func=func, **kw), r, w)

    def tt(eng, out, in0, in1, op, r, w):
        S.add(eng, lambda e: e.tensor_tensor(out=out, in0=in0, in1=in1, op=op), r, w)

    def ts(eng, out, in0, s1, s2, op0, op1, r, w):
        if s2 is None:
            S.add(eng, lambda e: e.tensor_scalar(out=out, in0=in0, scalar1=s1, scalar2=None, op0=op0), r, w)
        else:
            S.add(eng, lambda e: e.tensor_scalar(out=out, in0=in0, scalar1=s1, scalar2=s2, op0=op0, op1=op1), r, w)

    def stt(eng, out, in0, scalar, in1, op0, op1, r, w):
        S.add(eng, lambda e: e.scalar_tensor_tensor(out=out, in0=in0, scalar=scalar, in1=in1, op0=op0, op1=op1), r, w)

    def cp(eng, out, in_, r, w):
        if eng == "act":
            S.add("act", lambda e: e.activation(out=out, in_=in_, func=AF.Copy), r, w)
        else:
            S.add(eng, lambda e: e.tensor_copy(out=out, in_=in_), r, w)

    def red(out, in_, r, w):
        S.add("dve", lambda e: e.tensor_reduce(out=out, in_=in_, axis=AX.X, op=ALU.add), r, w)

    def rcp(out, in_, r, w):
        S.add("dve", lambda e: e.reciprocal(out=out, in_=in_), r, w)

    def mset(eng, ap, val, w):
        S.add(eng, lambda e: e.memset(ap, val), (), w)

    def dma(q, out, in_, r, w):
        S.add(q, lambda e: e.dma_start(out=out, in_=in_), r, w, dma=True)

    def rstd(dst, ss, n, r, w):
        act(dst, ss, AF.Ln, r, w, bias=EPS, scale=1.0 / n)
        act(dst, dst, AF.Exp, w, w, scale=-0.5)

    with contextlib.ExitStack() as outer:
        ARENA_WORDS = 52992
        arena_t = outer.enter_context(nc.sbuf_tensor("arena", [128, ARENA_WORDS], F32))
        A = {"top": 0, "peak": 0}

        def sb(name, shape, dt, st=None):
            n = 1
            for d_ in shape[1:]:
                n *= int(d_)
            nbytes = n * (2 if dt == BF16 else 4)
            w = (nbytes + 31) // 32 * 8
            off = A["top"]
            A["top"] += w
            assert A["top"] <= ARENA_WORDS, ("SBUF arena overflow", name, A["top"])
            A["peak"] = max(A["peak"], A["top"])
            v = arena_t[:, off:off + w]
            if dt == BF16:
                v = v.bitcast(BF16)
            v = v[:, 0:n]
            if len(shape) == 3:
                v = v.rearrange("p (a b) -> p a b", a=int(shape[1]))
            elif len(shape) == 4:
                v = v.rearrange("p (a b c) -> p a b c", a=int(shape[1]), b=int(shape[2]))
            return v

        def mark():
            return A["top"]

        def release(m):
            A["top"] = m

        pp = [outer.enter_context(nc.psum_tensor(f"pp{i}", [128, 1024], F32)) for i in range(4)]

        def pk(i, h):
            return f"pp{i}{'ab'[h]}"

        okeys = []
        try:
            ident = sb("ident", [128, 128], BF16)
            c_sb = sb("c_sb", [128, 8], F32)
            gate_bc = sb("gate_bc", [128, D], F32)
            junk = sb("junk", [128, D], BF16)
            stat = sb("stat", [128, 64], F32)
            mixT = sb("mixT", [128, 8, TQ], BF16)

            dma("pool", ident[:], ident_d, (), ["ident"])
            dma("sp", c_sb[:], c_d, (), ["c_sb"])
            act(c_sb[:], c_sb[:], AF.Silu, ["c_sb"], ["c_sb"])

            def emit_mods(layer, g_bc, sh_bc, SW, nslots=2, queues=("sp",)):
                m = mark()
                ones_f = sb("ones_f", [128, 128], F32)
                crep = sb("crep", [128, 8, 128], F32)
                aw = [sb(f"aw{i}", [128, 8, SW], F32) for i in range(nslots)]
                bb = sb("adab", [128, SW], F32)
                nw = sb("nw", [128, D], F32)
                mset("dve", ones_f[:], 1.0, ["ones_f"])
                for k in range(8):
                    ts("dve", crep[:, k, :], ones_f[:], c_sb[:, k:k + 1], None, ALU.mult, None,
                       ["ones_f", "c_sb"], [f"crep{k}"])
                dma("sp", nw[:], normw_d[layer].partition_broadcast(128), (), ["nw"])
                for s in range(3 * D // SW):
                    sl = s % nslots
                    kind = (s * SW) // D
                    c0 = (s * SW) % D
                    dma(queues[s % len(queues)], aw[sl][:].rearrange("p k n -> p (k n)"), adaw_d[layer][s], (), [f"aw{sl}"])
                    dma("sp", bb[:], adab_d[layer][s * SW:(s + 1) * SW].partition_broadcast(128), (), ["adab"])
                    ps = pp[s % 2][:, 0:SW]
                    for k in range(8):
                        mm(ps, crep[:, k, :], aw[sl][:, k, :], k == 0, k == 7, [f"crep{k}", f"aw{sl}"], [pk(s % 2, 0)])
                    col = slice(c0, c0 + SW)
                    if kind == 0:
                        tt("dve", sh_bc[:, col], ps, bb[:], ALU.add, [pk(s % 2, 0), "adab"], [f"modw{s}"])
                    elif kind == 1:
                        tt("dve", g_bc[:, col], ps, bb[:], ALU.add, [pk(s % 2, 0), "adab"], [f"modw{s}"])
                        stt("dve", g_bc[:, col], g_bc[:, col], 1.0, nw[:, col], ALU.add, ALU.mult,
                            [f"modw{s}", "nw"], [f"modw{s}"])
                    else:
                        tt("dve", gate_bc[:, col], ps, bb[:], ALU.add, [pk(s % 2, 0), "adab"], [f"modw{s}"])
                S.barrier()
                release(m)

            def norm_rope64(src, nt, nh, gain_bc, cos3, sin3, dst, scr, r, w, kp, tk=(), e2="pool"):
                n = nt * nh
                W = n * 64
                sq, qn, t1, t2 = scr["a"], scr["b"], scr["c"], scr["d"]
                ss = scr["st"]
                tt(e2, sq[:, :W], src, src, ALU.mult, r, [kp + "sq"])
                red(ss[:, :n], sq[:, :W].rearrange("p (n d) -> p n d", d=64), [kp + "sq"], [kp + "ss"])
                rstd(ss[:, :n], ss[:, :n], 64.0, [kp + "ss"], [kp + "ss"])
                tt("dve", qn[:, :W].rearrange("p (n d) -> p n d", d=64), src.rearrange("p (n d) -> p n d", d=64),
                   ss[:, :n].unsqueeze(2).broadcast_to([128, n, 64]), ALU.mult, list(r) + [kp + "ss"], [kp + "qn"])
                tt(e2, qn[:, :W].rearrange("p (n d) -> p n d", d=64), qn[:, :W].rearrange("p (n d) -> p n d", d=64),
                   gain_bc[:].unsqueeze(1).broadcast_to([128, n, 64]), ALU.mult, [kp + "qn", "gains"], [kp + "qn"])
                q4 = qn[:, :W].rearrange("p (t h d) -> p t h d", t=nt, h=nh)
                tt(e2, t1[:, :W].rearrange("p (t h d) -> p t h d", t=nt, h=nh), q4,
                   cos3.unsqueeze(2).broadcast_to([128, nt, nh, 64]), ALU.mult, [kp + "qn", "tabs"] + list(tk), [kp + "t1"])
                q6 = qn[:, :W].rearrange("p (t h x r i) -> p t h x r i", t=nt, h=nh, x=2, r=2)
                o6 = t2[:, :W].rearrange("p (t h x r i) -> p t h x r i", t=nt, h=nh, x=2, r=2)
                s5 = sin3.rearrange("p t (x r i) -> p t x r i", x=2, r=2)
                k = 0
                for h in range(nh):
                    for rr in range(2):
                        eng = e2 if k % 2 == 0 else "dve"
                        k += 1
                        tt(eng, o6[:, :, h, :, rr, :], q6[:, :, h, :, 1 - rr, :], s5[:, :, :, rr, :], ALU.mult,
                           [kp + "qn", "tabs"] + list(tk), [kp + f"t2_{h}{rr}"])
                tt("dve", dst, t1[:, :W], t2[:, :W], ALU.add,
                   [kp + "t1"] + [kp + f"t2_{h}{rr}" for h in range(nh) for rr in range(2)], w)

            def rope32(src3, a, cb, sbn, dst3, bufs, r, w, kp):
                u1 = bufs[0][:, :a * 32].rearrange("p (a d) -> p a d", d=32)
                u2 = bufs[1][:, :a * 32].rearrange("p (a d) -> p a d", d=32)
                tt("pool", u1, src3, cb, ALU.mult, list(r) + ["tabs"], [kp + "u1"])
                tt("pool", u2[:, :, 0:16], src3[:, :, 16:32], sbn[:, :, 0:16], ALU.mult, list(r) + ["tabs"], [kp + "u2a"])
                tt("dve", u2[:, :, 16:32], src3[:, :, 0:16], sbn[:, :, 16:32], ALU.mult, list(r) + ["tabs"], [kp + "u2b"])
                tt("dve", dst3, u1, u2, ALU.add, [kp + "u1", kp + "u2a", kp + "u2b"], w)

            MODK = []

            def hT_pre(src_ap, src_keys, s, xm_ap, xm_key, xs_ap, g_bc, sh_bc):
                ss = stat[:, s:s + 1]
                act(junk[:], src_ap, AF.Square, src_keys, ["junk", f"ss{s}"], accum=ss)
                rstd(ss, ss, float(D), [f"ss{s}"], [f"ss{s}"])
                stt("dve", xm_ap, src_ap, ss, g_bc[:], ALU.mult, ALU.mult, list(src_keys) + [f"ss{s}"], [xm_key])
                tt("dve", xs_ap, xm_ap, sh_bc[:], ALU.add, [xm_key], [f"xs{s}"])

            def hT_tr(s, xs_ap, dst, dkeys):
                pT = pp[2 * s][:, 0:512].bitcast(BF16)
                for c in range(8):
                    tp(pT[:, c * 128:(c + 1) * 128], xs_ap[:, c * 128:(c + 1) * 128], [f"xs{s}"], [pk(2 * s, 0)])
                cp("act", dst, pT.rearrange("p (c n) -> p c n", c=8), [pk(2 * s, 0)], dkeys)

            mL0 = mark()
            g_bc = sb("g_bc", [128, D], F32)
            sh_bc = sb("sh_bc", [128, D], F32)
            emit_mods(0, g_bc, sh_bc, 256, nslots=4, queues=("sp", "act"))
            if stage <= 0:
                raise _Stop()
            hTq = sb("hTq", [128, 8, TQ], BF16)
            tabB = sb("tabB", [128, NQ, 64], F32)
            aqn = sb("aqn", [128, 64], F32)
            akn = sb("akn", [128, 64], F32)
            bqn = sb("bqn", [128, 256], F32)
            bkvn = sb("bkvn", [128, 128], F32)
            ckvnT = sb("ckvnT", [128, SEQ], BF16)
            kropeT = sb("kropeT", [128, SEQ], BF16)
            cqnT = sb("cqnT", [128, 2, TQ], BF16)
            mLA = mark()
            KTA = sb("KTA", [128, SEQ], BF16)
            VA = sb("VA", [128, NT, 192], BF16)
            tabA = sb("tabA", [128, NQ, 128], F32)

            dma("sp", tabA[:].rearrange("p t d -> p (t d)"), tabqa_d, (), ["tabs"])
            dma("sp", tabB[:].rearrange("p t d -> p (t d)"), tabqb_d, (), ["tabs"])
            dma("sp", aqn[:], aqn_d.partition_broadcast(128), (), ["gains"])
            dma("sp", akn[:], akn_d.partition_broadcast(128), (), ["gains"])
            dma("sp", bqn[:], bqn_d.partition_broadcast(128), (), ["gains"])
            dma("sp", bkvn[:], bkvn_d.partition_broadcast(128), (), ["gains"])
            mset("pool", VA[:], 1.0, ["VA_init"])

            mP1 = mark()
            wk0 = sb("wk0", [128, 8, 672], BF16)
            xt = [sb(f"xt{i}", [128, D], F32) for i in range(2)]
            xs = [sb(f"xs{i}", [128, D], BF16) for i in range(2)]
            ks = [sb(f"ks{i}", [128, 672], F32) for i in range(3)]
            kbf = [sb(f"kbf{i}", [128, 544], BF16) for i in range(3)]
            hTt = sb("hTt", [128, 3, 8, 128], BF16)
            tabK = [sb(f"tabK{i}", [128, 192], F32) for i in range(3)]
            scr1 = [dict(a=sb("s1a", [128, 128], F32), b=sb("s1b", [128, 128], F32), c=sb("s1c", [128, 128], F32),
                         d=sb("s1d", [128, 128], F32), st=sb("s1s", [128, 8], F32), e=sb("s1e", [128, 32], F32),
                         f=sb("s1f", [128, 32], F32)) for i in range(3)]
            dma("pool", wk0[:].rearrange("p c n -> p (c n)"), wk0_d, (), ["wk0"])

            def p1_ctx(t):
                s = t % 2
                u = t % 3
                isq = t < NQ
                if isq:
                    hdst = hTq[:, :, t * 128:(t + 1) * 128]
                    hk = [f"hTq{t}"]
                    cA, sA = tabA[:, t:t + 1, 0:64], tabA[:, t:t + 1, 64:128]
                    cB, sB = tabB[:, t:t + 1, 0:32], tabB[:, t:t + 1, 32:64]
                    tk = []
                else:
                    hdst = hTt[:, u, :, :]
                    hk = [f"hTt{u}"]
                    cA, sA = tabK[u][:, 0:64].unsqueeze(1), tabK[u][:, 64:128].unsqueeze(1)
                    cB, sB = tabK[u][:, 128:160].unsqueeze(1), tabK[u][:, 160:192].unsqueeze(1)
                    tk = [f"tabK{u}"]
                return s, u, isq, hdst, hk, cA, sA, cB, sB, tk

            def p1_A1(t):
                S.tag = f'p1A1 t{t}'
                s, u, isq, hdst, hk, cA, sA, cB, sB, tk = p1_ctx(t)
                dma("sp", xt[s][:], x_d[t * 128:(t + 1) * 128, :], (), [f"xt{s}"])
                hT_pre(xt[s][:], [f"xt{s}"], s, xt[s][:], f"xt{s}", xs[s][:], g_bc, sh_bc)

            def p1_A2(t):
                S.tag = f'p1A2 t{t}'
                s, u, isq, hdst, hk, cA, sA, cB, sB, tk = p1_ctx(t)
                hT_tr(s, xs[s][:], hdst, hk)

            def p1_A3(t):
                S.tag = f'p1A3 t{t}'
                s, u, isq, hdst, hk, cA, sA, cB, sB, tk = p1_ctx(t)
                if not isq:
                    dma("sp", tabK[u][:], tab_d[t], (), [f"tabK{u}"])
                pj = pp[2 * s + 1]
                for c in range(8):
                    mm(pj[:, 0:416], hdst[:, c, :], wk0[:, c, 0:416], c == 0, c == 7, hk + ["wk0"], [pk(2 * s + 1, 0)])
                cp("act", ks[u][:, 0:416], pj[:, 0:416], [pk(2 * s + 1, 0)], [f"ksa{u}"])
                if isq:
                    for c in range(8):
                        mm(pj[:, 512:768], hdst[:, c, :], wk0[:, c, 416:672], c == 0, c == 7, hk + ["wk0"], [pk(2 * s + 1, 1)])
                    cp("act", ks[u][:, 416:672], pj[:, 512:768], [pk(2 * s + 1, 1)], [f"ksb{u}"])

            def p1_B(t):
                S.tag = f'p1B t{t}'
                s, u, isq, hdst, hk, cA, sA, cB, sB, tk = p1_ctx(t)
                sc = scr1[u]
                norm_rope64(ks[u][:, 0:128], 1, 2, akn, cA, sA, kbf[u][:, 0:128], sc,
                            [f"ksa{u}"] + tk, [f"kbfa{u}"], f"ka{u}", tk=tk)
                cp("pool", VA[:, t, :].rearrange("p (a b) -> p a b", b=64)[:, 0:3:2, :],
                   ks[u][:, 128:256].rearrange("p (a b) -> p a b", b=64), [f"ksa{u}", "VA_init"], [f"VA{t}"])
                ssc = stat[:, 4 + u:5 + u]
                act(junk[:, 0:128], ks[u][:, 256:384], AF.Square, [f"ksa{u}"], ["junk", f"ssc{u}"], accum=ssc)
                rstd(ssc, ssc, 128.0, [f"ssc{u}"], [f"ssc{u}"])
                stt("dve", kbf[u][:, 128:256], ks[u][:, 256:384], ssc, bkvn[:], ALU.mult, ALU.mult,
                    [f"ksa{u}", f"ssc{u}", "gains"], [f"kbfb{u}"])
                rope32(ks[u][:, 384:416].unsqueeze(1), 1, cB, sB, kbf[u][:, 256:288].unsqueeze(1), (sc["e"], sc["f"]),
                       [f"ksa{u}"] + tk, [f"kbfc{u}"], f"kr{u}")
                pS = pp[2 * s][:, 512:1024].bitcast(BF16)
                tp(pS[:, 0:128], kbf[u][:, 0:128], [f"kbfa{u}"], [pk(2 * s, 1)])
                tp(pS[:, 192:320], kbf[u][:, 128:256], [f"kbfb{u}"], [pk(2 * s, 1)])
                if isq:
                    ssq = stat[:, 8 + u:9 + u]
                    act(junk[:, 0:256], ks[u][:, 416:672], AF.Square, [f"ksb{u}"], ["junk", f"ssq{u}"], accum=ssq)
                    rstd(ssq, ssq, 256.0, [f"ssq{u}"], [f"ssq{u}"])
                    stt("dve", kbf[u][:, 288:544], ks[u][:, 416:672], ssq, bqn[:], ALU.mult, ALU.mult,
                        [f"ksb{u}", f"ssq{u}", "gains"], [f"kbfd{u}"])
                    tp(pS[:, 384:512], kbf[u][:, 288:416], [f"kbfd{u}"], [pk(2 * s, 1)])
                    tp(pS[:, 576:704], kbf[u][:, 416:544], [f"kbfd{u}"], [pk(2 * s, 1)])
                tp(pS[0:32, 768:896], kbf[u][:, 256:288], [f"kbfc{u}"], [pk(2 * s, 1)])
                cp("dve", KTA[:, t * 128:(t + 1) * 128], pS[:, 0:128], [pk(2 * s, 1)], [f"KTA{t}"])
                cp("dve", ckvnT[:, t * 128:(t + 1) * 128], pS[:, 192:320], [pk(2 * s, 1)], [f"ckvnT{t}"])
                cp("dve", kropeT[64:96, t * 128:(t + 1) * 128], pS[0:32, 768:896], [pk(2 * s, 1)], [f"kropeT{t}"])
                if isq:
                    cp("dve", cqnT[:, :, t * 128:(t + 1) * 128], pS[:, 384:768].rearrange("p (c n) -> p c n", c=2)[:, :, 0:128],
                       [pk(2 * s, 1)], [f"cqnT{t}"])

            for step in range(NT + 4):
                if step < NT:
                    p1_A1(step)
                if 0 <= step - 1 < NT:
                    p1_A2(step - 1)
                if 0 <= step - 2 < NT:
                    p1_A3(step - 2)
                if 0 <= step - 4 < NT:
                    p1_B(step - 4)
            S.barrier()
            if debug:
                dma("sp", dbg2_d, hTq[:].rearrange("p c n -> p (c n)"), (), ["dbg2"])
                dma("sp", dbg3_d[:, 0:SEQ], KTA[:], (), ["dbg3a"])
                dma("sp", dbg3_d[:, SEQ:2 * SEQ], ckvnT[:], (), ["dbg3b"])
                dma("sp", dbg3_d[64:96, 2 * SEQ:3 * SEQ], kropeT[64:96, :], (), ["dbg3c"])
                dma("sp", dbg3_d[:, 3 * SEQ:3 * SEQ + NT * 192], VA[:].rearrange("p t n -> p (t n)"), (), ["dbg3d"])
                dma("sp", dbg3_d[:, 3 * SEQ + NT * 192:], cqnT[:].rearrange("p c n -> p (c n)"), (), ["dbg3e"])
                dma("sp", dbg4_d[:, 0:D], g_bc[:], (), ["dbg4a"])
                dma("sp", dbg4_d[:, D:2 * D], sh_bc[:], (), ["dbg4b"])
                dma("sp", dbg4_d[:, 2 * D:3 * D], gate_bc[:], (), ["dbg4c"])
                S.barrier()
            if stage <= 1:
                raise _Stop()
            release(mP1)

            def attention_l0(qk_l, qk_r, v_l, scale, GT, chunk, kp, PT, rd, fint):
                its = [(qg, kb) for qg in range(5) for kb in range(NT)]
                n = len(its)

                QSZ = [448, 448, 448, 448, 384]
                QOFF = [0, 448, 896, 1344, 1792]

                def geo(i):
                    qg, kb = its[i]
                    return qg, kb, QOFF[qg], QSZ[qg], qg % 2, i % 2, i % 3

                def st_qk(i):
                    qg, kb, q0, nq, osl, ssl, psl = geo(i)
                    psS = pp[ssl]
                    mm(psS[:, 0:nq], qk_l(0, kb), qk_r(0, q0, nq), True, True, [kp + "K", kp + "Kr", kp + "Q"], [pk(ssl, 0)])
                    mm(psS[:, 512:512 + nq], qk_l(1, kb), qk_r(1, q0, nq), True, True, [kp + "K", kp + "Kr", kp + "Q"], [pk(ssl, 1)])

                def st_exp(i):
                    qg, kb, q0, nq, osl, ssl, psl = geo(i)
                    act(PT[psl][:].rearrange("p (h n) -> p h n", h=2)[:, :, 0:nq],
                        pp[ssl][:].rearrange("p (h n) -> p h n", h=2)[:, :, 0:nq], AF.Exp,
                        [pk(ssl, 0), pk(ssl, 1)], [f"PT{psl}"], scale=scale)

                def st_pv(i):
                    qg, kb, q0, nq, osl, ssl, psl = geo(i)
                    psO = pp[2 + osl]
                    mm(psO[:, 0:nq], v_l(0, kb), PT[psl][:, 0:nq], kb == 0, kb == NT - 1, [kp + "V", f"PT{psl}"], [pk(2 + osl, 0)])
                    mm(psO[:, 512:512 + nq], v_l(1, kb), PT[psl][:, 512:512 + nq], kb == 0, kb == NT - 1,
                       [kp + "V", f"PT{psl}"], [pk(2 + osl, 1)])
                    if kb != NT - 1:
                        return
                    r_ = rd[osl]
                    tmp = fint[osl]
                    rcp(r_[64:128, 0:nq], psO[64:128, 0:nq], [pk(2 + osl, 0)], [f"rd{osl}"])
                    rcp(r_[0:64, 0:nq], psO[0:64, 512:512 + nq], [pk(2 + osl, 1)], [f"rd{osl}"])
                    tt("dve", tmp[0:64, 0:nq], psO[0:64, 0:nq], r_[64:128, 0:nq], ALU.mult, [pk(2 + osl, 0), f"rd{osl}"], [f"fin{osl}"])
                    tt("dve", tmp[64:128, 0:nq], psO[64:128, 512:512 + nq], r_[0:64, 0:nq], ALU.mult,
                       [pk(2 + osl, 1), f"rd{osl}"], [f"fin{osl}"])
                    tt("pool", mixT[:, chunk, q0:q0 + nq], tmp[:, 0:nq], GT[:, q0:q0 + nq], ALU.mult,
                       [f"fin{osl}", kp + "G"], [f"mixT{chunk}_{qg}"])

                import os
                l1_, l2_ = [int(v) for v in os.environ.get("LAG_" + kp, "1,2").split(",")]
                for step in range(n + l2_):
                    if step < n:
                        st_qk(step)
                    if 0 <= step - l1_ < n:
                        st_exp(step - l1_)
                    if 0 <= step - l2_ < n:
                        st_pv(step - l2_)

            def gates_fm(wslice, GT, r, kp):
                for qg in range(5):
                    q0 = qg * 512
                    nq = min(512, TQ - q0)
                    i = qg % 2
                    for c in range(8):
                        mm(pp[i][:, 0:nq], wslice(c), hTq[:, c, q0:q0 + nq], c == 0, c == 7, list(r), [pk(i, 0)])
                    act(GT[:, q0:q0 + nq], pp[i][:, 0:nq], AF.Silu, [pk(i, 0)], [kp + "G"])

            mP2 = mark()
            wA = [sb(f"wA{i}", [128, 8, 256], BF16) for i in range(2)]
            QTA = sb("QTA", [128, TQ], BF16)
            GTA = sb("GTA", [128, TQ], BF16)
            qs = [sb(f"qs{i}", [128, 512], F32) for i in range(3)]
            qbf = [sb(f"qbf{i}", [128, 512], BF16) for i in range(3)]
            PT = [sb(f"PT{i}", [128, 1024], BF16) for i in range(3)]
            rd = [sb(f"rd{i}", [128, 512], F32) for i in range(2)]
            fint = [sb(f"fint{i}", [128, 512], F32) for i in range(2)]
            scr2 = dict(a=sb("s2a", [128, 512], F32), b=sb("s2b", [128, 512], F32), c=sb("s2c", [128, 512], F32),
                        d=sb("s2d", [128, 512], F32), st=sb("s2s", [128, 16], F32))
            dma("pool", wA[0][:].rearrange("p c n -> p (c n)"), wA_d[0], (), ["wA0"])
            for g in range(4):
                S.tag = f'p2 g{g}'
                w = wA[g % 2]
                wk = f"wA{g % 2}"
                if g + 1 < 4:
                    dma("pool", wA[(g + 1) % 2][:].rearrange("p c n -> p (c n)"), wA_d[g + 1], (), [f"wA{(g + 1) % 2}"])
                def qa_A(bi, w=w, wk=wk):
                    t0 = bi * 4
                    nt = min(4, NQ - t0)
                    i = (bi + 1) % 2
                    pj = pp[2 + i]
                    for tl in range(nt):
                        t = t0 + tl
                        for c in range(8):
                            mm(pj[:, tl * 128:(tl + 1) * 128], hTq[:, c, t * 128:(t + 1) * 128], w[:, c, 0:128],
                               c == 0, c == 7, [wk], [pk(2 + i, 0)])
                    cp("act", qs[bi % 3][:, 0:nt * 128], pj[:, 0:nt * 128], [pk(2 + i, 0)], [f"qs{bi % 3}"])

                def qa_G(qg, w=w, wk=wk):
                    q0 = qg * 512
                    nq = min(512, TQ - q0)
                    i = qg % 2
                    for c in range(8):
                        mm(pp[i][:, 0:nq], w[:, c, 128:256], hTq[:, c, q0:q0 + nq], c == 0, c == 7, [wk], [pk(i, 0)])
                    act(GTA[:, q0:q0 + nq], pp[i][:, 0:nq], AF.Silu, [pk(i, 0)], ["AG"])

                def qa_B(bi):
                    t0 = bi * 4
                    nt = min(4, NQ - t0)
                    i = (bi + 1) % 2
                    u = bi % 3
                    norm_rope64(qs[u][:, 0:nt * 128], nt, 2, aqn, tabA[:, t0:t0 + nt, 0:64], tabA[:, t0:t0 + nt, 64:128],
                                qbf[u][:, 0:nt * 128], scr2, [f"qs{u}"], [f"qbf{u}"], "qa", e2="dve")
                    pS = pp[2 + i][:, 512:1024].bitcast(BF16)
                    for tl in range(nt):
                        tp(pS[:, tl * 128:(tl + 1) * 128], qbf[u][:, tl * 128:(tl + 1) * 128], [f"qbf{u}"], [pk(2 + i, 1)])
                    cp("dve", QTA[:, t0 * 128:(t0 + nt) * 128], pS[:, 0:nt * 128], [pk(2 + i, 1)], ["AQ"])

                for step in range(7):
                    if step < 5:
                        qa_A(step)
                        qa_G(step)
                    if 0 <= step - 2 < 5:
                        qa_B(step - 2)
                attention_l0(lambda h, kb: KTA[64 * h:64 * h + 64, kb * 128:(kb + 1) * 128],
                             lambda h, q0, nq: QTA[64 * h:64 * h + 64, q0:q0 + nq],
                             lambda h, kb: VA[:, kb, 64 * h:64 * h + 128],
                             SCALE_A, GTA, g, "A", PT, rd, fint)
            S.barrier()
            if stage <= 2:
                raise _Stop()
            release(mLA)

            wuq = sb("wuq", [128, 2, 768], BF16)
            wuk = sb("wuk", [128, 512], BF16)
            wuv = sb("wuv", [128, 512], BF16)
            wgb = [sb(f"wgb{i}", [128, 8, 128], BF16) for i in range(2)]
            KTB = [sb(f"KTB{i}", [128, SEQ], BF16) for i in range(2)]
            QTB = [sb(f"QTB{i}", [128, TQ], BF16) for i in range(2)]
            VB = sb("VB", [128, NT, 192], BF16)
            GTB = sb("GTB", [128, TQ], BF16)
            qs3 = [sb(f"qs3_{i}", [128, 384], F32) for i in range(3)]
            qb3 = [sb(f"qb3_{i}", [128, 384], BF16) for i in range(3)]
            PT = [sb(f"PTb{i}", [128, 1024], BF16) for i in range(3)]
            rd = [sb(f"rdb{i}", [128, 512], F32) for i in range(2)]
            fint = [sb(f"fintb{i}", [128, 512], F32) for i in range(2)]
            s3 = [[sb(f"s3_{i}{k}", [128, 64], F32) for k in range(4)] for i in range(3)]
            dma("pool", wuq[:].rearrange("p c n -> p (c n)"), wuq_d, (), ["wuq"])
            dma("pool", wuk[:], wuk_d, (), ["wuk"])
            dma("pool", wuv[:], wuv_d, (), ["wuv"])
            dma("pool", wgb[0][:].rearrange("p c n -> p (c n)"), wgb_d[0], (), ["wgb0"])
            mset("pool", VB[:], 1.0, ["VB_init"])
            for hh_ in range(2):
                dma("sp", KTB[hh_][64:96, :], kropeT[64:96, :], (), ["BKr"])
            for p in range(4):
                S.tag = f'p3 p{p}'
                wg = wgb[p % 2]
                wgk = f"wgb{p % 2}"
                if p + 1 < 4:
                    dma("pool", wgb[(p + 1) % 2][:].rearrange("p c n -> p (c n)"), wgb_d[p + 1], (), [f"wgb{(p + 1) % 2}"])
                def kb_unit(u_, p=p):
                    hh, kg = u_ // 8, u_ % 8
                    h = 2 * p + hh
                    i = (kg + 1) % 2
                    mm(pp[2 + i][0:64, 0:512], wuk[:, h * 64:(h + 1) * 64], ckvnT[:, kg * 512:(kg + 1) * 512],
                       True, True, ["wuk"], [pk(2 + i, 0)])
                    cp("act" if kg % 2 == 0 else "dve", KTB[hh][0:64, kg * 512:(kg + 1) * 512], pp[2 + i][0:64, 0:512],
                       [pk(2 + i, 0)], ["BK"])

                def vb_unit(kg, p=p):
                    i = (kg + 1) % 2
                    for tl in range(4):
                        kb = kg * 4 + tl
                        mm(pp[2 + i][:, 512 + tl * 128:512 + (tl + 1) * 128], ckvnT[:, kb * 128:(kb + 1) * 128],
                           wuv[:, p * 128:(p + 1) * 128], True, True, ["wuv"], [pk(2 + i, 1)])
                    cp("dve" if kg % 2 == 0 else "act",
                       VB[:, kg * 4:(kg + 1) * 4, :].rearrange("p t (a b) -> p t a b", b=64)[:, :, 0:3:2, :],
                       pp[2 + i][:, 512:1024].rearrange("p (t a b) -> p t a b", t=4, a=2), [pk(2 + i, 1), "VB_init"], ["BV"])

                def qb_A(bi, p=p):
                    t0 = bi * 2
                    nt = min(2, NQ - t0)
                    i = bi % 2
                    pj = pp[i]
                    for tl in range(nt):
                        t = t0 + tl
                        for cc in range(2):
                            mm(pj[:, tl * 192:(tl + 1) * 192], cqnT[:, cc, t * 128:(t + 1) * 128],
                               wuq[:, cc, p * 192:(p + 1) * 192], cc == 0, cc == 1, ["wuq"], [pk(i, 0)])
                    cp("act", qs3[bi % 3][:, 0:nt * 192], pj[:, 0:nt * 192], [pk(i, 0)], [f"qs3{bi % 3}"])

                def qb_B(bi):
                    t0 = bi * 2
                    nt = min(2, NQ - t0)
                    i = bi % 2
                    u = bi % 3
                    W = nt * 192
                    v_s = qs3[u][:, 0:W].rearrange("p (a d) -> p a d", d=96)
                    v_d = qb3[u][:, 0:W].rearrange("p (a d) -> p a d", d=96)
                    cp("pool", v_d[:, :, 0:64], v_s[:, :, 0:64], [f"qs3{u}"], [f"qb3n{u}"])
                    rk = [f"qb3n{u}"]
                    for tl in range(nt):
                        t = t0 + tl
                        rope32(v_s[:, 2 * tl:2 * tl + 2, 64:96], 2,
                               tabB[:, t:t + 1, 0:32].broadcast_to([128, 2, 32]), tabB[:, t:t + 1, 32:64].broadcast_to([128, 2, 32]),
                               v_d[:, 2 * tl:2 * tl + 2, 64:96], (s3[u][2 * tl], s3[u][2 * tl + 1]),
                               [f"qs3{u}"], [f"qb3r{u}_{tl}"], f"qr{u}_{tl}")
                        rk.append(f"qb3r{u}_{tl}")
                    pS = pp[i][:, 512:1024].bitcast(BF16)
                    for tl in range(nt):
                        for hh in range(2):
                            tp(pS[0:96, (tl * 2 + hh) * 192:(tl * 2 + hh) * 192 + 128],
                               qb3[u][:, tl * 192 + hh * 96:tl * 192 + (hh + 1) * 96], rk, [pk(i, 1)])
                    for hh in range(2):
                        cp("dve", QTB[hh][0:96, t0 * 128:(t0 + nt) * 128].rearrange("p (t n) -> p t n", t=nt),
                           pS[0:96, 0:nt * 384].rearrange("p (t h n) -> p t h n", t=nt, h=2)[:, :, hh, 0:128], [pk(i, 1)], ["BQ"])

                for step in range(11):
                    if step < 9:
                        qb_A(step)
                    for u_ in (2 * step, 2 * step + 1):
                        if u_ < 16:
                            kb_unit(u_)
                    if step < 8:
                        vb_unit(step)
                    if 0 <= step - 2 < 9:
                        qb_B(step - 2)
                gates_fm(lambda c, wg=wg: wg[:, c, :], GTB, [wgk], "B")
                attention_l0(lambda h, kb: KTB[h][0:96, kb * 128:(kb + 1) * 128],
                             lambda h, q0, nq: QTB[h][0:96, q0:q0 + nq],
                             lambda h, kb: VB[:, kb, 64 * h:64 * h + 128],
                             SCALE_B, GTB, 4 + p, "B", PT, rd, fint)
            S.barrier()
            if stage <= 3:
                raise _Stop()
            release(mL0)

            x1 = sb("x1", [128, NQ, D], F32)
            mP4 = mark()
            wo = sb("wo", [128, 8, D], BF16)
            xt4 = [sb(f"xt4_{i}", [128, D], F32) for i in range(2)]
            yg = [sb(f"yg{i}", [128, D], F32) for i in range(2)]
            dma("pool", wo[:].rearrange("p c n -> p (c n)"), wout0_d, (), ["wo"])
            for t in range(NQ):
                S.tag = f'p4 t{t}'
                s = t % 2
                dma("sp", xt4[s][:], x_d[t * 128:(t + 1) * 128, :], (), [f"xt4{s}"])
                for half in range(2):
                    for c in range(8):
                        mm(pp[2 * s + half][:, 0:512], mixT[:, c, t * 128:(t + 1) * 128], wo[:, c, half * 512:(half + 1) * 512],
                           c == 0, c == 7, ["wo"], [pk(2 * s + half, 0)])
                    tt("dve", yg[s][:, half * 512:(half + 1) * 512], pp[2 * s + half][:, 0:512], gate_bc[:, half * 512:(half + 1) * 512],
                       ALU.mult, [pk(2 * s + half, 0)], [f"yg{s}_{half}"])
                tt("dve", x1[:, t, :], yg[s][:], xt4[s][:], ALU.add, [f"yg{s}_0", f"yg{s}_1", f"xt4{s}"], [f"x1_{t}"])
                if debug:
                    dma("sp", dbg_d[t * 128:(t + 1) * 128, :], x1[:, t, :], [f"x1_{t}"], [f"dbg{t}"])
            S.barrier()
            if stage <= 4:
                raise _Stop()
            release(mP4)

            mL1 = mark()
            h1T = sb("h1T", [128, 8, TQ], BF16)
            K1T = sb("K1T", [128, 2, TQ], BF16)
            V1 = sb("V1", [128, NQ, 384], BF16)
            dtab = sb("dtab", [128, 384], F32)
            esink = sb("esink", [128, 16], F32)
            mL1a = mark()
            g_bc = sb("g_bc1", [128, D], F32)
            sh_bc = sb("sh_bc1", [128, D], F32)
            emit_mods(1, g_bc, sh_bc, 256)
            xs1 = [sb(f"xs1_{i}", [128, D], BF16) for i in range(2)]
            xm1 = [sb(f"xm1_{i}", [128, D], F32) for i in range(2)]
            wkv1 = sb("wkv1", [128, 8, 512], BF16)
            dma("pool", wkv1[:].rearrange("p c n -> p (c n)"), wkv1_d, (), ["wkv1"])
            dma("sp", dtab[:], dtab_d, (), ["dtab"])
            dma("sp", esink[:], sink_d.partition_broadcast(128), (), ["esink"])
            act(esink[:], esink[:], AF.Exp, ["esink"], ["esink"])
            mset("pool", V1[:], 1.0, ["V1_init"])

            def l1_pre(t):
                s = t % 2
                hT_pre(x1[:, t, :], [], s, xm1[s][:], f"xm1{s}", xs1[s][:], g_bc, sh_bc)

            def l1_tr(t):
                s = t % 2
                hT_tr(s, xs1[s][:], h1T[:, :, t * 128:(t + 1) * 128], [f"h1T{t}"])

            def l1_v(t):
                s = t % 2
                pj = pp[2 * s + 1]
                for c in range(8):
                    mm(pj[:, 0:256], h1T[:, c, t * 128:(t + 1) * 128], wkv1[:, c, 256:512], c == 0, c == 7,
                       [f"h1T{t}", "wkv1"], [pk(2 * s + 1, 0)])
                cp("dve", V1[:, t, :].rearrange("p (j a b) -> p j a b", j=2, a=3)[:, :, 0:3:2, :],
                   pj[:, 0:256].rearrange("p (j a b) -> p j a b", j=2, a=2), [pk(2 * s + 1, 0), "V1_init"], ["V1"])

            for step in range(NQ + 2):
                if step < NQ:
                    l1_pre(step)
                if 0 <= step - 1 < NQ:
                    l1_tr(step - 1)
                if 0 <= step - 2 < NQ:
                    l1_v(step - 2)
            for j in range(2):
                for qg in range(5):
                    q0 = qg * 512
                    nq = min(512, TQ - q0)
                    i = (j * 5 + qg) % 2
                    for c in range(8):
                        mm(pp[i][:, 0:nq], wkv1[:, c, j * 128:(j + 1) * 128], h1T[:, c, q0:q0 + nq], c == 0, c == 7,
                           ["wkv1"] + [f"h1T{t}" for t in range(q0 // 128, (q0 + nq) // 128)], [pk(i, 0)])
                    cp("act" if qg % 2 == 0 else "dve", K1T[:, j, q0:q0 + nq], pp[i][:, 0:nq], [pk(i, 0)], ["K1T"])
            S.barrier()
            if stage <= 5:
                raise _Stop()
            release(mL1a)

            wqg1 = [sb(f"wqg1_{i}", [128, 8, 256], BF16) for i in range(2)]
            Q1T = sb("Q1T", [128, TO], BF16)
            G1T = sb("G1T", [128, TO], BF16)
            Etab = sb("Etab", [128, 16, 384], BF16)
            esb = sb("esb", [128, 1], F32)
            PT1 = [sb(f"PT1_{i}", [128, 768], BF16) for i in range(4)]
            rd1 = [sb("rd1_0", [128, 512], F32)] * 2
            tm1 = [sb("tm1_0", [128, 512], F32)] * 2
            dma("pool", wqg1[0][:].rearrange("p c n -> p (c n)"), wqg1_d[0], (), ["wqg0"])

            slopes = [2.0 ** (-8.0 * (h + 1) / 16.0) for h in range(16)]
            for h_ in range(16):
                act(Etab[:, h_, :], dtab[:], AF.Exp, ["dtab"], ["Etab"], scale=-slopes[h_])
            for P_ in range(8):
                S.tag = f'L1 P{P_}'
                j, g = P_ // 4, P_ % 4
                heads = [(2 * j) * 4 + g, (2 * j + 1) * 4 + g]
                w = wqg1[P_ % 2]
                wk = f"wqg{P_ % 2}"
                if P_ + 1 < 8:
                    dma("pool", wqg1[(P_ + 1) % 2][:].rearrange("p c n -> p (c n)"), wqg1_d[P_ + 1], (), [f"wqg{(P_ + 1) % 2}"])
                cp("pool", esb[64:128, 0:1], esink[64:128, heads[0]:heads[0] + 1], ["esink"], ["esb"])
                cp("pool", esb[0:64, 0:1], esink[0:64, heads[1]:heads[1] + 1], ["esink"], ["esb"])
                for qg in range(4):
                    q0 = qg * 512
                    i = qg % 2
                    for c in range(8):
                        mm(pp[i][:, 0:512], w[:, c, 0:128], h1T[:, c, q0:q0 + 512], c == 0, c == 7, [wk], [pk(i, 0)])
                    cp("dve", Q1T[:, q0:q0 + 512], pp[i][:, 0:512], [pk(i, 0)], ["Q1T"])
                    for c in range(8):
                        mm(pp[i][:, 512:1024], w[:, c, 128:256], h1T[:, c, q0:q0 + 512], c == 0, c == 7, [wk], [pk(i, 1)])
                    act(G1T[:, q0:q0 + 512], pp[i][:, 512:1024], AF.Silu, [pk(i, 1)], ["G1T"])
                its1 = []
                for G in range(4):
                    jb0 = G * 4
                    kbs = list(range(max(jb0 - 1, 0), min(jb0 + 4, 16) + 1))
                    for idx, kb in enumerate(kbs):
                        qlo = max(kb - 1, jb0)
                        qhi = min(kb + 1, jb0 + 3)
                        its1.append(dict(G=G, kb=kb, first=(idx == 0), last=(idx == len(kbs) - 1), qlo=qlo,
                                         nq=(qhi - qlo + 1) * 128, d0=(qlo - (kb - 1)) * 128, lo=(qlo - jb0) * 128))
                n1 = len(its1)

                def l1_qk(i):
                    it = its1[i]
                    ssl = i % 2
                    nq, kb, qlo = it["nq"], it["kb"], it["qlo"]
                    for hh in range(2):
                        mm(pp[ssl][:, hh * 512:hh * 512 + nq], K1T[64 * hh:64 * hh + 64, j, kb * 128:(kb + 1) * 128],
                           Q1T[64 * hh:64 * hh + 64, qlo * 128:qlo * 128 + nq], True, True, ["Q1T"], [pk(ssl, hh)])

                def l1_exp(i):
                    it = its1[i]
                    ssl = i % 2
                    psl = i % 4
                    nq = it["nq"]
                    act(PT1[psl][:].rearrange("p (h n) -> p h n", h=2)[:, :, 0:nq],
                        pp[ssl][:].rearrange("p (h n) -> p h n", h=2)[:, :, 0:nq], AF.Exp,
                        [pk(ssl, 0), pk(ssl, 1)], [f"PT1_{psl}_0", f"PT1_{psl}_1"], scale=SCALE_C)

                def l1_bias(i):
                    it = its1[i]
                    ssl = i % 4
                    nq, d0 = it["nq"], it["d0"]
                    for hh in range(2):
                        tt("dve", PT1[ssl][:, hh * 384:hh * 384 + nq], PT1[ssl][:, hh * 384:hh * 384 + nq],
                           Etab[:, heads[hh], d0:d0 + nq], ALU.mult, [f"PT1_{ssl}_{hh}", "Etab"], [f"PT1_{ssl}_{hh}"])

                def l1_pv(i):
                    it = its1[i]
                    ssl = i % 4
                    osl = it["G"] % 2
                    psO = pp[2 + osl]
                    nq, kb, lo = it["nq"], it["kb"], it["lo"]
                    for hh in range(2):
                        mm(psO[:, hh * 512 + lo:hh * 512 + lo + nq], V1[:, kb, j * 192 + 64 * hh:j * 192 + 64 * hh + 128],
                           PT1[ssl][:, hh * 384:hh * 384 + nq], it["first"], it["last"], [f"PT1_{ssl}_{hh}"], [pk(2 + osl, hh)],
                           skip=True)
                    if not it["last"]:
                        return
                    r_ = rd1[osl]
                    tmp = tm1[osl]
                    q0 = it["G"] * 512
                    act(r_[64:128, :], psO[64:128, 0:512], AF.Ln, [pk(2 + osl, 0), "esb"], ["rd1a"], bias=esb[64:128, 0:1])
                    act(r_[0:64, :], psO[0:64, 512:1024], AF.Ln, [pk(2 + osl, 1), "esb"], ["rd1b"], bias=esb[0:64, 0:1])
                    act(r_[:, :], r_[:, :], AF.Exp, ["rd1a", "rd1b"], ["rd1"], scale=-1.0)
                    tt("dve", tmp[0:64, :], psO[0:64, 0:512], r_[64:128, :], ALU.mult, [pk(2 + osl, 0), "rd1"], ["tm1"])
                    tt("dve", tmp[64:128, :], psO[64:128, 512:1024], r_[0:64, :], ALU.mult, [pk(2 + osl, 1), "rd1"], ["tm1"])
                    tt("pool", mixT[:, P_, q0:q0 + 512], tmp[:, :], G1T[:, q0:q0 + 512], ALU.mult, ["tm1", "G1T"], ["mix1"])

                for step in range(n1 + 3):
                    if 0 <= step - 3 < n1:
                        l1_pv(step - 3)
                    if step < n1:
                        l1_qk(step)
                    if 0 <= step - 1 < n1:
                        l1_exp(step - 1)
                    if 0 <= step - 2 < n1:
                        l1_bias(step - 2)
            S.barrier()
            if stage <= 6:
                raise _Stop()
            release(mL1)

            wo1 = sb("wo1", [128, 8, D], BF16)
            fn_bc = sb("fn_bc", [128, D], F32)
            yg = [sb(f"yg1_{i}", [128, D], F32) for i in range(2)]
            x2 = [sb(f"x2_{i}", [128, D], F32) for i in range(2)]
            ot = [sb(f"ot{i}", [128, D], F32) for i in range(2)]
            dma("pool", wo1[:].rearrange("p c n -> p (c n)"), wout1_d, (), ["wo1"])
            dma("sp", fn_bc[:], fnorm_d.partition_broadcast(128), (), ["fn_bc"])
            for t in range(16):
                s = t % 2
                for half in range(2):
                    for c in range(8):
                        mm(pp[2 * s + half][:, 0:512], mixT[:, c, t * 128:(t + 1) * 128], wo1[:, c, half * 512:(half + 1) * 512],
                           c == 0, c == 7, ["wo1"], [pk(2 * s + half, 0)])
                    tt("dve", yg[s][:, half * 512:(half + 1) * 512], pp[2 * s + half][:, 0:512], gate_bc[:, half * 512:(half + 1) * 512],
                       ALU.mult, [pk(2 * s + half, 0)], [f"yg{s}_{half}"])
                tt("dve", x2[s][:], yg[s][:], x1[:, t, :], ALU.add, [f"yg{s}_0", f"yg{s}_1"], [f"x2_{s}"])
                ss = stat[:, 16 + s:17 + s]
                act(junk[:], x2[s][:], AF.Square, [f"x2_{s}"], ["junk", f"fss{s}"], accum=ss)
                rstd(ss, ss, float(D), [f"fss{s}"], [f"fss{s}"])
                stt("dve", ot[s][:], x2[s][:], ss, fn_bc[:], ALU.mult, ALU.mult, [f"x2_{s}", f"fss{s}", "fn_bc"], [f"ot{s}"])
                dma("sp", y_d[t * 128:(t + 1) * 128, :], ot[s][:], [f"ot{s}"], [f"y{t}"])
                okeys.append(f"y{t}")
        except _Stop:
            pass
        print("SBUF arena peak words:", A["peak"], "of", ARENA_WORDS)
        S.emit(nc, final_wait_keys=okeys)
    nc._sched = S
    return nc


def _rope_cs(pos, dim):
    inv = (np.float32(10000.0) ** (-(np.arange(0, dim, 2, dtype=np.float32)) / np.float32(dim))).astype(np.float32)
    ang = pos.astype(np.float32)[:, None] * inv[None, :]
    return np.cos(ang).astype(np.float32), np.sin(ang).astype(np.float32)


def _tables(tok):
    row = tok // 64
    col = tok % 64
    cr, sr = _rope_cs(row, 32)
    cc, sc_ = _rope_cs(col, 32)
    ct, st_ = _rope_cs(tok, 32)
    cosA = np.concatenate([cr, cr, cc, cc], axis=1)
    sinA = np.concatenate([-sr, sr, -sc_, sc_], axis=1)
    cosB = np.concatenate([ct, ct], axis=1)
    sinB = np.concatenate([-st_, st_], axis=1)

    tab = np.concatenate([cosA, sinA, cosB, sinB], axis=1).astype(np.float32)
    return np.ascontiguousarray(tab.reshape(NT, 128, 192))


def _dtab():
    s = np.arange(128)[:, None]
    q = np.arange(384)[None, :] - 128
    d = np.abs(q - s).astype(np.float32)
    return np.where(d <= 128, d, np.float32(BIGD)).astype(np.float32)


def _prep(inputs):
    f = lambda a: np.ascontiguousarray(np.asarray(a, dtype=np.float32))
    x = f(inputs["x"])
    c = f(inputs["c"])
    w_in = f(inputs["even_w_in"])[0]
    qa, ka, va, ga = w_in[:, 0:512], w_in[:, 512:640], w_in[:, 640:768], w_in[:, 768:1280]
    cq, ckv, kr, gb = w_in[:, 1280:1536], w_in[:, 1536:1664], w_in[:, 1664:1696], w_in[:, 1696:2208]
    wk0 = np.concatenate([ka, va, ckv, kr, cq], axis=1)
    permA = [np.concatenate([np.arange((0 * 4 + g) * 64, (0 * 4 + g) * 64 + 64), np.arange((1 * 4 + g) * 64, (1 * 4 + g) * 64 + 64)])
             for g in range(4)]
    wA = np.stack([np.concatenate([qa[:, permA[g]], ga[:, permA[g]]], axis=1) for g in range(4)])
    wgb = np.stack([gb[:, p * 128:(p + 1) * 128] for p in range(4)])
    wout = f(inputs["even_w_out"])[0]
    rows0 = np.concatenate([np.concatenate(permA), np.arange(512, 1024)])
    wout0 = wout[rows0, :]
    w1 = f(inputs["odd_w_in"])[0]
    qc, kc, vc, gc = w1[:, 0:1024], w1[:, 1024:1280], w1[:, 1280:1536], w1[:, 1536:2560]
    perm1 = []
    for P_ in range(8):
        j, g = P_ // 4, P_ % 4
        hA, hB = (2 * j) * 4 + g, (2 * j + 1) * 4 + g
        perm1.append(np.concatenate([np.arange(hA * 64, hA * 64 + 64), np.arange(hB * 64, hB * 64 + 64)]))
    wqg1 = np.stack([np.concatenate([qc[:, perm1[P_]], gc[:, perm1[P_]]], axis=1) for P_ in range(8)])
    wkv1 = np.concatenate([kc, vc], axis=1)
    wout1 = f(inputs["odd_w_out"])[0][np.concatenate(perm1), :]
    def pm(w):
        cN = w.shape[0] // 128
        return np.ascontiguousarray(w.reshape(cN, 128, w.shape[1]).transpose(1, 0, 2).reshape(128, cN * w.shape[1]))
    adaw = f(inputs["ada_w"]).reshape(2, 8, 128, 12, 256).transpose(0, 3, 2, 1, 4).reshape(2, 12, 128, 8 * 256)
    shared = {
        "ada_w": np.ascontiguousarray(adaw), "ada_b": f(inputs["ada_b"]), "norm_w": f(inputs["norm_w"]),
        "final_norm": f(inputs["final_norm"]), "ident": np.eye(128, dtype=np.float32),
        "wk0": pm(wk0), "wA": np.stack([pm(wA[i]) for i in range(4)]), "wgb": np.stack([pm(wgb[i]) for i in range(4)]),
        "wuq": pm(f(inputs["b_w_uq"])[0]), "wuk": np.ascontiguousarray(f(inputs["b_w_uk"])[0].reshape(128, 512)),
        "wuv": np.ascontiguousarray(f(inputs["b_w_uv"])[0].reshape(128, 512)),
        "wout0": pm(wout0),
        "aqn": f(inputs["a_q_norm"])[0], "akn": f(inputs["a_k_norm"])[0],
        "bqn": f(inputs["b_q_lora_norm"])[0], "bkvn": f(inputs["b_kv_lora_norm"])[0],
        "wkv1": pm(wkv1), "wqg1": np.stack([pm(wqg1[i]) for i in range(8)]), "wout1": pm(wout1),
        "sink": f(inputs["c_sink"])[0], "dtab": _dtab(),
    }
    tabs = [_tables(np.arange(SEQ)), _tables(SEQ - 1 - np.arange(SEQ))]
    in_maps = []
    for core in range(8):
        b, hf = core // 2, core % 2
        xl = x[b] if hf == 0 else x[b][::-1]
        m = dict(shared)
        m["x_loc"] = np.ascontiguousarray(xl)
        m["c_col"] = np.ascontiguousarray(c[b].reshape(8, 128).T)
        m["tab"] = tabs[hf]
        tq = tabs[hf][0:NQ]
        m["tabqa"] = np.ascontiguousarray(tq[:, :, 0:128].transpose(1, 0, 2).reshape(128, NQ * 128))
        m["tabqb"] = np.ascontiguousarray(tq[:, :, 128:192].transpose(1, 0, 2).reshape(128, NQ * 64))
        in_maps.append(m)
    return in_maps


_NC_CACHE = {}


def kernel(**inputs):
    debug = bool(inputs.pop("_debug", False))
    in_maps = _prep(inputs)
    if debug not in _NC_CACHE:
        _NC_CACHE[debug] = build_program(debug)
    nc = _NC_CACHE[debug]
    res = run_bass_kernel_spmd(nc, in_maps, core_ids=list(range(8)))
    out = np.empty((4, SEQ, D), dtype=np.float32)
    for core in range(8):
        b, hf = core // 2, core % 2
        y = np.asarray(res.results[core]["y"], dtype=np.float32)
        if hf == 0:
            out[b, 0:TO] = y
        else:
            out[b, TO:SEQ] = y[::-1]
    if debug:
        return out, [np.asarray(res.results[core]["dbg"]) for core in range(8)], [dict(h=np.asarray(res.results[core]["dbg2"]).astype(np.float32).reshape(128, 8, TQ), k=np.asarray(res.results[core]["dbg3"]).astype(np.float32), m=np.asarray(res.results[core]["dbg4"])) for core in range(8)]
    return out
```

```python
import contextlib
import numpy as np
import concourse.bass as bass
import concourse.mybir as mybir
from concourse.bass_utils import run_bass_kernel_spmd

F32 = mybir.dt.float32
BF16 = mybir.dt.bfloat16
AF = mybir.ActivationFunctionType
ALU = mybir.AluOpType
AX = mybir.AxisListType

D = 1024
SEQ = 4096
NT = 32
NQ = 17
TQ = NQ * 128
TO = 2048
EPS = 1e-6
SCALE_A = 64 ** -0.5
SCALE_B = 96 ** -0.5
SCALE_C = 64 ** -0.5
BIGD = 1.0e5


class _Ins:
    __slots__ = ("eng", "idx", "fn", "deps", "dma", "signal", "sem", "val", "tag", "rw")

    def __init__(self, eng, idx, fn, deps, dma):
        self.eng, self.idx, self.fn, self.deps, self.dma = eng, idx, fn, deps, dma
        self.signal = dma
        self.sem = None
        self.val = 0


class Sched:
    ENGS = ["pe", "act", "dve", "pool", "sp"]
    NDS = 8

    def __init__(self):
        self.ins = {e: [] for e in self.ENGS}
        self.last_w = {}
        self.readers = {}
        self.bar = set()
        self.bar_done = {e: True for e in self.ENGS}
        self.tag = ""
        self.names = {}

    def barrier(self):
        deps = set()
        for e in self.ENGS:
            lst = self.ins[e]
            last_c = None
            nd = 0
            for i in range(len(lst) - 1, -1, -1):
                t = lst[i]
                if t.dma:
                    if nd < self.NDS:
                        deps.add((e, i))
                        nd += 1
                elif last_c is None:
                    last_c = i
                    deps.add((e, i))
                if nd >= self.NDS and last_c is not None:
                    break
        for (e, i) in deps:
            self.ins[e][i].signal = True
        self.bar = deps
        self.bar_done = {e: False for e in self.ENGS}
        self.last_w = {}
        self.readers = {}

    def add(self, eng, fn, reads=(), writes=(), dma=False):
        lst = self.ins[eng]
        idx = len(lst)
        deps = set()
        for k in list(reads) + list(writes):
            w = self.last_w.get(k)
            if w is not None:
                deps.add(w)
        for k in writes:
            for rk, i in self.readers.get(k, {}).items():
                e = rk if isinstance(rk, str) else rk[1]
                deps.add((e, i))
        final = set()
        for (e, i) in deps:
            t = self.ins[e][i]
            if e == eng and eng == "pe" and not t.dma and not dma:
                continue
            t.signal = True
            final.add((e, i))
        if not self.bar_done[eng]:
            final |= {d for d in self.bar if not (d[0] == eng and eng == "pe" and not self.ins[d[0]][d[1]].dma)}
            self.bar_done[eng] = True
        ins = _Ins(eng, idx, fn, final, dma)
        ins.tag = self.tag
        ins.rw = (tuple(reads), tuple(writes))
        lst.append(ins)
        for k in writes:
            self.last_w[k] = (eng, idx)
            self.readers[k] = {}
        for k in reads:
            d = self.readers.setdefault(k, {})
            if dma:
                d[("dma", eng, idx)] = idx
            else:
                d[eng] = max(d.get(eng, -1), idx)
        return ins

    def emit(self, nc, final_wait_keys=()):
        with contextlib.ExitStack() as st:
            csem = {e: st.enter_context(nc.semaphore("c_" + e)) for e in self.ENGS if e != "sp"}
            dsem = {e: [st.enter_context(nc.semaphore(f"d_{e}{j}")) for j in range(self.NDS)]
                    for e in ("sp", "pool", "act")}
            for e in self.ENGS:
                cc = 0
                dc = 0
                for t in self.ins[e]:
                    if t.dma:
                        t.sem = dsem[e][dc % self.NDS]
                        t.val = 16 * (dc // self.NDS + 1)
                        dc += 1
                    elif t.signal:
                        cc += 1
                        t.sem = csem[e]
                        t.val = cc
            final_deps = set()
            for k in final_wait_keys:
                w = self.last_w.get(k)
                if w is not None:
                    final_deps.add(w)
            block = st.enter_context(nc.Block())

            def run(engh, e):
                known = {}

                def waits_raw(sem, v):
                    key = id(sem)
                    if known.get(key, 0) >= v:
                        return
                    engh.wait_ge(sem, v)
                    known[key] = v

                def waits(deps):
                    need = {}
                    for (de, di) in deps:
                        d = self.ins[de][di]
                        key = id(d.sem)
                        if known.get(key, 0) >= d.val:
                            continue
                        if key not in need or need[key][1] < d.val:
                            need[key] = (d.sem, d.val)
                    for key, (s, v) in need.items():
                        engh.wait_ge(s, v)
                        known[key] = v

                for t in self.ins[e]:
                    waits(t.deps)
                    if t.dma and t.val > 16:
                        waits_raw(t.sem, t.val - 16)
                    r = t.fn(engh)
                    try:
                        self.names[r.ins.name] = (t.eng, t.idx, t.tag, t.rw)
                    except Exception:
                        pass
                    if t.signal:
                        r.then_inc(t.sem, 16 if t.dma else 1)
                if e == "sp":
                    waits(final_deps)

            @block.tensor
            def _(h):
                run(h, "pe")

            @block.scalar
            def _(h):
                run(h, "act")

            @block.vector
            def _(h):
                run(h, "dve")

            @block.gpsimd
            def _(h):
                run(h, "pool")

            @block.sync
            def _(h):
                run(h, "sp")


class _Stop(Exception):
    pass


def build_program(debug=False, stage=99):
    nc = bass.Bass("TRN2", target_bir_lowering=False)
    S = Sched()

    def din(name, shape):
        return nc.dram_tensor(name, list(shape), F32, kind="ExternalInput").ap()

    x_d = din("x_loc", [SEQ, D])
    c_d = din("c_col", [128, 8])
    adaw_d = din("ada_w", [2, 12, 128, 8 * 256])
    adab_d = din("ada_b", [2, 3 * D])
    normw_d = din("norm_w", [2, D])
    fnorm_d = din("final_norm", [D])
    ident_d = din("ident", [128, 128])
    wk0_d = din("wk0", [128, 8 * 672])
    wA_d = din("wA", [4, 128, 8 * 256])
    wgb_d = din("wgb", [4, 128, 8 * 128])
    wuq_d = din("wuq", [128, 2 * 768])
    wuk_d = din("wuk", [128, 512])
    wuv_d = din("wuv", [128, 512])
    wout0_d = din("wout0", [128, 8 * D])
    aqn_d = din("aqn", [64])
    akn_d = din("akn", [64])
    bqn_d = din("bqn", [256])
    bkvn_d = din("bkvn", [128])
    tab_d = din("tab", [NT, 128, 192])
    tabqa_d = din("tabqa", [128, NQ * 128])
    tabqb_d = din("tabqb", [128, NQ * 64])
    wkv1_d = din("wkv1", [128, 8 * 512])
    wqg1_d = din("wqg1", [8, 128, 8 * 256])
    wout1_d = din("wout1", [128, 8 * D])
    sink_d = din("sink", [16])
    dtab_d = din("dtab", [128, 384])
    y_d = nc.dram_tensor("y", [TO, D], F32, kind="ExternalOutput").ap()
    dbg_d = None
    if debug:
        dbg_d = nc.dram_tensor("dbg", [TQ, D], F32, kind="ExternalOutput").ap()
        dbg2_d = nc.dram_tensor("dbg2", [128, 8 * TQ], BF16, kind="ExternalOutput").ap()
        dbg3_d = nc.dram_tensor("dbg3", [128, 3 * SEQ + NT * 192 + 2 * TQ], BF16, kind="ExternalOutput").ap()
        dbg4_d = nc.dram_tensor("dbg4", [128, 3 * D], F32, kind="ExternalOutput").ap()

    def mm(out, lhsT, rhs, start, stop, r, w, skip=False):
        if skip:
            S.add("pe", lambda e: e.matmul(out, lhsT=lhsT, rhs=rhs, start=start, stop=stop, skip_group_check=True), r, w)
        else:
            S.add("pe", lambda e: e.matmul(out, lhsT=lhsT, rhs=rhs, start=start, stop=stop), r, w)

    def tp(out, in_, r, w):
        S.add("pe", lambda e: e.transpose(out, in_, ident[:]), list(r) + ["ident"], w)

    def act(out, in_, func, r, w, bias=None, scale=None, accum=None):
        kw = {}
        if bias is not None:
            kw["bias"] = bias
        if scale is not None:
            kw["scale"] = scale
        if accum is not None:
            kw["accum_out"] = accum
        S.add("act", lambda e: e.activation(out=out, in_=in_, func=func, **kw), r, w)

    def tt(eng, out, in0, in1, op, r, w):
        S.add(eng, lambda e: e.tensor_tensor(out=out, in0=in0, in1=in1, op=op), r, w)

    def ts(eng, out, in0, s1, s2, op0, op1, r, w):
        if s2 is None:
            S.add(eng, lambda e: e.tensor_scalar(out=out, in0=in0, scalar1=s1, scalar2=None, op0=op0), r, w)
        else:
            S.add(eng, lambda e: e.tensor_scalar(out=out, in0=in0, scalar1=s1, scalar2=s2, op0=op0, op1=op1), r, w)

    def stt(eng, out, in0, scalar, in1, op0, op1, r, w):
        S.add(eng, lambda e: e.scalar_tensor_tensor(out=out, in0=in0, scalar=scalar, in1=in1, op0=op0, op1=op1), r, w)

    def cp(eng, out, in_, r, w):
        if eng == "act":
            S.add("act", lambda e: e.activation(out=out, in_=in_, func=AF.Copy), r, w)
        else:
            S.add(eng, lambda e: e.tensor_copy(out=out, in_=in_), r, w)

    def red(out, in_, r, w):
        S.add("dve", lambda e: e.tensor_reduce(out=out, in_=in_, axis=AX.X, op=ALU.add), r, w)

    def rcp(out, in_, r, w):
        S.add("dve", lambda e: e.reciprocal(out=out, in_=in_), r, w)

    def mset(eng, ap, val, w):
        S.add(eng, lambda e: e.memset(ap, val), (), w)

    def dma(q, out, in_, r, w):
        S.add(q, lambda e: e.dma_start(out=out, in_=in_), r, w, dma=True)

    def rstd(dst, ss, n, r, w):
        act(dst, ss, AF.Ln, r, w, bias=EPS, scale=1.0 / n)
        act(dst, dst, AF.Exp, w, w, scale=-0.5)

    with contextlib.ExitStack() as outer:
        ARENA_WORDS = 52992
        arena_t = outer.enter_context(nc.sbuf_tensor("arena", [128, ARENA_WORDS], F32))
        A = {"top": 0, "peak": 0}

        def sb(name, shape, dt, st=None):
            n = 1
            for d_ in shape[1:]:
                n *= int(d_)
            nbytes = n * (2 if dt == BF16 else 4)
            w = (nbytes + 31) // 32 * 8
            off = A["top"]
            A["top"] += w
            assert A["top"] <= ARENA_WORDS, ("SBUF arena overflow", name, A["top"])
            A["peak"] = max(A["peak"], A["top"])
            v = arena_t[:, off:off + w]
            if dt == BF16:
                v = v.bitcast(BF16)
            v = v[:, 0:n]
            if len(shape) == 3:
                v = v.rearrange("p (a b) -> p a b", a=int(shape[1]))
            elif len(shape) == 4:
                v = v.rearrange("p (a b c) -> p a b c", a=int(shape[1]), b=int(shape[2]))
            return v

        def mark():
            return A["top"]

        def release(m):
            A["top"] = m

        pp = [outer.enter_context(nc.psum_tensor(f"pp{i}", [128, 1024], F32)) for i in range(4)]

        def pk(i, h):
            return f"pp{i}{'ab'[h]}"

        okeys = []
        try:
            ident = sb("ident", [128, 128], BF16)
            c_sb = sb("c_sb", [128, 8], F32)
            gate_bc = sb("gate_bc", [128, D], F32)
            junk = sb("junk", [128, D], BF16)
            stat = sb("stat", [128, 64], F32)
            mixT = sb("mixT", [128, 8, TQ], BF16)

            dma("pool", ident[:], ident_d, (), ["ident"])
            dma("sp", c_sb[:], c_d, (), ["c_sb"])
            act(c_sb[:], c_sb[:], AF.Silu, ["c_sb"], ["c_sb"])

            def emit_mods(layer, g_bc, sh_bc, SW, nslots=2, queues=("sp",)):
                m = mark()
                ones_f = sb("ones_f", [128, 128], F32)
                crep = sb("crep", [128, 8, 128], F32)
                aw = [sb(f"aw{i}", [128, 8, SW], F32) for i in range(nslots)]
                bb = sb("adab", [128, SW], F32)
                nw = sb("nw", [128, D], F32)
                mset("dve", ones_f[:], 1.0, ["ones_f"])
                for k in range(8):
                    ts("dve", crep[:, k, :], ones_f[:], c_sb[:, k:k + 1], None, ALU.mult, None,
                       ["ones_f", "c_sb"], [f"crep{k}"])
                dma("sp", nw[:], normw_d[layer].partition_broadcast(128), (), ["nw"])
                for s in range(3 * D // SW):
                    sl = s % nslots
                    kind = (s * SW) // D
                    c0 = (s * SW) % D
                    dma(queues[s % len(queues)], aw[sl][:].rearrange("p k n -> p (k n)"), adaw_d[layer][s], (), [f"aw{sl}"])
                    dma("sp", bb[:], adab_d[layer][s * SW:(s + 1) * SW].partition_broadcast(128), (), ["adab"])
                    ps = pp[s % 2][:, 0:SW]
                    for k in range(8):
                        mm(ps, crep[:, k, :], aw[sl][:, k, :], k == 0, k == 7, [f"crep{k}", f"aw{sl}"], [pk(s % 2, 0)])
                    col = slice(c0, c0 + SW)
                    if kind == 0:
                        tt("dve", sh_bc[:, col], ps, bb[:], ALU.add, [pk(s % 2, 0), "adab"], [f"modw{s}"])
                    elif kind == 1:
                        tt("dve", g_bc[:, col], ps, bb[:], ALU.add, [pk(s % 2, 0), "adab"], [f"modw{s}"])
                        stt("dve", g_bc[:, col], g_bc[:, col], 1.0, nw[:, col], ALU.add, ALU.mult,
                            [f"modw{s}", "nw"], [f"modw{s}"])
                    else:
                        tt("dve", gate_bc[:, col], ps, bb[:], ALU.add, [pk(s % 2, 0), "adab"], [f"modw{s}"])
                S.barrier()
                release(m)

            def norm_rope64(src, nt, nh, gain_bc, cos3, sin3, dst, scr, r, w, kp, tk=(), e2="pool"):
                n = nt * nh
                W = n * 64
                sq, qn, t1, t2 = scr["a"], scr["b"], scr["c"], scr["d"]
                ss = scr["st"]
                tt(e2, sq[:, :W], src, src, ALU.mult, r, [kp + "sq"])
                red(ss[:, :n], sq[:, :W].rearrange("p (n d) -> p n d", d=64), [kp + "sq"], [kp + "ss"])
                rstd(ss[:, :n], ss[:, :n], 64.0, [kp + "ss"], [kp + "ss"])
                tt("dve", qn[:, :W].rearrange("p (n d) -> p n d", d=64), src.rearrange("p (n d) -> p n d", d=64),
                   ss[:, :n].unsqueeze(2).broadcast_to([128, n, 64]), ALU.mult, list(r) + [kp + "ss"], [kp + "qn"])
                tt(e2, qn[:, :W].rearrange("p (n d) -> p n d", d=64), qn[:, :W].rearrange("p (n d) -> p n d", d=64),
                   gain_bc[:].unsqueeze(1).broadcast_to([128, n, 64]), ALU.mult, [kp + "qn", "gains"], [kp + "qn"])
                q4 = qn[:, :W].rearrange("p (t h d) -> p t h d", t=nt, h=nh)
                tt(e2, t1[:, :W].rearrange("p (t h d) -> p t h d", t=nt, h=nh), q4,
                   cos3.unsqueeze(2).broadcast_to([128, nt, nh, 64]), ALU.mult, [kp + "qn", "tabs"] + list(tk), [kp + "t1"])
                q6 = qn[:, :W].rearrange("p (t h x r i) -> p t h x r i", t=nt, h=nh, x=2, r=2)
                o6 = t2[:, :W].rearrange("p (t h x r i) -> p t h x r i", t=nt, h=nh, x=2, r=2)
                s5 = sin3.rearrange("p t (x r i) -> p t x r i", x=2, r=2)
                k = 0
                for h in range(nh):
                    for rr in range(2):
                        eng = e2 if k % 2 == 0 else "dve"
                        k += 1
                        tt(eng, o6[:, :, h, :, rr, :], q6[:, :, h, :, 1 - rr, :], s5[:, :, :, rr, :], ALU.mult,
                           [kp + "qn", "tabs"] + list(tk), [kp + f"t2_{h}{rr}"])
                tt("dve", dst, t1[:, :W], t2[:, :W], ALU.add,
                   [kp + "t1"] + [kp + f"t2_{h}{rr}" for h in range(nh) for rr in range(2)], w)

            def rope32(src3, a, cb, sbn, dst3, bufs, r, w, kp):
                u1 = bufs[0][:, :a * 32].rearrange("p (a d) -> p a d", d=32)
                u2 = bufs[1][:, :a * 32].rearrange("p (a d) -> p a d", d=32)
                tt("pool", u1, src3, cb, ALU.mult, list(r) + ["tabs"], [kp + "u1"])
                tt("pool", u2[:, :, 0:16], src3[:, :, 16:32], sbn[:, :, 0:16], ALU.mult, list(r) + ["tabs"], [kp + "u2a"])
                tt("dve", u2[:, :, 16:32], src3[:, :, 0:16], sbn[:, :, 16:32], ALU.mult, list(r) + ["tabs"], [kp + "u2b"])
                tt("dve", dst3, u1, u2, ALU.add, [kp + "u1", kp + "u2a", kp + "u2b"], w)

            MODK = []

            def hT_pre(src_ap, src_keys, s, xm_ap, xm_key, xs_ap, g_bc, sh_bc):
                ss = stat[:, s:s + 1]
                act(junk[:], src_ap, AF.Square, src_keys, ["junk", f"ss{s}"], accum=ss)
                rstd(ss, ss, float(D), [f"ss{s}"], [f"ss{s}"])
                stt("dve", xm_ap, src_ap, ss, g_bc[:], ALU.mult, ALU.mult, list(src_keys) + [f"ss{s}"], [xm_key])
                tt("dve", xs_ap, xm_ap, sh_bc[:], ALU.add, [xm_key], [f"xs{s}"])

            def hT_tr(s, xs_ap, dst, dkeys):
                pT = pp[2 * s][:, 0:512].bitcast(BF16)
                for c in range(8):
                    tp(pT[:, c * 128:(c + 1) * 128], xs_ap[:, c * 128:(c + 1) * 128], [f"xs{s}"], [pk(2 * s, 0)])
                cp("act", dst, pT.rearrange("p (c n) -> p c n", c=8), [pk(2 * s, 0)], dkeys)

            mL0 = mark()
            g_bc = sb("g_bc", [128, D], F32)
            sh_bc = sb("sh_bc", [128, D], F32)
            emit_mods(0, g_bc, sh_bc, 256, nslots=4, queues=("sp", "act"))
            if stage <= 0:
                raise _Stop()
            hTq = sb("hTq", [128, 8, TQ], BF16)
            tabB = sb("tabB", [128, NQ, 64], F32)
            aqn = sb("aqn", [128, 64], F32)
            akn = sb("akn", [128, 64], F32)
            bqn = sb("bqn", [128, 256], F32)
            bkvn = sb("bkvn", [128, 128], F32)
            ckvnT = sb("ckvnT", [128, SEQ], BF16)
            kropeT = sb("kropeT", [128, SEQ], BF16)
            cqnT = sb("cqnT", [128, 2, TQ], BF16)
            mLA = mark()
            KTA = sb("KTA", [128, SEQ], BF16)
            VA = sb("VA", [128, NT, 192], BF16)
            tabA = sb("tabA", [128, NQ, 128], F32)

            dma("sp", tabA[:].rearrange("p t d -> p (t d)"), tabqa_d, (), ["tabs"])
            dma("sp", tabB[:].rearrange("p t d -> p (t d)"), tabqb_d, (), ["tabs"])
            dma("sp", aqn[:], aqn_d.partition_broadcast(128), (), ["gains"])
            dma("sp", akn[:], akn_d.partition_broadcast(128), (), ["gains"])
            dma("sp", bqn[:], bqn_d.partition_broadcast(128), (), ["gains"])
            dma("sp", bkvn[:], bkvn_d.partition_broadcast(128), (), ["gains"])
            mset("pool", VA[:], 1.0, ["VA_init"])

            mP1 = mark()
            wk0 = sb("wk0", [128, 8, 672], BF16)
            xt = [sb(f"xt{i}", [128, D], F32) for i in range(2)]
            xs = [sb(f"xs{i}", [128, D], BF16) for i in range(2)]
            ks = [sb(f"ks{i}", [128, 672], F32) for i in range(3)]
            kbf = [sb(f"kbf{i}", [128, 544], BF16) for i in range(3)]
            hTt = sb("hTt", [128, 3, 8, 128], BF16)
            tabK = [sb(f"tabK{i}", [128, 192], F32) for i in range(3)]
            scr1 = [dict(a=sb("s1a", [128, 128], F32), b=sb("s1b", [128, 128], F32), c=sb("s1c", [128, 128], F32),
                         d=sb("s1d", [128, 128], F32), st=sb("s1s", [128, 8], F32), e=sb("s1e", [128, 32], F32),
                         f=sb("s1f", [128, 32], F32)) for i in range(3)]
            dma("pool", wk0[:].rearrange("p c n -> p (c n)"), wk0_d, (), ["wk0"])

            def p1_ctx(t):
                s = t % 2
                u = t % 3
                isq = t < NQ
                if isq:
                    hdst = hTq[:, :, t * 128:(t + 1) * 128]
                    hk = [f"hTq{t}"]
                    cA, sA = tabA[:, t:t + 1, 0:64], tabA[:, t:t + 1, 64:128]
                    cB, sB = tabB[:, t:t + 1, 0:32], tabB[:, t:t + 1, 32:64]
                    tk = []
                else:
                    hdst = hTt[:, u, :, :]
                    hk = [f"hTt{u}"]
                    cA, sA = tabK[u][:, 0:64].unsqueeze(1), tabK[u][:, 64:128].unsqueeze(1)
                    cB, sB = tabK[u][:, 128:160].unsqueeze(1), tabK[u][:, 160:192].unsqueeze(1)
                    tk = [f"tabK{u}"]
                return s, u, isq, hdst, hk, cA, sA, cB, sB, tk

            def p1_A1(t):
                S.tag = f'p1A1 t{t}'
                s, u, isq, hdst, hk, cA, sA, cB, sB, tk = p1_ctx(t)
                dma("sp", xt[s][:], x_d[t * 128:(t + 1) * 128, :], (), [f"xt{s}"])
                hT_pre(xt[s][:], [f"xt{s}"], s, xt[s][:], f"xt{s}", xs[s][:], g_bc, sh_bc)

            def p1_A2(t):
                S.tag = f'p1A2 t{t}'
                s, u, isq, hdst, hk, cA, sA, cB, sB, tk = p1_ctx(t)
                hT_tr(s, xs[s][:], hdst, hk)

            def p1_A3(t):
                S.tag = f'p1A3 t{t}'
                s, u, isq, hdst, hk, cA, sA, cB, sB, tk = p1_ctx(t)
                if not isq:
                    dma("sp", tabK[u][:], tab_d[t], (), [f"tabK{u}"])
                pj = pp[2 * s + 1]
                for c in range(8):
                    mm(pj[:, 0:416], hdst[:, c, :], wk0[:, c, 0:416], c == 0, c == 7, hk + ["wk0"], [pk(2 * s + 1, 0)])
                cp("act", ks[u][:, 0:416], pj[:, 0:416], [pk(2 * s + 1, 0)], [f"ksa{u}"])
                if isq:
                    for c in range(8):
                        mm(pj[:, 512:768], hdst[:, c, :], wk0[:, c, 416:672], c == 0, c == 7, hk + ["wk0"], [pk(2 * s + 1, 1)])
                    cp("act", ks[u][:, 416:672], pj[:, 512:768], [pk(2 * s + 1, 1)], [f"ksb{u}"])

            def p1_B(t):
                S.tag = f'p1B t{t}'
                s, u, isq, hdst, hk, cA, sA, cB, sB, tk = p1_ctx(t)
                sc = scr1[u]
                norm_rope64(ks[u][:, 0:128], 1, 2, akn, cA, sA, kbf[u][:, 0:128], sc,
                            [f"ksa{u}"] + tk, [f"kbfa{u}"], f"ka{u}", tk=tk)
                cp("pool", VA[:, t, :].rearrange("p (a b) -> p a b", b=64)[:, 0:3:2, :],
                   ks[u][:, 128:256].rearrange("p (a b) -> p a b", b=64), [f"ksa{u}", "VA_init"], [f"VA{t}"])
                ssc = stat[:, 4 + u:5 + u]
                act(junk[:, 0:128], ks[u][:, 256:384], AF.Square, [f"ksa{u}"], ["junk", f"ssc{u}"], accum=ssc)
                rstd(ssc, ssc, 128.0, [f"ssc{u}"], [f"ssc{u}"])
                stt("dve", kbf[u][:, 128:256], ks[u][:, 256:384], ssc, bkvn[:], ALU.mult, ALU.mult,
                    [f"ksa{u}", f"ssc{u}", "gains"], [f"kbfb{u}"])
                rope32(ks[u][:, 384:416].unsqueeze(1), 1, cB, sB, kbf[u][:, 256:288].unsqueeze(1), (sc["e"], sc["f"]),
                       [f"ksa{u}"] + tk, [f"kbfc{u}"], f"kr{u}")
                pS = pp[2 * s][:, 512:1024].bitcast(BF16)
                tp(pS[:, 0:128], kbf[u][:, 0:128], [f"kbfa{u}"], [pk(2 * s, 1)])
                tp(pS[:, 192:320], kbf[u][:, 128:256], [f"kbfb{u}"], [pk(2 * s, 1)])
                if isq:
                    ssq = stat[:, 8 + u:9 + u]
                    act(junk[:, 0:256], ks[u][:, 416:672], AF.Square, [f"ksb{u}"], ["junk", f"ssq{u}"], accum=ssq)
                    rstd(ssq, ssq, 256.0, [f"ssq{u}"], [f"ssq{u}"])
                    stt("dve", kbf[u][:, 288:544], ks[u][:, 416:672], ssq, bqn[:], ALU.mult, ALU.mult,
                        [f"ksb{u}", f"ssq{u}", "gains"], [f"kbfd{u}"])
                    tp(pS[:, 384:512], kbf[u][:, 288:416], [f"kbfd{u}"], [pk(2 * s, 1)])
                    tp(pS[:, 576:704], kbf[u][:, 416:544], [f"kbfd{u}"], [pk(2 * s, 1)])
                tp(pS[0:32, 768:896], kbf[u][:, 256:288], [f"kbfc{u}"], [pk(2 * s, 1)])
                cp("dve", KTA[:, t * 128:(t + 1) * 128], pS[:, 0:128], [pk(2 * s, 1)], [f"KTA{t}"])
                cp("dve", ckvnT[:, t * 128:(t + 1) * 128], pS[:, 192:320], [pk(2 * s, 1)], [f"ckvnT{t}"])
                cp("dve", kropeT[64:96, t * 128:(t + 1) * 128], pS[0:32, 768:896], [pk(2 * s, 1)], [f"kropeT{t}"])
                if isq:
                    cp("dve", cqnT[:, :, t * 128:(t + 1) * 128], pS[:, 384:768].rearrange("p (c n) -> p c n", c=2)[:, :, 0:128],
                       [pk(2 * s, 1)], [f"cqnT{t}"])

            for step in range(NT + 4):
                if step < NT:
                    p1_A1(step)
                if 0 <= step - 1 < NT:
                    p1_A2(step - 1)
                if 0 <= step - 2 < NT:
                    p1_A3(step - 2)
                if 0 <= step - 4 < NT:
                    p1_B(step - 4)
            S.barrier()
            if debug:
                dma("sp", dbg2_d, hTq[:].rearrange("p c n -> p (c n)"), (), ["dbg2"])
                dma("sp", dbg3_d[:, 0:SEQ], KTA[:], (), ["dbg3a"])
                dma("sp", dbg3_d[:, SEQ:2 * SEQ], ckvnT[:], (), ["dbg3b"])
                dma("sp", dbg3_d[64:96, 2 * SEQ:3 * SEQ], kropeT[64:96, :], (), ["dbg3c"])
                dma("sp", dbg3_d[:, 3 * SEQ:3 * SEQ + NT * 192], VA[:].rearrange("p t n -> p (t n)"), (), ["dbg3d"])
                dma("sp", dbg3_d[:, 3 * SEQ + NT * 192:], cqnT[:].rearrange("p c n -> p (c n)"), (), ["dbg3e"])
                dma("sp", dbg4_d[:, 0:D], g_bc[:], (), ["dbg4a"])
                dma("sp", dbg4_d[:, D:2 * D], sh_bc[:], (), ["dbg4b"])
                dma("sp", dbg4_d[:, 2 * D:3 * D], gate_bc[:], (), ["dbg4c"])
                S.barrier()
            if stage <= 1:
                raise _Stop()
            release(mP1)

            def attention_l0(qk_l, qk_r, v_l, scale, GT, chunk, kp, PT, rd, fint):
                its = [(qg, kb) for qg in range(5) for kb in range(NT)]
                n = len(its)

                QSZ = [448, 448, 448, 448, 384]
                QOFF = [0, 448, 896, 1344, 1792]

                def geo(i):
                    qg, kb = its[i]
                    return qg, kb, QOFF[qg], QSZ[qg], qg % 2, i % 2, i % 3

                def st_qk(i):
                    qg, kb, q0, nq, osl, ssl, psl = geo(i)
                    psS = pp[ssl]
                    mm(psS[:, 0:nq], qk_l(0, kb), qk_r(0, q0, nq), True, True, [kp + "K", kp + "Kr", kp + "Q"], [pk(ssl, 0)])
                    mm(psS[:, 512:512 + nq], qk_l(1, kb), qk_r(1, q0, nq), True, True, [kp + "K", kp + "Kr", kp + "Q"], [pk(ssl, 1)])

                def st_exp(i):
                    qg, kb, q0, nq, osl, ssl, psl = geo(i)
                    act(PT[psl][:].rearrange("p (h n) -> p h n", h=2)[:, :, 0:nq],
                        pp[ssl][:].rearrange("p (h n) -> p h n", h=2)[:, :, 0:nq], AF.Exp,
                        [pk(ssl, 0), pk(ssl, 1)], [f"PT{psl}"], scale=scale)

                def st_pv(i):
                    qg, kb, q0, nq, osl, ssl, psl = geo(i)
                    psO = pp[2 + osl]
                    mm(psO[:, 0:nq], v_l(0, kb), PT[psl][:, 0:nq], kb == 0, kb == NT - 1, [kp + "V", f"PT{psl}"], [pk(2 + osl, 0)])
                    mm(psO[:, 512:512 + nq], v_l(1, kb), PT[psl][:, 512:512 + nq], kb == 0, kb == NT - 1,
                       [kp + "V", f"PT{psl}"], [pk(2 + osl, 1)])
                    if kb != NT - 1:
                        return
                    r_ = rd[osl]
                    tmp = fint[osl]
                    rcp(r_[64:128, 0:nq], psO[64:128, 0:nq], [pk(2 + osl, 0)], [f"rd{osl}"])
                    rcp(r_[0:64, 0:nq], psO[0:64, 512:512 + nq], [pk(2 + osl, 1)], [f"rd{osl}"])
                    tt("dve", tmp[0:64, 0:nq], psO[0:64, 0:nq], r_[64:128, 0:nq], ALU.mult, [pk(2 + osl, 0), f"rd{osl}"], [f"fin{osl}"])
                    tt("dve", tmp[64:128, 0:nq], psO[64:128, 512:512 + nq], r_[0:64, 0:nq], ALU.mult,
                       [pk(2 + osl, 1), f"rd{osl}"], [f"fin{osl}"])
                    tt("pool", mixT[:, chunk, q0:q0 + nq], tmp[:, 0:nq], GT[:, q0:q0 + nq], ALU.mult,
                       [f"fin{osl}", kp + "G"], [f"mixT{chunk}_{qg}"])

                import os
                l1_, l2_ = [int(v) for v in os.environ.get("LAG_" + kp, "1,2").split(",")]
                for step in range(n + l2_):
                    if step < n:
                        st_qk(step)
                    if 0 <= step - l1_ < n:
                        st_exp(step - l1_)
                    if 0 <= step - l2_ < n:
                        st_pv(step - l2_)

            def gates_fm(wslice, GT, r, kp):
                for qg in range(5):
                    q0 = qg * 512
                    nq = min(512, TQ - q0)
                    i = qg % 2
                    for c in range(8):
                        mm(pp[i][:, 0:nq], wslice(c), hTq[:, c, q0:q0 + nq], c == 0, c == 7, list(r), [pk(i, 0)])
                    act(GT[:, q0:q0 + nq], pp[i][:, 0:nq], AF.Silu, [pk(i, 0)], [kp + "G"])

            mP2 = mark()
            wA = [sb(f"wA{i}", [128, 8, 256], BF16) for i in range(2)]
            QTA = sb("QTA", [128, TQ], BF16)
            GTA = sb("GTA", [128, TQ], BF16)
            qs = [sb(f"qs{i}", [128, 512], F32) for i in range(3)]
            qbf = [sb(f"qbf{i}", [128, 512], BF16) for i in range(3)]
            PT = [sb(f"PT{i}", [128, 1024], BF16) for i in range(3)]
            rd = [sb(f"rd{i}", [128, 512], F32) for i in range(2)]
            fint = [sb(f"fint{i}", [128, 512], F32) for i in range(2)]
            scr2 = [dict(a=sb(f"s2a{k}", [128, 512], F32), b=sb(f"s2b{k}", [128, 512], F32), c=sb(f"s2c{k}", [128, 512], F32),
                         d=sb(f"s2d{k}", [128, 512], F32), st=sb(f"s2s{k}", [128, 16], F32)) for k in range(2)]
            dma("pool", wA[0][:].rearrange("p c n -> p (c n)"), wA_d[0], (), ["wA0"])
            for g in range(4):
                S.tag = f'p2 g{g}'
                w = wA[g % 2]
                wk = f"wA{g % 2}"
                if g + 1 < 4:
                    dma("pool", wA[(g + 1) % 2][:].rearrange("p c n -> p (c n)"), wA_d[g + 1], (), [f"wA{(g + 1) % 2}"])
                def qa_A(bi, w=w, wk=wk):
                    t0 = bi * 4
                    nt = min(4, NQ - t0)
                    i = (bi + 1) % 2
                    pj = pp[2 + i]
                    for tl in range(nt):
                        t = t0 + tl
                        for c in range(8):
                            mm(pj[:, tl * 128:(tl + 1) * 128], hTq[:, c, t * 128:(t + 1) * 128], w[:, c, 0:128],
                               c == 0, c == 7, [wk], [pk(2 + i, 0)])
                    cp("act", qs[bi % 3][:, 0:nt * 128], pj[:, 0:nt * 128], [pk(2 + i, 0)], [f"qs{bi % 3}"])

                def qa_G(qg, w=w, wk=wk):
                    q0 = qg * 512
                    nq = min(512, TQ - q0)
                    i = qg % 2
                    for c in range(8):
                        mm(pp[i][:, 0:nq], w[:, c, 128:256], hTq[:, c, q0:q0 + nq], c == 0, c == 7, [wk], [pk(i, 0)])
                    act(GTA[:, q0:q0 + nq], pp[i][:, 0:nq], AF.Silu, [pk(i, 0)], ["AG"])

                def qa_B(bi):
                    t0 = bi * 4
                    nt = min(4, NQ - t0)
                    i = (bi + 1) % 2
                    u = bi % 3
                    norm_rope64(qs[u][:, 0:nt * 128], nt, 2, aqn, tabA[:, t0:t0 + nt, 0:64], tabA[:, t0:t0 + nt, 64:128],
                                qbf[u][:, 0:nt * 128], scr2[bi % 2], [f"qs{u}"], [f"qbf{u}"], f"qa{bi % 2}", e2="dve")
                    pS = pp[2 + i][:, 512:1024].bitcast(BF16)
                    for tl in range(nt):
                        tp(pS[:, tl * 128:(tl + 1) * 128], qbf[u][:, tl * 128:(tl + 1) * 128], [f"qbf{u}"], [pk(2 + i, 1)])
                    cp("dve", QTA[:, t0 * 128:(t0 + nt) * 128], pS[:, 0:nt * 128], [pk(2 + i, 1)], ["AQ"])

                for step in range(7):
                    if step < 5:
                        qa_A(step)
                    if 0 <= step - 2 < 5:
                        qa_B(step - 2)
                    if step >= 2:
                        qa_G(step - 2)
                attention_l0(lambda h, kb: KTA[64 * h:64 * h + 64, kb * 128:(kb + 1) * 128],
                             lambda h, q0, nq: QTA[64 * h:64 * h + 64, q0:q0 + nq],
                             lambda h, kb: VA[:, kb, 64 * h:64 * h + 128],
                             SCALE_A, GTA, g, "A", PT, rd, fint)
            S.barrier()
            if stage <= 2:
                raise _Stop()
            release(mLA)

            wuq = sb("wuq", [128, 2, 768], BF16)
            wuk = sb("wuk", [128, 512], BF16)
            wuv = sb("wuv", [128, 512], BF16)
            wgb = [sb(f"wgb{i}", [128, 8, 128], BF16) for i in range(2)]
            KTB = [sb(f"KTB{i}", [128, SEQ], BF16) for i in range(2)]
            QTB = [sb(f"QTB{i}", [128, TQ], BF16) for i in range(2)]
            VB = sb("VB", [128, NT, 192], BF16)
            GTB = sb("GTB", [128, TQ], BF16)
            qs3 = [sb(f"qs3_{i}", [128, 384], F32) for i in range(3)]
            qb3 = [sb(f"qb3_{i}", [128, 384], BF16) for i in range(3)]
            PT = [sb(f"PTb{i}", [128, 1024], BF16) for i in range(3)]
            rd = [sb(f"rdb{i}", [128, 512], F32) for i in range(2)]
            fint = [sb(f"fintb{i}", [128, 512], F32) for i in range(2)]
            s3 = [[sb(f"s3_{i}{k}", [128, 64], F32) for k in range(4)] for i in range(3)]
            dma("pool", wuq[:].rearrange("p c n -> p (c n)"), wuq_d, (), ["wuq"])
            dma("pool", wuk[:], wuk_d, (), ["wuk"])
            dma("pool", wuv[:], wuv_d, (), ["wuv"])
            dma("pool", wgb[0][:].rearrange("p c n -> p (c n)"), wgb_d[0], (), ["wgb0"])
            mset("pool", VB[:], 1.0, ["VB_init"])
            for hh_ in range(2):
                dma("sp", KTB[hh_][64:96, :], kropeT[64:96, :], (), ["BKr"])
            for p in range(4):
                S.tag = f'p3 p{p}'
                wg = wgb[p % 2]
                wgk = f"wgb{p % 2}"
                if p + 1 < 4:
                    dma("pool", wgb[(p + 1) % 2][:].rearrange("p c n -> p (c n)"), wgb_d[p + 1], (), [f"wgb{(p + 1) % 2}"])
                def kb_unit(u_, p=p):
                    hh, kg = u_ // 8, u_ % 8
                    h = 2 * p + hh
                    i = (kg + 1) % 2
                    mm(pp[2 + i][0:64, 0:512], wuk[:, h * 64:(h + 1) * 64], ckvnT[:, kg * 512:(kg + 1) * 512],
                       True, True, ["wuk"], [pk(2 + i, 0)])
                    cp("act" if kg % 2 == 0 else "dve", KTB[hh][0:64, kg * 512:(kg + 1) * 512], pp[2 + i][0:64, 0:512],
                       [pk(2 + i, 0)], ["BK"])

                def vb_unit(kg, p=p):
                    i = (kg + 1) % 2
                    for tl in range(4):
                        kb = kg * 4 + tl
                        mm(pp[2 + i][:, 512 + tl * 128:512 + (tl + 1) * 128], ckvnT[:, kb * 128:(kb + 1) * 128],
                           wuv[:, p * 128:(p + 1) * 128], True, True, ["wuv"], [pk(2 + i, 1)])
                    cp("dve" if kg % 2 == 0 else "act",
                       VB[:, kg * 4:(kg + 1) * 4, :].rearrange("p t (a b) -> p t a b", b=64)[:, :, 0:3:2, :],
                       pp[2 + i][:, 512:1024].rearrange("p (t a b) -> p t a b", t=4, a=2), [pk(2 + i, 1), "VB_init"], ["BV"])

                def qb_A(bi, p=p):
                    t0 = bi * 2
                    nt = min(2, NQ - t0)
                    i = bi % 2
                    pj = pp[i]
                    for tl in range(nt):
                        t = t0 + tl
                        for cc in range(2):
                            mm(pj[:, tl * 192:(tl + 1) * 192], cqnT[:, cc, t * 128:(t + 1) * 128],
                               wuq[:, cc, p * 192:(p + 1) * 192], cc == 0, cc == 1, ["wuq"], [pk(i, 0)])
                    cp("act", qs3[bi % 3][:, 0:nt * 192], pj[:, 0:nt * 192], [pk(i, 0)], [f"qs3{bi % 3}"])

                def qb_B(bi):
                    t0 = bi * 2
                    nt = min(2, NQ - t0)
                    i = bi % 2
                    u = bi % 3
                    W = nt * 192
                    v_s = qs3[u][:, 0:W].rearrange("p (a d) -> p a d", d=96)
                    v_d = qb3[u][:, 0:W].rearrange("p (a d) -> p a d", d=96)
                    cp("pool", v_d[:, :, 0:64], v_s[:, :, 0:64], [f"qs3{u}"], [f"qb3n{u}"])
                    rk = [f"qb3n{u}"]
                    for tl in range(nt):
                        t = t0 + tl
                        rope32(v_s[:, 2 * tl:2 * tl + 2, 64:96], 2,
                               tabB[:, t:t + 1, 0:32].broadcast_to([128, 2, 32]), tabB[:, t:t + 1, 32:64].broadcast_to([128, 2, 32]),
                               v_d[:, 2 * tl:2 * tl + 2, 64:96], (s3[u][2 * tl], s3[u][2 * tl + 1]),
                               [f"qs3{u}"], [f"qb3r{u}_{tl}"], f"qr{u}_{tl}")
                        rk.append(f"qb3r{u}_{tl}")
                    pS = pp[i][:, 512:1024].bitcast(BF16)
                    for tl in range(nt):
                        for hh in range(2):
                            tp(pS[0:96, (tl * 2 + hh) * 192:(tl * 2 + hh) * 192 + 128],
                               qb3[u][:, tl * 192 + hh * 96:tl * 192 + (hh + 1) * 96], rk, [pk(i, 1)])
                    for hh in range(2):
                        cp("dve", QTB[hh][0:96, t0 * 128:(t0 + nt) * 128].rearrange("p (t n) -> p t n", t=nt),
                           pS[0:96, 0:nt * 384].rearrange("p (t h n) -> p t h n", t=nt, h=2)[:, :, hh, 0:128], [pk(i, 1)], ["BQ"])

                for step in range(11):
                    if step < 9:
                        qb_A(step)
                    for u_ in (2 * step, 2 * step + 1):
                        if u_ < 16:
                            kb_unit(u_)
                    if step < 8:
                        vb_unit(step)
                    if 0 <= step - 2 < 9:
                        qb_B(step - 2)
                gates_fm(lambda c, wg=wg: wg[:, c, :], GTB, [wgk], "B")
                attention_l0(lambda h, kb: KTB[h][0:96, kb * 128:(kb + 1) * 128],
                             lambda h, q0, nq: QTB[h][0:96, q0:q0 + nq],
                             lambda h, kb: VB[:, kb, 64 * h:64 * h + 128],
                             SCALE_B, GTB, 4 + p, "B", PT, rd, fint)
            S.barrier()
            if stage <= 3:
                raise _Stop()
            release(mL0)

            x1 = sb("x1", [128, NQ, D], F32)
            mP4 = mark()
            wo = sb("wo", [128, 8, D], BF16)
            xt4 = [sb(f"xt4_{i}", [128, D], F32) for i in range(2)]
            yg = [sb(f"yg{i}", [128, D], F32) for i in range(2)]
            dma("pool", wo[:].rearrange("p c n -> p (c n)"), wout0_d, (), ["wo"])
            for t in range(NQ):
                S.tag = f'p4 t{t}'
                s = t % 2
                dma("sp", xt4[s][:], x_d[t * 128:(t + 1) * 128, :], (), [f"xt4{s}"])
                for half in range(2):
                    for c in range(8):
                        mm(pp[2 * s + half][:, 0:512], mixT[:, c, t * 128:(t + 1) * 128], wo[:, c, half * 512:(half + 1) * 512],
                           c == 0, c == 7, ["wo"], [pk(2 * s + half, 0)])
                    tt("dve", yg[s][:, half * 512:(half + 1) * 512], pp[2 * s + half][:, 0:512], gate_bc[:, half * 512:(half + 1) * 512],
                       ALU.mult, [pk(2 * s + half, 0)], [f"yg{s}_{half}"])
                tt("dve", x1[:, t, :], yg[s][:], xt4[s][:], ALU.add, [f"yg{s}_0", f"yg{s}_1", f"xt4{s}"], [f"x1_{t}"])
                if debug:
                    dma("sp", dbg_d[t * 128:(t + 1) * 128, :], x1[:, t, :], [f"x1_{t}"], [f"dbg{t}"])
            S.barrier()
            if stage <= 4:
                raise _Stop()
            release(mP4)

            mL1 = mark()
            h1T = sb("h1T", [128, 8, TQ], BF16)
            K1T = sb("K1T", [128, 2, TQ], BF16)
            V1 = sb("V1", [128, NQ, 384], BF16)
            dtab = sb("dtab", [128, 384], F32)
            esink = sb("esink", [128, 16], F32)
            mL1a = mark()
            g_bc = sb("g_bc1", [128, D], F32)
            sh_bc = sb("sh_bc1", [128, D], F32)
            emit_mods(1, g_bc, sh_bc, 256)
            xs1 = [sb(f"xs1_{i}", [128, D], BF16) for i in range(2)]
            xm1 = [sb(f"xm1_{i}", [128, D], F32) for i in range(2)]
            wkv1 = sb("wkv1", [128, 8, 512], BF16)
            dma("pool", wkv1[:].rearrange("p c n -> p (c n)"), wkv1_d, (), ["wkv1"])
            dma("sp", dtab[:], dtab_d, (), ["dtab"])
            dma("sp", esink[:], sink_d.partition_broadcast(128), (), ["esink"])
            act(esink[:], esink[:], AF.Exp, ["esink"], ["esink"])
            mset("pool", V1[:], 1.0, ["V1_init"])

            def l1_pre(t):
                s = t % 2
                hT_pre(x1[:, t, :], [], s, xm1[s][:], f"xm1{s}", xs1[s][:], g_bc, sh_bc)

            def l1_tr(t):
                s = t % 2
                hT_tr(s, xs1[s][:], h1T[:, :, t * 128:(t + 1) * 128], [f"h1T{t}"])

            def l1_v(t):
                s = t % 2
                pj = pp[2 * s + 1]
                for c in range(8):
                    mm(pj[:, 0:256], h1T[:, c, t * 128:(t + 1) * 128], wkv1[:, c, 256:512], c == 0, c == 7,
                       [f"h1T{t}", "wkv1"], [pk(2 * s + 1, 0)])
                cp("dve", V1[:, t, :].rearrange("p (j a b) -> p j a b", j=2, a=3)[:, :, 0:3:2, :],
                   pj[:, 0:256].rearrange("p (j a b) -> p j a b", j=2, a=2), [pk(2 * s + 1, 0), "V1_init"], ["V1"])

            for step in range(NQ + 2):
                if step < NQ:
                    l1_pre(step)
                if 0 <= step - 1 < NQ:
                    l1_tr(step - 1)
                if 0 <= step - 2 < NQ:
                    l1_v(step - 2)
            for j in range(2):
                for qg in range(5):
                    q0 = qg * 512
                    nq = min(512, TQ - q0)
                    i = (j * 5 + qg) % 2
                    for c in range(8):
                        mm(pp[i][:, 0:nq], wkv1[:, c, j * 128:(j + 1) * 128], h1T[:, c, q0:q0 + nq], c == 0, c == 7,
                           ["wkv1"] + [f"h1T{t}" for t in range(q0 // 128, (q0 + nq) // 128)], [pk(i, 0)])
                    cp("act" if qg % 2 == 0 else "dve", K1T[:, j, q0:q0 + nq], pp[i][:, 0:nq], [pk(i, 0)], ["K1T"])
            S.barrier()
            if stage <= 5:
                raise _Stop()
            release(mL1a)

            wqg1 = [sb(f"wqg1_{i}", [128, 8, 256], BF16) for i in range(2)]
            Q1T = sb("Q1T", [128, TO], BF16)
            G1T = sb("G1T", [128, TO], BF16)
            Etab = sb("Etab", [128, 16, 384], BF16)
            esb = sb("esb", [128, 1], F32)
            PT1 = [sb(f"PT1_{i}", [128, 768], BF16) for i in range(4)]
            rd1 = [sb("rd1_0", [128, 512], F32)] * 2
            tm1 = [sb("tm1_0", [128, 512], F32)] * 2
            dma("pool", wqg1[0][:].rearrange("p c n -> p (c n)"), wqg1_d[0], (), ["wqg0"])

            slopes = [2.0 ** (-8.0 * (h + 1) / 16.0) for h in range(16)]
            for h_ in range(16):
                act(Etab[:, h_, :], dtab[:], AF.Exp, ["dtab"], ["Etab"], scale=-slopes[h_])
            for P_ in range(8):
                S.tag = f'L1 P{P_}'
                j, g = P_ // 4, P_ % 4
                heads = [(2 * j) * 4 + g, (2 * j + 1) * 4 + g]
                w = wqg1[P_ % 2]
                wk = f"wqg{P_ % 2}"
                if P_ + 1 < 8:
                    dma("pool", wqg1[(P_ + 1) % 2][:].rearrange("p c n -> p (c n)"), wqg1_d[P_ + 1], (), [f"wqg{(P_ + 1) % 2}"])
                cp("pool", esb[64:128, 0:1], esink[64:128, heads[0]:heads[0] + 1], ["esink"], ["esb"])
                cp("pool", esb[0:64, 0:1], esink[0:64, heads[1]:heads[1] + 1], ["esink"], ["esb"])
                for qg in range(4):
                    q0 = qg * 512
                    i = qg % 2
                    for c in range(8):
                        mm(pp[i][:, 0:512], w[:, c, 0:128], h1T[:, c, q0:q0 + 512], c == 0, c == 7, [wk], [pk(i, 0)])
                    cp("dve", Q1T[:, q0:q0 + 512], pp[i][:, 0:512], [pk(i, 0)], ["Q1T"])
                    for c in range(8):
                        mm(pp[i][:, 512:1024], w[:, c, 128:256], h1T[:, c, q0:q0 + 512], c == 0, c == 7, [wk], [pk(i, 1)])
                    act(G1T[:, q0:q0 + 512], pp[i][:, 512:1024], AF.Silu, [pk(i, 1)], ["G1T"])
                its1 = []
                for G in range(4):
                    jb0 = G * 4
                    kbs = list(range(max(jb0 - 1, 0), min(jb0 + 4, 16) + 1))
                    for idx, kb in enumerate(kbs):
                        qlo = max(kb - 1, jb0)
                        qhi = min(kb + 1, jb0 + 3)
                        its1.append(dict(G=G, kb=kb, first=(idx == 0), last=(idx == len(kbs) - 1), qlo=qlo,
                                         nq=(qhi - qlo + 1) * 128, d0=(qlo - (kb - 1)) * 128, lo=(qlo - jb0) * 128))
                n1 = len(its1)

                def l1_qk(i):
                    it = its1[i]
                    ssl = i % 2
                    nq, kb, qlo = it["nq"], it["kb"], it["qlo"]
                    for hh in range(2):
                        mm(pp[ssl][:, hh * 512:hh * 512 + nq], K1T[64 * hh:64 * hh + 64, j, kb * 128:(kb + 1) * 128],
                           Q1T[64 * hh:64 * hh + 64, qlo * 128:qlo * 128 + nq], True, True, ["Q1T"], [pk(ssl, hh)])

                def l1_exp(i):
                    it = its1[i]
                    ssl = i % 2
                    psl = i % 4
                    nq = it["nq"]
                    act(PT1[psl][:].rearrange("p (h n) -> p h n", h=2)[:, :, 0:nq],
                        pp[ssl][:].rearrange("p (h n) -> p h n", h=2)[:, :, 0:nq], AF.Exp,
                        [pk(ssl, 0), pk(ssl, 1)], [f"PT1_{psl}_0", f"PT1_{psl}_1"], scale=SCALE_C)

                def l1_bias(i):
                    it = its1[i]
                    ssl = i % 4
                    nq, d0 = it["nq"], it["d0"]
                    for hh in range(2):
                        tt("dve", PT1[ssl][:, hh * 384:hh * 384 + nq], PT1[ssl][:, hh * 384:hh * 384 + nq],
                           Etab[:, heads[hh], d0:d0 + nq], ALU.mult, [f"PT1_{ssl}_{hh}", "Etab"], [f"PT1_{ssl}_{hh}"])

                def l1_pv(i):
                    it = its1[i]
                    ssl = i % 4
                    osl = it["G"] % 2
                    psO = pp[2 + osl]
                    nq, kb, lo = it["nq"], it["kb"], it["lo"]
                    for hh in range(2):
                        mm(psO[:, hh * 512 + lo:hh * 512 + lo + nq], V1[:, kb, j * 192 + 64 * hh:j * 192 + 64 * hh + 128],
                           PT1[ssl][:, hh * 384:hh * 384 + nq], it["first"], it["last"], [f"PT1_{ssl}_{hh}"], [pk(2 + osl, hh)],
                           skip=True)
                    if not it["last"]:
                        return
                    r_ = rd1[osl]
                    tmp = tm1[osl]
                    q0 = it["G"] * 512
                    act(r_[64:128, :], psO[64:128, 0:512], AF.Ln, [pk(2 + osl, 0), "esb"], ["rd1a"], bias=esb[64:128, 0:1])
                    act(r_[0:64, :], psO[0:64, 512:1024], AF.Ln, [pk(2 + osl, 1), "esb"], ["rd1b"], bias=esb[0:64, 0:1])
                    act(r_[:, :], r_[:, :], AF.Exp, ["rd1a", "rd1b"], ["rd1"], scale=-1.0)
                    tt("dve", tmp[0:64, :], psO[0:64, 0:512], r_[64:128, :], ALU.mult, [pk(2 + osl, 0), "rd1"], ["tm1"])
                    tt("dve", tmp[64:128, :], psO[64:128, 512:1024], r_[0:64, :], ALU.mult, [pk(2 + osl, 1), "rd1"], ["tm1"])
                    tt("pool", mixT[:, P_, q0:q0 + 512], tmp[:, :], G1T[:, q0:q0 + 512], ALU.mult, ["tm1", "G1T"], ["mix1"])

                for step in range(n1 + 3):
                    if 0 <= step - 3 < n1:
                        l1_pv(step - 3)
                    if step < n1:
                        l1_qk(step)
                    if 0 <= step - 1 < n1:
                        l1_exp(step - 1)
                    if 0 <= step - 2 < n1:
                        l1_bias(step - 2)
            S.barrier()
            if stage <= 6:
                raise _Stop()
            release(mL1)

            wo1 = sb("wo1", [128, 8, D], BF16)
            fn_bc = sb("fn_bc", [128, D], F32)
            yg = [sb(f"yg1_{i}", [128, D], F32) for i in range(2)]
            x2 = [sb(f"x2_{i}", [128, D], F32) for i in range(2)]
            ot = [sb(f"ot{i}", [128, D], F32) for i in range(2)]
            dma("pool", wo1[:].rearrange("p c n -> p (c n)"), wout1_d, (), ["wo1"])
            dma("sp", fn_bc[:], fnorm_d.partition_broadcast(128), (), ["fn_bc"])
            for t in range(16):
                s = t % 2
                for half in range(2):
                    for c in range(8):
                        mm(pp[2 * s + half][:, 0:512], mixT[:, c, t * 128:(t + 1) * 128], wo1[:, c, half * 512:(half + 1) * 512],
                           c == 0, c == 7, ["wo1"], [pk(2 * s + half, 0)])
                    tt("dve", yg[s][:, half * 512:(half + 1) * 512], pp[2 * s + half][:, 0:512], gate_bc[:, half * 512:(half + 1) * 512],
                       ALU.mult, [pk(2 * s + half, 0)], [f"yg{s}_{half}"])
                tt("dve", x2[s][:], yg[s][:], x1[:, t, :], ALU.add, [f"yg{s}_0", f"yg{s}_1"], [f"x2_{s}"])
                ss = stat[:, 16 + s:17 + s]
                act(junk[:], x2[s][:], AF.Square, [f"x2_{s}"], ["junk", f"fss{s}"], accum=ss)
                rstd(ss, ss, float(D), [f"fss{s}"], [f"fss{s}"])
                stt("dve", ot[s][:], x2[s][:], ss, fn_bc[:], ALU.mult, ALU.mult, [f"x2_{s}", f"fss{s}", "fn_bc"], [f"ot{s}"])
                dma("sp", y_d[t * 128:(t + 1) * 128, :], ot[s][:], [f"ot{s}"], [f"y{t}"])
                okeys.append(f"y{t}")
        except _Stop:
            pass
        print("SBUF arena peak words:", A["peak"], "of", ARENA_WORDS)
        S.emit(nc, final_wait_keys=okeys)
    nc._sched = S
    return nc


def _rope_cs(pos, dim):
    inv = (np.float32(10000.0) ** (-(np.arange(0, dim, 2, dtype=np.float32)) / np.float32(dim))).astype(np.float32)
    ang = pos.astype(np.float32)[:, None] * inv[None, :]
    return np.cos(ang).astype(np.float32), np.sin(ang).astype(np.float32)


def _tables(tok):
    row = tok // 64
    col = tok % 64
    cr, sr = _rope_cs(row, 32)
    cc, sc_ = _rope_cs(col, 32)
    ct, st_ = _rope_cs(tok, 32)
    cosA = np.concatenate([cr, cr, cc, cc], axis=1)
    sinA = np.concatenate([-sr, sr, -sc_, sc_], axis=1)
    cosB = np.concatenate([ct, ct], axis=1)
    sinB = np.concatenate([-st_, st_], axis=1)

    tab = np.concatenate([cosA, sinA, cosB, sinB], axis=1).astype(np.float32)
    return np.ascontiguousarray(tab.reshape(NT, 128, 192))


def _dtab():
    s = np.arange(128)[:, None]
    q = np.arange(384)[None, :] - 128
    d = np.abs(q - s).astype(np.float32)
    return np.where(d <= 128, d, np.float32(BIGD)).astype(np.float32)


def _prep(inputs):
    f = lambda a: np.ascontiguousarray(np.asarray(a, dtype=np.float32))
    x = f(inputs["x"])
    c = f(inputs["c"])
    w_in = f(inputs["even_w_in"])[0]
    qa, ka, va, ga = w_in[:, 0:512], w_in[:, 512:640], w_in[:, 640:768], w_in[:, 768:1280]
    cq, ckv, kr, gb = w_in[:, 1280:1536], w_in[:, 1536:1664], w_in[:, 1664:1696], w_in[:, 1696:2208]
    wk0 = np.concatenate([ka, va, ckv, kr, cq], axis=1)
    permA = [np.concatenate([np.arange((0 * 4 + g) * 64, (0 * 4 + g) * 64 + 64), np.arange((1 * 4 + g) * 64, (1 * 4 + g) * 64 + 64)])
             for g in range(4)]
    wA = np.stack([np.concatenate([qa[:, permA[g]], ga[:, permA[g]]], axis=1) for g in range(4)])
    wgb = np.stack([gb[:, p * 128:(p + 1) * 128] for p in range(4)])
    wout = f(inputs["even_w_out"])[0]
    rows0 = np.concatenate([np.concatenate(permA), np.arange(512, 1024)])
    wout0 = wout[rows0, :]
    w1 = f(inputs["odd_w_in"])[0]
    qc, kc, vc, gc = w1[:, 0:1024], w1[:, 1024:1280], w1[:, 1280:1536], w1[:, 1536:2560]
    perm1 = []
    for P_ in range(8):
        j, g = P_ // 4, P_ % 4
        hA, hB = (2 * j) * 4 + g, (2 * j + 1) * 4 + g
        perm1.append(np.concatenate([np.arange(hA * 64, hA * 64 + 64), np.arange(hB * 64, hB * 64 + 64)]))
    wqg1 = np.stack([np.concatenate([qc[:, perm1[P_]], gc[:, perm1[P_]]], axis=1) for P_ in range(8)])
    wkv1 = np.concatenate([kc, vc], axis=1)
    wout1 = f(inputs["odd_w_out"])[0][np.concatenate(perm1), :]
    def pm(w):
        cN = w.shape[0] // 128
        return np.ascontiguousarray(w.reshape(cN, 128, w.shape[1]).transpose(1, 0, 2).reshape(128, cN * w.shape[1]))
    adaw = f(inputs["ada_w"]).reshape(2, 8, 128, 12, 256).transpose(0, 3, 2, 1, 4).reshape(2, 12, 128, 8 * 256)
    shared = {
        "ada_w": np.ascontiguousarray(adaw), "ada_b": f(inputs["ada_b"]), "norm_w": f(inputs["norm_w"]),
        "final_norm": f(inputs["final_norm"]), "ident": np.eye(128, dtype=np.float32),
        "wk0": pm(wk0), "wA": np.stack([pm(wA[i]) for i in range(4)]), "wgb": np.stack([pm(wgb[i]) for i in range(4)]),
        "wuq": pm(f(inputs["b_w_uq"])[0]), "wuk": np.ascontiguousarray(f(inputs["b_w_uk"])[0].reshape(128, 512)),
        "wuv": np.ascontiguousarray(f(inputs["b_w_uv"])[0].reshape(128, 512)),
        "wout0": pm(wout0),
        "aqn": f(inputs["a_q_norm"])[0], "akn": f(inputs["a_k_norm"])[0],
        "bqn": f(inputs["b_q_lora_norm"])[0], "bkvn": f(inputs["b_kv_lora_norm"])[0],
        "wkv1": pm(wkv1), "wqg1": np.stack([pm(wqg1[i]) for i in range(8)]), "wout1": pm(wout1),
        "sink": f(inputs["c_sink"])[0], "dtab": _dtab(),
    }
    tabs = [_tables(np.arange(SEQ)), _tables(SEQ - 1 - np.arange(SEQ))]
    in_maps = []
    for core in range(8):
        b, hf = core // 2, core % 2
        xl = x[b] if hf == 0 else x[b][::-1]
        m = dict(shared)
        m["x_loc"] = np.ascontiguousarray(xl)
        m["c_col"] = np.ascontiguousarray(c[b].reshape(8, 128).T)
        m["tab"] = tabs[hf]
        tq = tabs[hf][0:NQ]
        m["tabqa"] = np.ascontiguousarray(tq[:, :, 0:128].transpose(1, 0, 2).reshape(128, NQ * 128))
        m["tabqb"] = np.ascontiguousarray(tq[:, :, 128:192].transpose(1, 0, 2).reshape(128, NQ * 64))
        in_maps.append(m)
    return in_maps


_NC_CACHE = {}


def kernel(**inputs):
    debug = bool(inputs.pop("_debug", False))
    in_maps = _prep(inputs)
    if debug not in _NC_CACHE:
        _NC_CACHE[debug] = build_program(debug)
    nc = _NC_CACHE[debug]
    res = run_bass_kernel_spmd(nc, in_maps, core_ids=list(range(8)))
    out = np.empty((4, SEQ, D), dtype=np.float32)
    for core in range(8):
        b, hf = core // 2, core % 2
        y = np.asarray(res.results[core]["y"], dtype=np.float32)
        if hf == 0:
            out[b, 0:TO] = y
        else:
            out[b, TO:SEQ] = y[::-1]
    if debug:
        return out, [np.asarray(res.results[core]["dbg"]) for core in range(8)], [dict(h=np.asarray(res.results[core]["dbg2"]).astype(np.float32).reshape(128, 8, TQ), k=np.asarray(res.results[core]["dbg3"]).astype(np.float32), m=np.asarray(res.results[core]["dbg4"])) for core in range(8)]
    return out
```

```python
import contextlib
import numpy as np
import concourse.bass as bass
import concourse.mybir as mybir
from concourse.bass_utils import run_bass_kernel_spmd

F32 = mybir.dt.float32
BF16 = mybir.dt.bfloat16
AF = mybir.ActivationFunctionType
ALU = mybir.AluOpType
AX = mybir.AxisListType

D = 1024
SEQ = 4096
NT = 32
NQ = 17
TQ = NQ * 128
TO = 2048
EPS = 1e-6
SCALE_A = 64 ** -0.5
SCALE_B = 96 ** -0.5
SCALE_C = 64 ** -0.5
BIGD = 1.0e5


class _Ins:
    __slots__ = ("eng", "idx", "fn", "deps", "dma", "signal", "sem", "val", "tag", "rw")

    def __init__(self, eng, idx, fn, deps, dma):
        self.eng, self.idx, self.fn, self.deps, self.dma = eng, idx, fn, deps, dma
        self.signal = dma
        self.sem = None
        self.val = 0


class Sched:
    ENGS = ["pe", "act", "dve", "pool", "sp"]
    NDS = 8

    def __init__(self):
        self.ins = {e: [] for e in self.ENGS}
        self.last_w = {}
        self.readers = {}
        self.bar = set()
        self.bar_done = {e: True for e in self.ENGS}
        self.tag = ""
        self.names = {}

    def barrier(self):
        deps = set()
        for e in self.ENGS:
            lst = self.ins[e]
            last_c = None
            nd = 0
            for i in range(len(lst) - 1, -1, -1):
                t = lst[i]
                if t.dma:
                    if nd < self.NDS:
                        deps.add((e, i))
                        nd += 1
                elif last_c is None:
                    last_c = i
                    deps.add((e, i))
                if nd >= self.NDS and last_c is not None:
                    break
        for (e, i) in deps:
            self.ins[e][i].signal = True
        self.bar = deps
        self.bar_done = {e: False for e in self.ENGS}
        self.last_w = {}
        self.readers = {}

    def add(self, eng, fn, reads=(), writes=(), dma=False):
        lst = self.ins[eng]
        idx = len(lst)
        deps = set()
        for k in list(reads) + list(writes):
            w = self.last_w.get(k)
            if w is not None:
                deps.add(w)
        for k in writes:
            for rk, i in self.readers.get(k, {}).items():
                e = rk if isinstance(rk, str) else rk[1]
                deps.add((e, i))
        final = set()
        for (e, i) in deps:
            t = self.ins[e][i]
            if e == eng and eng == "pe" and not t.dma and not dma:
                continue
            t.signal = True
            final.add((e, i))
        if not self.bar_done[eng]:
            final |= {d for d in self.bar if not (d[0] == eng and eng == "pe" and not self.ins[d[0]][d[1]].dma)}
            self.bar_done[eng] = True
        ins = _Ins(eng, idx, fn, final, dma)
        ins.tag = self.tag
        ins.rw = (tuple(reads), tuple(writes))
        lst.append(ins)
        for k in writes:
            self.last_w[k] = (eng, idx)
            self.readers[k] = {}
        for k in reads:
            d = self.readers.setdefault(k, {})
            if dma:
                d[("dma", eng, idx)] = idx
            else:
                d[eng] = max(d.get(eng, -1), idx)
        return ins

    def emit(self, nc, final_wait_keys=()):
        with contextlib.ExitStack() as st:
            csem = {e: st.enter_context(nc.semaphore("c_" + e)) for e in self.ENGS if e != "sp"}
            dsem = {e: [st.enter_context(nc.semaphore(f"d_{e}{j}")) for j in range(self.NDS)]
                    for e in ("sp", "pool", "act")}
            for e in self.ENGS:
                cc = 0
                dc = 0
                for t in self.ins[e]:
                    if t.dma:
                        t.sem = dsem[e][dc % self.NDS]
                        t.val = 16 * (dc // self.NDS + 1)
                        dc += 1
                    elif t.signal:
                        cc += 1
                        t.sem = csem[e]
                        t.val = cc
            final_deps = set()
            for k in final_wait_keys:
                w = self.last_w.get(k)
                if w is not None:
                    final_deps.add(w)
            block = st.enter_context(nc.Block())

            def run(engh, e):
                known = {}

                def waits_raw(sem, v):
                    key = id(sem)
                    if known.get(key, 0) >= v:
                        return
                    engh.wait_ge(sem, v)
                    known[key] = v

                def waits(deps):
                    need = {}
                    for (de, di) in deps:
                        d = self.ins[de][di]
                        key = id(d.sem)
                        if known.get(key, 0) >= d.val:
                            continue
                        if key not in need or need[key][1] < d.val:
                            need[key] = (d.sem, d.val)
                    for key, (s, v) in need.items():
                        engh.wait_ge(s, v)
                        known[key] = v

                for t in self.ins[e]:
                    waits(t.deps)
                    if t.dma and t.val > 16:
                        waits_raw(t.sem, t.val - 16)
                    r = t.fn(engh)
                    try:
                        self.names[r.ins.name] = (t.eng, t.idx, t.tag, t.rw)
                    except Exception:
                        pass
                    if t.signal:
                        r.then_inc(t.sem, 16 if t.dma else 1)
                if e == "sp":
                    waits(final_deps)

            @block.tensor
            def _(h):
                run(h, "pe")

            @block.scalar
            def _(h):
                run(h, "act")

            @block.vector
            def _(h):
                run(h, "dve")

            @block.gpsimd
            def _(h):
                run(h, "pool")

            @block.sync
            def _(h):
                run(h, "sp")


class _Stop(Exception):
    pass


def build_program(debug=False, stage=99):
    nc = bass.Bass("TRN2", target_bir_lowering=False)
    S = Sched()

    def din(name, shape):
        return nc.dram_tensor(name, list(shape), F32, kind="ExternalInput").ap()

    x_d = din("x_loc", [SEQ, D])
    c_d = din("c_col", [128, 8])
    adaw_d = din("ada_w", [2, 12, 128, 8 * 256])
    adab_d = din("ada_b", [2, 3 * D])
    normw_d = din("norm_w", [2, D])
    fnorm_d = din("final_norm", [D])
    ident_d = din("ident", [128, 128])
    wk0_d = din("wk0", [128, 8 * 672])
    wA_d = din("wA", [4, 128, 8 * 256])
    wgb_d = din("wgb", [4, 128, 8 * 128])
    wuq_d = din("wuq", [128, 2 * 768])
    wuk_d = din("wuk", [128, 512])
    wuv_d = din("wuv", [128, 512])
    wout0_d = din("wout0", [128, 8 * D])
    aqn_d = din("aqn", [64])
    akn_d = din("akn", [64])
    bqn_d = din("bqn", [256])
    bkvn_d = din("bkvn", [128])
    tab_d = din("tab", [NT, 128, 192])
    tabqa_d = din("tabqa", [128, NQ * 128])
    tabqb_d = din("tabqb", [128, NQ * 64])
    wkv1_d = din("wkv1", [128, 8 * 512])
    wqg1_d = din("wqg1", [8, 128, 8 * 256])
    wout1_d = din("wout1", [128, 8 * D])
    sink_d = din("sink", [16])
    dtab_d = din("dtab", [128, 384])
    y_d = nc.dram_tensor("y", [TO, D], F32, kind="ExternalOutput").ap()
    dbg_d = None
    if debug:
        dbg_d = nc.dram_tensor("dbg", [TQ, D], F32, kind="ExternalOutput").ap()
        dbg2_d = nc.dram_tensor("dbg2", [128, 8 * TQ], BF16, kind="ExternalOutput").ap()
        dbg3_d = nc.dram_tensor("dbg3", [128, 3 * SEQ + NT * 192 + 2 * TQ], BF16, kind="ExternalOutput").ap()
        dbg4_d = nc.dram_tensor("dbg4", [128, 3 * D], F32, kind="ExternalOutput").ap()

    def mm(out, lhsT, rhs, start, stop, r, w, skip=False):
        if skip:
            S.add("pe", lambda e: e.matmul(out, lhsT=lhsT, rhs=rhs, start=start, stop=stop, skip_group_check=True), r, w)
        else:
            S.add("pe", lambda e: e.matmul(out, lhsT=lhsT, rhs=rhs, start=start, stop=stop), r, w)

    def tp(out, in_, r, w):
        S.add("pe", lambda e: e.transpose(out, in_, ident[:]), list(r) + ["ident"], w)

    def act(out, in_, func, r, w, bias=None, scale=None, accum=None):
        kw = {}
        if bias is not None:
            kw["bias"] = bias
        if scale is not None:
            kw["scale"] = scale
        if accum is not None:
            kw["accum_out"] = accum
        S.add("act", lambda e: e.activation(out=out, in_=in_, func=func, **kw), r, w)

    def tt(eng, out, in0, in1, op, r, w):
        S.add(eng, lambda e: e.tensor_tensor(out=out, in0=in0, in1=in1, op=op), r, w)

    def ts(eng, out, in0, s1, s2, op0, op1, r, w):
        if s2 is None:
            S.add(eng, lambda e: e.tensor_scalar(out=out, in0=in0, scalar1=s1, scalar2=None, op0=op0), r, w)
        else:
            S.add(eng, lambda e: e.tensor_scalar(out=out, in0=in0, scalar1=s1, scalar2=s2, op0=op0, op1=op1), r, w)

    def stt(eng, out, in0, scalar, in1, op0, op1, r, w):
        S.add(eng, lambda e: e.scalar_tensor_tensor(out=out, in0=in0, scalar=scalar, in1=in1, op0=op0, op1=op1), r, w)

    def cp(eng, out, in_, r, w):
        if eng == "act":
            S.add("act", lambda e: e.activation(out=out, in_=in_, func=AF.Copy), r, w)
        else:
            S.add(eng, lambda e: e.tensor_copy(out=out, in_=in_), r, w)

    def red(out, in_, r, w):
        S.add("dve", lambda e: e.tensor_reduce(out=out, in_=in_, axis=AX.X, op=ALU.add), r, w)

    def rcp(out, in_, r, w):
        S.add("dve", lambda e: e.reciprocal(out=out, in_=in_), r, w)

    def mset(eng, ap, val, w):
        S.add(eng, lambda e: e.memset(ap, val), (), w)

    def dma(q, out, in_, r, w):
        S.add(q, lambda e: e.dma_start(out=out, in_=in_), r, w, dma=True)

    def rstd(dst, ss, n, r, w):
        act(dst, ss, AF.Ln, r, w, bias=EPS, scale=1.0 / n)
        act(dst, dst, AF.Exp, w, w, scale=-0.5)

    with contextlib.ExitStack() as outer:
        ARENA_WORDS = 52992
        arena_t = outer.enter_context(nc.sbuf_tensor("arena", [128, ARENA_WORDS], F32))
        A = {"top": 0, "peak": 0}

        def sb(name, shape, dt, st=None):
            n = 1
            for d_ in shape[1:]:
                n *= int(d_)
            nbytes = n * (2 if dt == BF16 else 4)
            w = (nbytes + 31) // 32 * 8
            off = A["top"]
            A["top"] += w
            assert A["top"] <= ARENA_WORDS, ("SBUF arena overflow", name, A["top"])
            A["peak"] = max(A["peak"], A["top"])
            v = arena_t[:, off:off + w]
            if dt == BF16:
                v = v.bitcast(BF16)
            v = v[:, 0:n]
            if len(shape) == 3:
                v = v.rearrange("p (a b) -> p a b", a=int(shape[1]))
            elif len(shape) == 4:
                v = v.rearrange("p (a b c) -> p a b c", a=int(shape[1]), b=int(shape[2]))
            return v

        def mark():
            return A["top"]

        def release(m):
            A["top"] = m

        pp = [outer.enter_context(nc.psum_tensor(f"pp{i}", [128, 1024], F32)) for i in range(4)]

        def pk(i, h):
            return f"pp{i}{'ab'[h]}"

        okeys = []
        try:
            ident = sb("ident", [128, 128], BF16)
            c_sb = sb("c_sb", [128, 8], F32)
            gate_bc = sb("gate_bc", [128, D], F32)
            junk = sb("junk", [128, D], BF16)
            stat = sb("stat", [128, 64], F32)
            mixT = sb("mixT", [128, 8, TQ], BF16)

            dma("pool", ident[:], ident_d, (), ["ident"])
            dma("sp", c_sb[:], c_d, (), ["c_sb"])
            act(c_sb[:], c_sb[:], AF.Silu, ["c_sb"], ["c_sb"])

            def emit_mods(layer, g_bc, sh_bc, SW, nslots=2, queues=("sp",)):
                m = mark()
                ones_f = sb("ones_f", [128, 128], F32)
                crep = sb("crep", [128, 8, 128], F32)
                aw = [sb(f"aw{i}", [128, 8, SW], F32) for i in range(nslots)]
                bb = sb("adab", [128, SW], F32)
                nw = sb("nw", [128, D], F32)
                mset("dve", ones_f[:], 1.0, ["ones_f"])
                for k in range(8):
                    ts("dve", crep[:, k, :], ones_f[:], c_sb[:, k:k + 1], None, ALU.mult, None,
                       ["ones_f", "c_sb"], [f"crep{k}"])
                dma("sp", nw[:], normw_d[layer].partition_broadcast(128), (), ["nw"])
                for s in range(3 * D // SW):
                    sl = s % nslots
                    kind = (s * SW) // D
                    c0 = (s * SW) % D
                    dma(queues[s % len(queues)], aw[sl][:].rearrange("p k n -> p (k n)"), adaw_d[layer][s], (), [f"aw{sl}"])
                    dma("sp", bb[:], adab_d[layer][s * SW:(s + 1) * SW].partition_broadcast(128), (), ["adab"])
                    ps = pp[s % 2][:, 0:SW]
                    for k in range(8):
                        mm(ps, crep[:, k, :], aw[sl][:, k, :], k == 0, k == 7, [f"crep{k}", f"aw{sl}"], [pk(s % 2, 0)])
                    col = slice(c0, c0 + SW)
                    if kind == 0:
                        tt("dve", sh_bc[:, col], ps, bb[:], ALU.add, [pk(s % 2, 0), "adab"], [f"modw{s}"])
                    elif kind == 1:
                        tt("dve", g_bc[:, col], ps, bb[:], ALU.add, [pk(s % 2, 0), "adab"], [f"modw{s}"])
                        stt("dve", g_bc[:, col], g_bc[:, col], 1.0, nw[:, col], ALU.add, ALU.mult,
                            [f"modw{s}", "nw"], [f"modw{s}"])
                    else:
                        tt("dve", gate_bc[:, col], ps, bb[:], ALU.add, [pk(s % 2, 0), "adab"], [f"modw{s}"])
                S.barrier()
                release(m)

            def norm_rope64(src, nt, nh, gain_bc, cos3, sin3, dst, scr, r, w, kp, tk=(), e2="pool"):
                n = nt * nh
                W = n * 64
                sq, qn, t1, t2 = scr["a"], scr["b"], scr["c"], scr["d"]
                ss = scr["st"]
                tt(e2, sq[:, :W], src, src, ALU.mult, r, [kp + "sq"])
                red(ss[:, :n], sq[:, :W].rearrange("p (n d) -> p n d", d=64), [kp + "sq"], [kp + "ss"])
                rstd(ss[:, :n], ss[:, :n], 64.0, [kp + "ss"], [kp + "ss"])
                tt("dve", qn[:, :W].rearrange("p (n d) -> p n d", d=64), src.rearrange("p (n d) -> p n d", d=64),
                   ss[:, :n].unsqueeze(2).broadcast_to([128, n, 64]), ALU.mult, list(r) + [kp + "ss"], [kp + "qn"])
                tt(e2, qn[:, :W].rearrange("p (n d) -> p n d", d=64), qn[:, :W].rearrange("p (n d) -> p n d", d=64),
                   gain_bc[:].unsqueeze(1).broadcast_to([128, n, 64]), ALU.mult, [kp + "qn", "gains"], [kp + "qn"])
                q4 = qn[:, :W].rearrange("p (t h d) -> p t h d", t=nt, h=nh)
                tt(e2, t1[:, :W].rearrange("p (t h d) -> p t h d", t=nt, h=nh), q4,
                   cos3.unsqueeze(2).broadcast_to([128, nt, nh, 64]), ALU.mult, [kp + "qn", "tabs"] + list(tk), [kp + "t1"])
                q6 = qn[:, :W].rearrange("p (t h x r i) -> p t h x r i", t=nt, h=nh, x=2, r=2)
                o6 = t2[:, :W].rearrange("p (t h x r i) -> p t h x r i", t=nt, h=nh, x=2, r=2)
                s5 = sin3.rearrange("p t (x r i) -> p t x r i", x=2, r=2)
                k = 0
                for h in range(nh):
                    for rr in range(2):
                        eng = e2 if k % 2 == 0 else "dve"
                        k += 1
                        tt(eng, o6[:, :, h, :, rr, :], q6[:, :, h, :, 1 - rr, :], s5[:, :, :, rr, :], ALU.mult,
                           [kp + "qn", "tabs"] + list(tk), [kp + f"t2_{h}{rr}"])
                tt("dve", dst, t1[:, :W], t2[:, :W], ALU.add,
                   [kp + "t1"] + [kp + f"t2_{h}{rr}" for h in range(nh) for rr in range(2)], w)

            def rope32(src3, a, cb, sbn, dst3, bufs, r, w, kp):
                u1 = bufs[0][:, :a * 32].rearrange("p (a d) -> p a d", d=32)
                u2 = bufs[1][:, :a * 32].rearrange("p (a d) -> p a d", d=32)
                tt("pool", u1, src3, cb, ALU.mult, list(r) + ["tabs"], [kp + "u1"])
                tt("pool", u2[:, :, 0:16], src3[:, :, 16:32], sbn[:, :, 0:16], ALU.mult, list(r) + ["tabs"], [kp + "u2a"])
                tt("dve", u2[:, :, 16:32], src3[:, :, 0:16], sbn[:, :, 16:32], ALU.mult, list(r) + ["tabs"], [kp + "u2b"])
                tt("dve", dst3, u1, u2, ALU.add, [kp + "u1", kp + "u2a", kp + "u2b"], w)

            MODK = []

            def hT_pre(src_ap, src_keys, s, xm_ap, xm_key, xs_ap, g_bc, sh_bc):
                ss = stat[:, s:s + 1]
                act(junk[:], src_ap, AF.Square, src_keys, ["junk", f"ss{s}"], accum=ss)
                rstd(ss, ss, float(D), [f"ss{s}"], [f"ss{s}"])
                stt("dve", xm_ap, src_ap, ss, g_bc[:], ALU.mult, ALU.mult, list(src_keys) + [f"ss{s}"], [xm_key])
                tt("dve", xs_ap, xm_ap, sh_bc[:], ALU.add, [xm_key], [f"xs{s}"])

            def hT_tr(s, xs_ap, dst, dkeys):
                pT = pp[2 * s][:, 0:512].bitcast(BF16)
                for c in range(8):
                    tp(pT[:, c * 128:(c + 1) * 128], xs_ap[:, c * 128:(c + 1) * 128], [f"xs{s}"], [pk(2 * s, 0)])
                cp("act", dst, pT.rearrange("p (c n) -> p c n", c=8), [pk(2 * s, 0)], dkeys)

            mL0 = mark()
            g_bc = sb("g_bc", [128, D], F32)
            sh_bc = sb("sh_bc", [128, D], F32)
            emit_mods(0, g_bc, sh_bc, 256, nslots=4, queues=("sp", "act"))
            if stage <= 0:
                raise _Stop()
            hTq = sb("hTq", [128, 8, TQ], BF16)
            tabB = sb("tabB", [128, NQ, 64], F32)
            aqn = sb("aqn", [128, 64], F32)
            akn = sb("akn", [128, 64], F32)
            bqn = sb("bqn", [128, 256], F32)
            bkvn = sb("bkvn", [128, 128], F32)
            ckvnT = sb("ckvnT", [128, SEQ], BF16)
            kropeT = sb("kropeT", [128, SEQ], BF16)
            cqnT = sb("cqnT", [128, 2, TQ], BF16)
            mLA = mark()
            KTA = sb("KTA", [128, SEQ], BF16)
            VA = sb("VA", [128, NT, 192], BF16)
            tabA = sb("tabA", [128, NQ, 128], F32)

            dma("sp", tabA[:].rearrange("p t d -> p (t d)"), tabqa_d, (), ["tabs"])
            dma("sp", tabB[:].rearrange("p t d -> p (t d)"), tabqb_d, (), ["tabs"])
            dma("sp", aqn[:], aqn_d.partition_broadcast(128), (), ["gains"])
            dma("sp", akn[:], akn_d.partition_broadcast(128), (), ["gains"])
            dma("sp", bqn[:], bqn_d.partition_broadcast(128), (), ["gains"])
            dma("sp", bkvn[:], bkvn_d.partition_broadcast(128), (), ["gains"])
            mset("pool", VA[:], 1.0, ["VA_init"])

            mP1 = mark()
            wk0 = sb("wk0", [128, 8, 672], BF16)
            xt = [sb(f"xt{i}", [128, D], F32) for i in range(2)]
            xs = [sb(f"xs{i}", [128, D], BF16) for i in range(2)]
            ks = [sb(f"ks{i}", [128, 672], F32) for i in range(3)]
            kbf = [sb(f"kbf{i}", [128, 544], BF16) for i in range(3)]
            hTt = sb("hTt", [128, 3, 8, 128], BF16)
            tabK = [sb(f"tabK{i}", [128, 192], F32) for i in range(3)]
            scr1 = [dict(a=sb("s1a", [128, 128], F32), b=sb("s1b", [128, 128], F32), c=sb("s1c", [128, 128], F32),
                         d=sb("s1d", [128, 128], F32), st=sb("s1s", [128, 8], F32), e=sb("s1e", [128, 32], F32),
                         f=sb("s1f", [128, 32], F32)) for i in range(3)]
            dma("pool", wk0[:].rearrange("p c n -> p (c n)"), wk0_d, (), ["wk0"])

            def p1_ctx(t):
                s = t % 2
                u = t % 3
                isq = t < NQ
                if isq:
                    hdst = hTq[:, :, t * 128:(t + 1) * 128]
                    hk = [f"hTq{t}"]
                    cA, sA = tabA[:, t:t + 1, 0:64], tabA[:, t:t + 1, 64:128]
                    cB, sB = tabB[:, t:t + 1, 0:32], tabB[:, t:t + 1, 32:64]
                    tk = []
                else:
                    hdst = hTt[:, u, :, :]
                    hk = [f"hTt{u}"]
                    cA, sA = tabK[u][:, 0:64].unsqueeze(1), tabK[u][:, 64:128].unsqueeze(1)
                    cB, sB = tabK[u][:, 128:160].unsqueeze(1), tabK[u][:, 160:192].unsqueeze(1)
                    tk = [f"tabK{u}"]
                return s, u, isq, hdst, hk, cA, sA, cB, sB, tk

            def p1_A1(t):
                S.tag = f'p1A1 t{t}'
                s, u, isq, hdst, hk, cA, sA, cB, sB, tk = p1_ctx(t)
                dma("sp", xt[s][:], x_d[t * 128:(t + 1) * 128, :], (), [f"xt{s}"])
                hT_pre(xt[s][:], [f"xt{s}"], s, xt[s][:], f"xt{s}", xs[s][:], g_bc, sh_bc)

            def p1_A2(t):
                S.tag = f'p1A2 t{t}'
                s, u, isq, hdst, hk, cA, sA, cB, sB, tk = p1_ctx(t)
                hT_tr(s, xs[s][:], hdst, hk)

            def p1_A3(t):
                S.tag = f'p1A3 t{t}'
                s, u, isq, hdst, hk, cA, sA, cB, sB, tk = p1_ctx(t)
                if not isq:
                    dma("sp", tabK[u][:], tab_d[t], (), [f"tabK{u}"])
                pj = pp[2 * s + 1]
                for c in range(8):
                    mm(pj[:, 0:416], hdst[:, c, :], wk0[:, c, 0:416], c == 0, c == 7, hk + ["wk0"], [pk(2 * s + 1, 0)])
                cp("act", ks[u][:, 0:416], pj[:, 0:416], [pk(2 * s + 1, 0)], [f"ksa{u}"])
                if isq:
                    for c in range(8):
                        mm(pj[:, 512:768], hdst[:, c, :], wk0[:, c, 416:672], c == 0, c == 7, hk + ["wk0"], [pk(2 * s + 1, 1)])
                    cp("act", ks[u][:, 416:672], pj[:, 512:768], [pk(2 * s + 1, 1)], [f"ksb{u}"])

            def p1_B(t):
                S.tag = f'p1B t{t}'
                s, u, isq, hdst, hk, cA, sA, cB, sB, tk = p1_ctx(t)
                sc = scr1[u]
                norm_rope64(ks[u][:, 0:128], 1, 2, akn, cA, sA, kbf[u][:, 0:128], sc,
                            [f"ksa{u}"] + tk, [f"kbfa{u}"], f"ka{u}", tk=tk)
                cp("pool", VA[:, t, :].rearrange("p (a b) -> p a b", b=64)[:, 0:3:2, :],
                   ks[u][:, 128:256].rearrange("p (a b) -> p a b", b=64), [f"ksa{u}", "VA_init"], [f"VA{t}"])
                ssc = stat[:, 4 + u:5 + u]
                act(junk[:, 0:128], ks[u][:, 256:384], AF.Square, [f"ksa{u}"], ["junk", f"ssc{u}"], accum=ssc)
                rstd(ssc, ssc, 128.0, [f"ssc{u}"], [f"ssc{u}"])
                stt("dve", kbf[u][:, 128:256], ks[u][:, 256:384], ssc, bkvn[:], ALU.mult, ALU.mult,
                    [f"ksa{u}", f"ssc{u}", "gains"], [f"kbfb{u}"])
                rope32(ks[u][:, 384:416].unsqueeze(1), 1, cB, sB, kbf[u][:, 256:288].unsqueeze(1), (sc["e"], sc["f"]),
                       [f"ksa{u}"] + tk, [f"kbfc{u}"], f"kr{u}")
                pS = pp[2 * s][:, 512:1024].bitcast(BF16)
                tp(pS[:, 0:128], kbf[u][:, 0:128], [f"kbfa{u}"], [pk(2 * s, 1)])
                tp(pS[:, 192:320], kbf[u][:, 128:256], [f"kbfb{u}"], [pk(2 * s, 1)])
                if isq:
                    ssq = stat[:, 8 + u:9 + u]
                    act(junk[:, 0:256], ks[u][:, 416:672], AF.Square, [f"ksb{u}"], ["junk", f"ssq{u}"], accum=ssq)
                    rstd(ssq, ssq, 256.0, [f"ssq{u}"], [f"ssq{u}"])
                    stt("dve", kbf[u][:, 288:544], ks[u][:, 416:672], ssq, bqn[:], ALU.mult, ALU.mult,
                        [f"ksb{u}", f"ssq{u}", "gains"], [f"kbfd{u}"])
                    tp(pS[:, 384:512], kbf[u][:, 288:416], [f"kbfd{u}"], [pk(2 * s, 1)])
                    tp(pS[:, 576:704], kbf[u][:, 416:544], [f"kbfd{u}"], [pk(2 * s, 1)])
                tp(pS[0:32, 768:896], kbf[u][:, 256:288], [f"kbfc{u}"], [pk(2 * s, 1)])
                cp("dve", KTA[:, t * 128:(t + 1) * 128], pS[:, 0:128], [pk(2 * s, 1)], [f"KTA{t}"])
                cp("dve", ckvnT[:, t * 128:(t + 1) * 128], pS[:, 192:320], [pk(2 * s, 1)], [f"ckvnT{t}"])
                cp("dve", kropeT[64:96, t * 128:(t + 1) * 128], pS[0:32, 768:896], [pk(2 * s, 1)], [f"kropeT{t}"])
                if isq:
                    cp("dve", cqnT[:, :, t * 128:(t + 1) * 128], pS[:, 384:768].rearrange("p (c n) -> p c n", c=2)[:, :, 0:128],
                       [pk(2 * s, 1)], [f"cqnT{t}"])

            for step in range(NT + 4):
                if step < NT:
                    p1_A1(step)
                if 0 <= step - 1 < NT:
                    p1_A2(step - 1)
                if 0 <= step - 2 < NT:
                    p1_A3(step - 2)
                if 0 <= step - 4 < NT:
                    p1_B(step - 4)
            S.barrier()
            if debug:
                dma("sp", dbg2_d, hTq[:].rearrange("p c n -> p (c n)"), (), ["dbg2"])
                dma("sp", dbg3_d[:, 0:SEQ], KTA[:], (), ["dbg3a"])
                dma("sp", dbg3_d[:, SEQ:2 * SEQ], ckvnT[:], (), ["dbg3b"])
                dma("sp", dbg3_d[64:96, 2 * SEQ:3 * SEQ], kropeT[64:96, :], (), ["dbg3c"])
                dma("sp", dbg3_d[:, 3 * SEQ:3 * SEQ + NT * 192], VA[:].rearrange("p t n -> p (t n)"), (), ["dbg3d"])
                dma("sp", dbg3_d[:, 3 * SEQ + NT * 192:], cqnT[:].rearrange("p c n -> p (c n)"), (), ["dbg3e"])
                dma("sp", dbg4_d[:, 0:D], g_bc[:], (), ["dbg4a"])
                dma("sp", dbg4_d[:, D:2 * D], sh_bc[:], (), ["dbg4b"])
                dma("sp", dbg4_d[:, 2 * D:3 * D], gate_bc[:], (), ["dbg4c"])
                S.barrier()
            if stage <= 1:
                raise _Stop()
            release(mP1)

            def attention_l0(qk_l, qk_r, v_l, scale, GT, chunk, kp, PT, rd, fint):
                its = [(qg, kb) for qg in range(5) for kb in range(NT)]
                n = len(its)

                QSZ = [448, 448, 448, 448, 384]
                QOFF = [0, 448, 896, 1344, 1792]

                def geo(i):
                    qg, kb = its[i]
                    return qg, kb, QOFF[qg], QSZ[qg], qg % 2, i % 2, i % 3

                def st_qk(i):
                    qg, kb, q0, nq, osl, ssl, psl = geo(i)
                    psS = pp[ssl]
                    mm(psS[:, 0:nq], qk_l(0, kb), qk_r(0, q0, nq), True, True, [kp + "K", kp + "Kr", kp + "Q"], [pk(ssl, 0)])
                    mm(psS[:, 512:512 + nq], qk_l(1, kb), qk_r(1, q0, nq), True, True, [kp + "K", kp + "Kr", kp + "Q"], [pk(ssl, 1)])

                def st_exp(i):
                    qg, kb, q0, nq, osl, ssl, psl = geo(i)
                    act(PT[psl][:].rearrange("p (h n) -> p h n", h=2)[:, :, 0:nq],
                        pp[ssl][:].rearrange("p (h n) -> p h n", h=2)[:, :, 0:nq], AF.Exp,
                        [pk(ssl, 0), pk(ssl, 1)], [f"PT{psl}"], scale=scale)

                def st_pv(i):
                    qg, kb, q0, nq, osl, ssl, psl = geo(i)
                    psO = pp[2 + osl]
                    mm(psO[:, 0:nq], v_l(0, kb), PT[psl][:, 0:nq], kb == 0, kb == NT - 1, [kp + "V", f"PT{psl}"], [pk(2 + osl, 0)])
                    mm(psO[:, 512:512 + nq], v_l(1, kb), PT[psl][:, 512:512 + nq], kb == 0, kb == NT - 1,
                       [kp + "V", f"PT{psl}"], [pk(2 + osl, 1)])
                    if kb != NT - 1:
                        return
                    r_ = rd[osl]
                    tmp = fint[osl]
                    rcp(r_[64:128, 0:nq], psO[64:128, 0:nq], [pk(2 + osl, 0)], [f"rd{osl}"])
                    rcp(r_[0:64, 0:nq], psO[0:64, 512:512 + nq], [pk(2 + osl, 1)], [f"rd{osl}"])
                    tt("dve", tmp[0:64, 0:nq], psO[0:64, 0:nq], r_[64:128, 0:nq], ALU.mult, [pk(2 + osl, 0), f"rd{osl}"], [f"fin{osl}"])
                    tt("dve", tmp[64:128, 0:nq], psO[64:128, 512:512 + nq], r_[0:64, 0:nq], ALU.mult,
                       [pk(2 + osl, 1), f"rd{osl}"], [f"fin{osl}"])
                    tt("pool", mixT[:, chunk, q0:q0 + nq], tmp[:, 0:nq], GT[:, q0:q0 + nq], ALU.mult,
                       [f"fin{osl}", kp + "G"], [f"mixT{chunk}_{qg}"])

                import os
                l1_, l2_ = [int(v) for v in os.environ.get("LAG_" + kp, "1,2").split(",")]
                for step in range(n + l2_):
                    if step < n:
                        st_qk(step)
                    if 0 <= step - l1_ < n:
                        st_exp(step - l1_)
                    if 0 <= step - l2_ < n:
                        st_pv(step - l2_)

            def gates_fm(wslice, GT, r, kp):
                for qg in range(5):
                    q0 = qg * 512
                    nq = min(512, TQ - q0)
                    i = qg % 2
                    for c in range(8):
                        mm(pp[i][:, 0:nq], wslice(c), hTq[:, c, q0:q0 + nq], c == 0, c == 7, list(r), [pk(i, 0)])
                    act(GT[:, q0:q0 + nq], pp[i][:, 0:nq], AF.Silu, [pk(i, 0)], [kp + "G"])

            mP2 = mark()
            wA = [sb(f"wA{i}", [128, 8, 256], BF16) for i in range(2)]
            QTA = sb("QTA", [128, TQ], BF16)
            GTA = sb("GTA", [128, TQ], BF16)
            qs = [sb(f"qs{i}", [128, 512], F32) for i in range(3)]
            qbf = [sb(f"qbf{i}", [128, 512], BF16) for i in range(3)]
            PT = [sb(f"PT{i}", [128, 1024], BF16) for i in range(3)]
            rd = [sb(f"rd{i}", [128, 512], F32) for i in range(2)]
            fint = [sb(f"fint{i}", [128, 512], F32) for i in range(2)]
            scr2 = [dict(a=sb(f"s2a{k}", [128, 512], F32), b=sb(f"s2b{k}", [128, 512], F32), c=sb(f"s2c{k}", [128, 512], F32),
                         d=sb(f"s2d{k}", [128, 512], F32), st=sb(f"s2s{k}", [128, 16], F32)) for k in range(2)]
            dma("pool", wA[0][:].rearrange("p c n -> p (c n)"), wA_d[0], (), ["wA0"])
            for g in range(4):
                S.tag = f'p2 g{g}'
                w = wA[g % 2]
                wk = f"wA{g % 2}"
                if g + 1 < 4:
                    dma("pool", wA[(g + 1) % 2][:].rearrange("p c n -> p (c n)"), wA_d[g + 1], (), [f"wA{(g + 1) % 2}"])
                def qa_A(bi, w=w, wk=wk):
                    t0 = bi * 4
                    nt = min(4, NQ - t0)
                    i = (bi + 1) % 2
                    pj = pp[2 + i]
                    for tl in range(nt):
                        t = t0 + tl
                        for c in range(8):
                            mm(pj[:, tl * 128:(tl + 1) * 128], hTq[:, c, t * 128:(t + 1) * 128], w[:, c, 0:128],
                               c == 0, c == 7, [wk], [pk(2 + i, 0)])
                    cp("act", qs[bi % 3][:, 0:nt * 128], pj[:, 0:nt * 128], [pk(2 + i, 0)], [f"qs{bi % 3}"])

                def qa_G(qg, w=w, wk=wk):
                    q0 = qg * 512
                    nq = min(512, TQ - q0)
                    i = qg % 2
                    for c in range(8):
                        mm(pp[i][:, 0:nq], w[:, c, 128:256], hTq[:, c, q0:q0 + nq], c == 0, c == 7, [wk], [pk(i, 0)])
                    act(GTA[:, q0:q0 + nq], pp[i][:, 0:nq], AF.Silu, [pk(i, 0)], ["AG"])

                def qa_B(bi):
                    t0 = bi * 4
                    nt = min(4, NQ - t0)
                    i = (bi + 1) % 2
                    u = bi % 3
                    norm_rope64(qs[u][:, 0:nt * 128], nt, 2, aqn, tabA[:, t0:t0 + nt, 0:64], tabA[:, t0:t0 + nt, 64:128],
                                qbf[u][:, 0:nt * 128], scr2[bi % 2], [f"qs{u}"], [f"qbf{u}"], f"qa{bi % 2}", e2="dve")
                    pS = pp[2 + i][:, 512:1024].bitcast(BF16)
                    for tl in range(nt):
                        tp(pS[:, tl * 128:(tl + 1) * 128], qbf[u][:, tl * 128:(tl + 1) * 128], [f"qbf{u}"], [pk(2 + i, 1)])
                    cp("dve", QTA[:, t0 * 128:(t0 + nt) * 128], pS[:, 0:nt * 128], [pk(2 + i, 1)], ["AQ"])

                for step in range(7):
                    if step < 5:
                        qa_A(step)
                        qa_G(step)
                    if 0 <= step - 2 < 5:
                        qa_B(step - 2)
                attention_l0(lambda h, kb: KTA[64 * h:64 * h + 64, kb * 128:(kb + 1) * 128],
                             lambda h, q0, nq: QTA[64 * h:64 * h + 64, q0:q0 + nq],
                             lambda h, kb: VA[:, kb, 64 * h:64 * h + 128],
                             SCALE_A, GTA, g, "A", PT, rd, fint)
            S.barrier()
            if stage <= 2:
                raise _Stop()
            release(mLA)

            wuq = sb("wuq", [128, 2, 768], BF16)
            wuk = sb("wuk", [128, 512], BF16)
            wuv = sb("wuv", [128, 512], BF16)
            wgb = [sb(f"wgb{i}", [128, 8, 128], BF16) for i in range(2)]
            KTB = [sb(f"KTB{i}", [128, SEQ], BF16) for i in range(2)]
            QTB = [sb(f"QTB{i}", [128, TQ], BF16) for i in range(2)]
            VB = sb("VB", [128, NT, 192], BF16)
            GTB = sb("GTB", [128, TQ], BF16)
            qs3 = [sb(f"qs3_{i}", [128, 384], F32) for i in range(3)]
            qb3 = [sb(f"qb3_{i}", [128, 384], BF16) for i in range(3)]
            PT = [sb(f"PTb{i}", [128, 1024], BF16) for i in range(3)]
            rd = [sb(f"rdb{i}", [128, 512], F32) for i in range(2)]
            fint = [sb(f"fintb{i}", [128, 512], F32) for i in range(2)]
            s3 = [[sb(f"s3_{i}{k}", [128, 64], F32) for k in range(4)] for i in range(3)]
            dma("pool", wuq[:].rearrange("p c n -> p (c n)"), wuq_d, (), ["wuq"])
            dma("pool", wuk[:], wuk_d, (), ["wuk"])
            dma("pool", wuv[:], wuv_d, (), ["wuv"])
            dma("pool", wgb[0][:].rearrange("p c n -> p (c n)"), wgb_d[0], (), ["wgb0"])
            mset("pool", VB[:], 1.0, ["VB_init"])
            for hh_ in range(2):
                dma("sp", KTB[hh_][64:96, :], kropeT[64:96, :], (), ["BKr"])
            for p in range(4):
                S.tag = f'p3 p{p}'
                wg = wgb[p % 2]
                wgk = f"wgb{p % 2}"
                if p + 1 < 4:
                    dma("pool", wgb[(p + 1) % 2][:].rearrange("p c n -> p (c n)"), wgb_d[p + 1], (), [f"wgb{(p + 1) % 2}"])
                def kb_unit(u_, p=p):
                    hh, kg = u_ // 8, u_ % 8
                    h = 2 * p + hh
                    i = (kg + 1) % 2
                    mm(pp[2 + i][0:64, 0:512], wuk[:, h * 64:(h + 1) * 64], ckvnT[:, kg * 512:(kg + 1) * 512],
                       True, True, ["wuk"], [pk(2 + i, 0)])
                    cp("act" if kg % 2 == 0 else "dve", KTB[hh][0:64, kg * 512:(kg + 1) * 512], pp[2 + i][0:64, 0:512],
                       [pk(2 + i, 0)], ["BK"])

                def vb_unit(kg, p=p):
                    i = (kg + 1) % 2
                    for tl in range(4):
                        kb = kg * 4 + tl
                        mm(pp[2 + i][:, 512 + tl * 128:512 + (tl + 1) * 128], ckvnT[:, kb * 128:(kb + 1) * 128],
                           wuv[:, p * 128:(p + 1) * 128], True, True, ["wuv"], [pk(2 + i, 1)])
                    cp("dve" if kg % 2 == 0 else "act",
                       VB[:, kg * 4:(kg + 1) * 4, :].rearrange("p t (a b) -> p t a b", b=64)[:, :, 0:3:2, :],
                       pp[2 + i][:, 512:1024].rearrange("p (t a b) -> p t a b", t=4, a=2), [pk(2 + i, 1), "VB_init"], ["BV"])

                def qb_A(bi, p=p):
                    t0 = bi * 2
                    nt = min(2, NQ - t0)
                    i = bi % 2
                    pj = pp[i]
                    for tl in range(nt):
                        t = t0 + tl
                        for cc in range(2):
                            mm(pj[:, tl * 192:(tl + 1) * 192], cqnT[:, cc, t * 128:(t + 1) * 128],
                               wuq[:, cc, p * 192:(p + 1) * 192], cc == 0, cc == 1, ["wuq"], [pk(i, 0)])
                    cp("act", qs3[bi % 3][:, 0:nt * 192], pj[:, 0:nt * 192], [pk(i, 0)], [f"qs3{bi % 3}"])

                def qb_B(bi):
                    t0 = bi * 2
                    nt = min(2, NQ - t0)
                    i = bi % 2
                    u = bi % 3
                    W = nt * 192
                    v_s = qs3[u][:, 0:W].rearrange("p (a d) -> p a d", d=96)
                    v_d = qb3[u][:, 0:W].rearrange("p (a d) -> p a d", d=96)
                    cp("pool", v_d[:, :, 0:64], v_s[:, :, 0:64], [f"qs3{u}"], [f"qb3n{u}"])
                    rk = [f"qb3n{u}"]
                    for tl in range(nt):
                        t = t0 + tl
                        rope32(v_s[:, 2 * tl:2 * tl + 2, 64:96], 2,
                               tabB[:, t:t + 1, 0:32].broadcast_to([128, 2, 32]), tabB[:, t:t + 1, 32:64].broadcast_to([128, 2, 32]),
                               v_d[:, 2 * tl:2 * tl + 2, 64:96], (s3[u][2 * tl], s3[u][2 * tl + 1]),
                               [f"qs3{u}"], [f"qb3r{u}_{tl}"], f"qr{u}_{tl}")
                        rk.append(f"qb3r{u}_{tl}")
                    pS = pp[i][:, 512:1024].bitcast(BF16)
                    for tl in range(nt):
                        for hh in range(2):
                            tp(pS[0:96, (tl * 2 + hh) * 192:(tl * 2 + hh) * 192 + 128],
                               qb3[u][:, tl * 192 + hh * 96:tl * 192 + (hh + 1) * 96], rk, [pk(i, 1)])
                    for hh in range(2):
                        cp("dve", QTB[hh][0:96, t0 * 128:(t0 + nt) * 128].rearrange("p (t n) -> p t n", t=nt),
                           pS[0:96, 0:nt * 384].rearrange("p (t h n) -> p t h n", t=nt, h=2)[:, :, hh, 0:128], [pk(i, 1)], ["BQ"])

                for step in range(11):
                    if step < 9:
                        qb_A(step)
                    for u_ in (2 * step, 2 * step + 1):
                        if u_ < 16:
                            kb_unit(u_)
                    if step < 8:
                        vb_unit(step)
                    if 0 <= step - 2 < 9:
                        qb_B(step - 2)
                gates_fm(lambda c, wg=wg: wg[:, c, :], GTB, [wgk], "B")
                attention_l0(lambda h, kb: KTB[h][0:96, kb * 128:(kb + 1) * 128],
                             lambda h, q0, nq: QTB[h][0:96, q0:q0 + nq],
                             lambda h, kb: VB[:, kb, 64 * h:64 * h + 128],
                             SCALE_B, GTB, 4 + p, "B", PT, rd, fint)
            S.barrier()
            if stage <= 3:
                raise _Stop()
            release(mL0)

            x1 = sb("x1", [128, NQ, D], F32)
            mP4 = mark()
            wo = sb("wo", [128, 8, D], BF16)
            xt4 = [sb(f"xt4_{i}", [128, D], F32) for i in range(2)]
            yg = [sb(f"yg{i}", [128, D], F32) for i in range(2)]
            dma("pool", wo[:].rearrange("p c n -> p (c n)"), wout0_d, (), ["wo"])
            for t in range(NQ):
                S.tag = f'p4 t{t}'
                s = t % 2
                dma("sp", xt4[s][:], x_d[t * 128:(t + 1) * 128, :], (), [f"xt4{s}"])
                for half in range(2):
                    for c in range(8):
                        mm(pp[2 * s + half][:, 0:512], mixT[:, c, t * 128:(t + 1) * 128], wo[:, c, half * 512:(half + 1) * 512],
                           c == 0, c == 7, ["wo"], [pk(2 * s + half, 0)])
                    tt("dve", yg[s][:, half * 512:(half + 1) * 512], pp[2 * s + half][:, 0:512], gate_bc[:, half * 512:(half + 1) * 512],
                       ALU.mult, [pk(2 * s + half, 0)], [f"yg{s}_{half}"])
                tt("dve", x1[:, t, :], yg[s][:], xt4[s][:], ALU.add, [f"yg{s}_0", f"yg{s}_1", f"xt4{s}"], [f"x1_{t}"])
                if debug:
                    dma("sp", dbg_d[t * 128:(t + 1) * 128, :], x1[:, t, :], [f"x1_{t}"], [f"dbg{t}"])
            S.barrier()
            if stage <= 4:
                raise _Stop()
            release(mP4)

            mL1 = mark()
            h1T = sb("h1T", [128, 8, TQ], BF16)
            K1T = sb("K1T", [128, 2, TQ], BF16)
            V1 = sb("V1", [128, NQ, 384], BF16)
            dtab = sb("dtab", [128, 384], F32)
            esink = sb("esink", [128, 16], F32)
            mL1a = mark()
            g_bc = sb("g_bc1", [128, D], F32)
            sh_bc = sb("sh_bc1", [128, D], F32)
            emit_mods(1, g_bc, sh_bc, 256)
            xs1 = [sb(f"xs1_{i}", [128, D], BF16) for i in range(2)]
            xm1 = [sb(f"xm1_{i}", [128, D], F32) for i in range(2)]
            wkv1 = sb("wkv1", [128, 8, 512], BF16)
            dma("pool", wkv1[:].rearrange("p c n -> p (c n)"), wkv1_d, (), ["wkv1"])
            dma("sp", dtab[:], dtab_d, (), ["dtab"])
            dma("sp", esink[:], sink_d.partition_broadcast(128), (), ["esink"])
            act(esink[:], esink[:], AF.Exp, ["esink"], ["esink"])
            mset("pool", V1[:], 1.0, ["V1_init"])

            def l1_pre(t):
                s = t % 2
                hT_pre(x1[:, t, :], [], s, xm1[s][:], f"xm1{s}", xs1[s][:], g_bc, sh_bc)

            def l1_tr(t):
                s = t % 2
                hT_tr(s, xs1[s][:], h1T[:, :, t * 128:(t + 1) * 128], [f"h1T{t}"])

            def l1_v(t):
                s = t % 2
                pj = pp[2 * s + 1]
                for c in range(8):
                    mm(pj[:, 0:256], h1T[:, c, t * 128:(t + 1) * 128], wkv1[:, c, 256:512], c == 0, c == 7,
                       [f"h1T{t}", "wkv1"], [pk(2 * s + 1, 0)])
                cp("dve", V1[:, t, :].rearrange("p (j a b) -> p j a b", j=2, a=3)[:, :, 0:3:2, :],
                   pj[:, 0:256].rearrange("p (j a b) -> p j a b", j=2, a=2), [pk(2 * s + 1, 0), "V1_init"], ["V1"])

            for step in range(NQ + 2):
                if step < NQ:
                    l1_pre(step)
                if 0 <= step - 1 < NQ:
                    l1_tr(step - 1)
                if 0 <= step - 2 < NQ:
                    l1_v(step - 2)
            for j in range(2):
                for qg in range(5):
                    q0 = qg * 512
                    nq = min(512, TQ - q0)
                    i = (j * 5 + qg) % 2
                    for c in range(8):
                        mm(pp[i][:, 0:nq], wkv1[:, c, j * 128:(j + 1) * 128], h1T[:, c, q0:q0 + nq], c == 0, c == 7,
                           ["wkv1"] + [f"h1T{t}" for t in range(q0 // 128, (q0 + nq) // 128)], [pk(i, 0)])
                    cp("act" if qg % 2 == 0 else "dve", K1T[:, j, q0:q0 + nq], pp[i][:, 0:nq], [pk(i, 0)], ["K1T"])
            S.barrier()
            if stage <= 5:
                raise _Stop()
            release(mL1a)

            wqg1 = [sb(f"wqg1_{i}", [128, 8, 256], BF16) for i in range(2)]
            Q1T = sb("Q1T", [128, TO], BF16)
            G1T = sb("G1T", [128, TO], BF16)
            Etab = sb("Etab", [128, 16, 384], BF16)
            esb = sb("esb", [128, 1], F32)
            PT1 = [sb(f"PT1_{i}", [128, 768], BF16) for i in range(4)]
            rd1 = [sb("rd1_0", [128, 512], F32)] * 2
            tm1 = [sb("tm1_0", [128, 512], F32)] * 2
            dma("pool", wqg1[0][:].rearrange("p c n -> p (c n)"), wqg1_d[0], (), ["wqg0"])

            slopes = [2.0 ** (-8.0 * (h + 1) / 16.0) for h in range(16)]
            for h_ in range(16):
                act(Etab[:, h_, :], dtab[:], AF.Exp, ["dtab"], ["Etab"], scale=-slopes[h_])
            for P_ in range(8):
                S.tag = f'L1 P{P_}'
                j, g = P_ // 4, P_ % 4
                heads = [(2 * j) * 4 + g, (2 * j + 1) * 4 + g]
                w = wqg1[P_ % 2]
                wk = f"wqg{P_ % 2}"
                if P_ + 1 < 8:
                    dma("pool", wqg1[(P_ + 1) % 2][:].rearrange("p c n -> p (c n)"), wqg1_d[P_ + 1], (), [f"wqg{(P_ + 1) % 2}"])
                cp("pool", esb[64:128, 0:1], esink[64:128, heads[0]:heads[0] + 1], ["esink"], ["esb"])
                cp("pool", esb[0:64, 0:1], esink[0:64, heads[1]:heads[1] + 1], ["esink"], ["esb"])
                for qg in range(4):
                    q0 = qg * 512
                    i = qg % 2
                    for c in range(8):
                        mm(pp[i][:, 0:512], w[:, c, 0:128], h1T[:, c, q0:q0 + 512], c == 0, c == 7, [wk], [pk(i, 0)])
                    cp("dve", Q1T[:, q0:q0 + 512], pp[i][:, 0:512], [pk(i, 0)], ["Q1T"])
                    for c in range(8):
                        mm(pp[i][:, 512:1024], w[:, c, 128:256], h1T[:, c, q0:q0 + 512], c == 0, c == 7, [wk], [pk(i, 1)])
                    act(G1T[:, q0:q0 + 512], pp[i][:, 512:1024], AF.Silu, [pk(i, 1)], ["G1T"])
                its1 = []
                for G in range(4):
                    jb0 = G * 4
                    kbs = list(range(max(jb0 - 1, 0), min(jb0 + 4, 16) + 1))
                    for idx, kb in enumerate(kbs):
                        qlo = max(kb - 1, jb0)
                        qhi = min(kb + 1, jb0 + 3)
                        its1.append(dict(G=G, kb=kb, first=(idx == 0), last=(idx == len(kbs) - 1), qlo=qlo,
                                         nq=(qhi - qlo + 1) * 128, d0=(qlo - (kb - 1)) * 128, lo=(qlo - jb0) * 128))
                n1 = len(its1)

                def l1_qk(i):
                    it = its1[i]
                    ssl = i % 2
                    nq, kb, qlo = it["nq"], it["kb"], it["qlo"]
                    for hh in range(2):
                        mm(pp[ssl][:, hh * 512:hh * 512 + nq], K1T[64 * hh:64 * hh + 64, j, kb * 128:(kb + 1) * 128],
                           Q1T[64 * hh:64 * hh + 64, qlo * 128:qlo * 128 + nq], True, True, ["Q1T"], [pk(ssl, hh)])

                def l1_exp(i):
                    it = its1[i]
                    ssl = i % 2
                    psl = i % 4
                    nq = it["nq"]
                    act(PT1[psl][:].rearrange("p (h n) -> p h n", h=2)[:, :, 0:nq],
                        pp[ssl][:].rearrange("p (h n) -> p h n", h=2)[:, :, 0:nq], AF.Exp,
                        [pk(ssl, 0), pk(ssl, 1)], [f"PT1_{psl}_0", f"PT1_{psl}_1"], scale=SCALE_C)

                def l1_bias(i):
                    it = its1[i]
                    ssl = i % 4
                    nq, d0 = it["nq"], it["d0"]
                    for hh in range(2):
                        tt("dve", PT1[ssl][:, hh * 384:hh * 384 + nq], PT1[ssl][:, hh * 384:hh * 384 + nq],
                           Etab[:, heads[hh], d0:d0 + nq], ALU.mult, [f"PT1_{ssl}_{hh}", "Etab"], [f"PT1_{ssl}_{hh}"])

                def l1_pv(i):
                    it = its1[i]
                    ssl = i % 4
                    osl = it["G"] % 2
                    psO = pp[2 + osl]
                    nq, kb, lo = it["nq"], it["kb"], it["lo"]
                    for hh in range(2):
                        mm(psO[:, hh * 512 + lo:hh * 512 + lo + nq], V1[:, kb, j * 192 + 64 * hh:j * 192 + 64 * hh + 128],
                           PT1[ssl][:, hh * 384:hh * 384 + nq], it["first"], it["last"], [f"PT1_{ssl}_{hh}"], [pk(2 + osl, hh)],
                           skip=True)
                    if not it["last"]:
                        return
                    r_ = rd1[osl]
                    tmp = tm1[osl]
                    q0 = it["G"] * 512
                    act(r_[64:128, :], psO[64:128, 0:512], AF.Ln, [pk(2 + osl, 0), "esb"], ["rd1a"], bias=esb[64:128, 0:1])
                    act(r_[0:64, :], psO[0:64, 512:1024], AF.Ln, [pk(2 + osl, 1), "esb"], ["rd1b"], bias=esb[0:64, 0:1])
                    act(r_[:, :], r_[:, :], AF.Exp, ["rd1a", "rd1b"], ["rd1"], scale=-1.0)
                    tt("dve", tmp[0:64, :], psO[0:64, 0:512], r_[64:128, :], ALU.mult, [pk(2 + osl, 0), "rd1"], ["tm1"])
                    tt("dve", tmp[64:128, :], psO[64:128, 512:1024], r_[0:64, :], ALU.mult, [pk(2 + osl, 1), "rd1"], ["tm1"])
                    tt("pool", mixT[:, P_, q0:q0 + 512], tmp[:, :], G1T[:, q0:q0 + 512], ALU.mult, ["tm1", "G1T"], ["mix1"])

                for step in range(n1 + 3):
                    if 0 <= step - 3 < n1:
                        l1_pv(step - 3)
                    if step < n1:
                        l1_qk(step)
                    if 0 <= step - 1 < n1:
                        l1_exp(step - 1)
                    if 0 <= step - 2 < n1:
                        l1_bias(step - 2)
            S.barrier()
            if stage <= 6:
                raise _Stop()
            release(mL1)

            wo1 = sb("wo1", [128, 8, D], BF16)
            fn_bc = sb("fn_bc", [128, D], F32)
            yg = [sb(f"yg1_{i}", [128, D], F32) for i in range(2)]
            x2 = [sb(f"x2_{i}", [128, D], F32) for i in range(2)]
            ot = [sb(f"ot{i}", [128, D], F32) for i in range(2)]
            dma("pool", wo1[:].rearrange("p c n -> p (c n)"), wout1_d, (), ["wo1"])
            dma("sp", fn_bc[:], fnorm_d.partition_broadcast(128), (), ["fn_bc"])
            for t in range(16):
                s = t % 2
                for half in range(2):
                    for c in range(8):
                        mm(pp[2 * s + half][:, 0:512], mixT[:, c, t * 128:(t + 1) * 128], wo1[:, c, half * 512:(half + 1) * 512],
                           c == 0, c == 7, ["wo1"], [pk(2 * s + half, 0)])
                    tt("dve", yg[s][:, half * 512:(half + 1) * 512], pp[2 * s + half][:, 0:512], gate_bc[:, half * 512:(half + 1) * 512],
                       ALU.mult, [pk(2 * s + half, 0)], [f"yg{s}_{half}"])
                tt("dve", x2[s][:], yg[s][:], x1[:, t, :], ALU.add, [f"yg{s}_0", f"yg{s}_1"], [f"x2_{s}"])
                ss = stat[:, 16 + s:17 + s]
                act(junk[:], x2[s][:], AF.Square, [f"x2_{s}"], ["junk", f"fss{s}"], accum=ss)
                rstd(ss, ss, float(D), [f"fss{s}"], [f"fss{s}"])
                stt("dve", ot[s][:], x2[s][:], ss, fn_bc[:], ALU.mult, ALU.mult, [f"x2_{s}", f"fss{s}", "fn_bc"], [f"ot{s}"])
                dma("sp", y_d[t * 128:(t + 1) * 128, :], ot[s][:], [f"ot{s}"], [f"y{t}"])
                okeys.append(f"y{t}")
        except _Stop:
            pass
        print("SBUF arena peak words:", A["peak"], "of", ARENA_WORDS)
        S.emit(nc, final_wait_keys=okeys)
    nc._sched = S
    return nc


def _rope_cs(pos, dim):
    inv = (np.float32(10000.0) ** (-(np.arange(0, dim, 2, dtype=np.float32)) / np.float32(dim))).astype(np.float32)
    ang = pos.astype(np.float32)[:, None] * inv[None, :]
    return np.cos(ang).astype(np.float32), np.sin(ang).astype(np.float32)


def _tables(tok):
    row = tok // 64
    col = tok % 64
    cr, sr = _rope_cs(row, 32)
    cc, sc_ = _rope_cs(col, 32)
    ct, st_ = _rope_cs(tok, 32)
    cosA = np.concatenate([cr, cr, cc, cc], axis=1)
    sinA = np.concatenate([-sr, sr, -sc_, sc_], axis=1)
    cosB = np.concatenate([ct, ct], axis=1)
    sinB = np.concatenate([-st_, st_], axis=1)

    tab = np.concatenate([cosA, sinA, cosB, sinB], axis=1).astype(np.float32)
    return np.ascontiguousarray(tab.reshape(NT, 128, 192))


def _dtab():
    s = np.arange(128)[:, None]
    q = np.arange(384)[None, :] - 128
    d = np.abs(q - s).astype(np.float32)
    return np.where(d <= 128, d, np.float32(BIGD)).astype(np.float32)


def _prep(inputs):
    f = lambda a: np.ascontiguousarray(np.asarray(a, dtype=np.float32))
    x = f(inputs["x"])
    c = f(inputs["c"])
    w_in = f(inputs["even_w_in"])[0]
    qa, ka, va, ga = w_in[:, 0:512], w_in[:, 512:640], w_in[:, 640:768], w_in[:, 768:1280]
    cq, ckv, kr, gb = w_in[:, 1280:1536], w_in[:, 1536:1664], w_in[:, 1664:1696], w_in[:, 1696:2208]
    wk0 = np.concatenate([ka, va, ckv, kr, cq], axis=1)
    permA = [np.concatenate([np.arange((0 * 4 + g) * 64, (0 * 4 + g) * 64 + 64), np.arange((1 * 4 + g) * 64, (1 * 4 + g) * 64 + 64)])
             for g in range(4)]
    wA = np.stack([np.concatenate([qa[:, permA[g]], ga[:, permA[g]]], axis=1) for g in range(4)])
    wgb = np.stack([gb[:, p * 128:(p + 1) * 128] for p in range(4)])
    wout = f(inputs["even_w_out"])[0]
    rows0 = np.concatenate([np.concatenate(permA), np.arange(512, 1024)])
    wout0 = wout[rows0, :]
    w1 = f(inputs["odd_w_in"])[0]
    qc, kc, vc, gc = w1[:, 0:1024], w1[:, 1024:1280], w1[:, 1280:1536], w1[:, 1536:2560]
    perm1 = []
    for P_ in range(8):
        j, g = P_ // 4, P_ % 4
        hA, hB = (2 * j) * 4 + g, (2 * j + 1) * 4 + g
        perm1.append(np.concatenate([np.arange(hA * 64, hA * 64 + 64), np.arange(hB * 64, hB * 64 + 64)]))
    wqg1 = np.stack([np.concatenate([qc[:, perm1[P_]], gc[:, perm1[P_]]], axis=1) for P_ in range(8)])
    wkv1 = np.concatenate([kc, vc], axis=1)
    wout1 = f(inputs["odd_w_out"])[0][np.concatenate(perm1), :]
    def pm(w):
        cN = w.shape[0] // 128
        return np.ascontiguousarray(w.reshape(cN, 128, w.shape[1]).transpose(1, 0, 2).reshape(128, cN * w.shape[1]))
    adaw = f(inputs["ada_w"]).reshape(2, 8, 128, 12, 256).transpose(0, 3, 2, 1, 4).reshape(2, 12, 128, 8 * 256)
    shared = {
        "ada_w": np.ascontiguousarray(adaw), "ada_b": f(inputs["ada_b"]), "norm_w": f(inputs["norm_w"]),
        "final_norm": f(inputs["final_norm"]), "ident": np.eye(128, dtype=np.float32),
        "wk0": pm(wk0), "wA": np.stack([pm(wA[i]) for i in range(4)]), "wgb": np.stack([pm(wgb[i]) for i in range(4)]),
        "wuq": pm(f(inputs["b_w_uq"])[0]), "wuk": np.ascontiguousarray(f(inputs["b_w_uk"])[0].reshape(128, 512)),
        "wuv": np.ascontiguousarray(f(inputs["b_w_uv"])[0].reshape(128, 512)),
        "wout0": pm(wout0),
        "aqn": f(inputs["a_q_norm"])[0], "akn": f(inputs["a_k_norm"])[0],
        "bqn": f(inputs["b_q_lora_norm"])[0], "bkvn": f(inputs["b_kv_lora_norm"])[0],
        "wkv1": pm(wkv1), "wqg1": np.stack([pm(wqg1[i]) for i in range(8)]), "wout1": pm(wout1),
        "sink": f(inputs["c_sink"])[0], "dtab": _dtab(),
    }
    tabs = [_tables(np.arange(SEQ)), _tables(SEQ - 1 - np.arange(SEQ))]
    in_maps = []
    for core in range(8):
        b, hf = core // 2, core % 2
        xl = x[b] if hf == 0 else x[b][::-1]
        m = dict(shared)
        m["x_loc"] = np.ascontiguousarray(xl)
        m["c_col"] = np.ascontiguousarray(c[b].reshape(8, 128).T)
        m["tab"] = tabs[hf]
        tq = tabs[hf][0:NQ]
        m["tabqa"] = np.ascontiguousarray(tq[:, :, 0:128].transpose(1, 0, 2).reshape(128, NQ * 128))
        m["tabqb"] = np.ascontiguousarray(tq[:, :, 128:192].transpose(1, 0, 2).reshape(128, NQ * 64))
        in_maps.append(m)
    return in_maps


_NC_CACHE = {}


def kernel(**inputs):
    debug = bool(inputs.pop("_debug", False))
    in_maps = _prep(inputs)
    if debug not in _NC_CACHE:
        _NC_CACHE[debug] = build_program(debug)
    nc = _NC_CACHE[debug]
    res = run_bass_kernel_spmd(nc, in_maps, core_ids=list(range(8)))
    out = np.empty((4, SEQ, D), dtype=np.float32)
    for core in range(8):
        b, hf = core // 2, core % 2
        y = np.asarray(res.results[core]["y"], dtype=np.float32)
        if hf == 0:
            out[b, 0:TO] = y
        else:
            out[b, TO:SEQ] = y[::-1]
    if debug:
        return out, [np.asarray(res.results[core]["dbg"]) for core in range(8)], [dict(h=np.asarray(res.results[core]["dbg2"]).astype(np.float32).reshape(128, 8, TQ), k=np.asarray(res.results[core]["dbg3"]).astype(np.float32), m=np.asarray(res.results[core]["dbg4"])) for core in range(8)]
    return out
```

```python
import contextlib
import numpy as np
import concourse.bass as bass
import concourse.mybir as mybir
from concourse.bass_utils import run_bass_kernel_spmd

F32 = mybir.dt.float32
BF16 = mybir.dt.bfloat16
AF = mybir.ActivationFunctionType
ALU = mybir.AluOpType
AX = mybir.AxisListType

D = 1024
SEQ = 4096
NT = 32
NQ = 17
TQ = NQ * 128
TO = 2048
EPS = 1e-6
SCALE_A = 64 ** -0.5
SCALE_B = 96 ** -0.5
SCALE_C = 64 ** -0.5
BIGD = 1.0e5


class _Ins:
    __slots__ = ("eng", "idx", "fn", "deps", "dma", "signal", "sem", "val", "tag", "rw")

    def __init__(self, eng, idx, fn, deps, dma):
        self.eng, self.idx, self.fn, self.deps, self.dma = eng, idx, fn, deps, dma
        self.signal = dma
        self.sem = None
        self.val = 0


class Sched:
    ENGS = ["pe", "act", "dve", "pool", "sp"]
    NDS = 8

    def __init__(self):
        self.ins = {e: [] for e in self.ENGS}
        self.last_w = {}
        self.readers = {}
        self.bar = set()
        self.bar_done = {e: True for e in self.ENGS}
        self.tag = ""
        self.names = {}

    def barrier(self):
        deps = set()
        for e in self.ENGS:
            lst = self.ins[e]
            last_c = None
            nd = 0
            for i in range(len(lst) - 1, -1, -1):
                t = lst[i]
                if t.dma:
                    if nd < self.NDS:
                        deps.add((e, i))
                        nd += 1
                elif last_c is None:
                    last_c = i
                    deps.add((e, i))
                if nd >= self.NDS and last_c is not None:
                    break
        for (e, i) in deps:
            self.ins[e][i].signal = True
        self.bar = deps
        self.bar_done = {e: False for e in self.ENGS}
        self.last_w = {}
        self.readers = {}

    def add(self, eng, fn, reads=(), writes=(), dma=False):
        lst = self.ins[eng]
        idx = len(lst)
        deps = set()
        for k in list(reads) + list(writes):
            w = self.last_w.get(k)
            if w is not None:
                deps.add(w)
        for k in writes:
            for rk, i in self.readers.get(k, {}).items():
                e = rk if isinstance(rk, str) else rk[1]
                deps.add((e, i))
        final = set()
        for (e, i) in deps:
            t = self.ins[e][i]
            if e == eng and eng == "pe" and not t.dma and not dma:
                continue
            t.signal = True
            final.add((e, i))
        if not self.bar_done[eng]:
            final |= {d for d in self.bar if not (d[0] == eng and eng == "pe" and not self.ins[d[0]][d[1]].dma)}
            self.bar_done[eng] = True
        ins = _Ins(eng, idx, fn, final, dma)
        ins.tag = self.tag
        ins.rw = (tuple(reads), tuple(writes))
        lst.append(ins)
        for k in writes:
            self.last_w[k] = (eng, idx)
            self.readers[k] = {}
        for k in reads:
            d = self.readers.setdefault(k, {})
            if dma:
                d[("dma", eng, idx)] = idx
            else:
                d[eng] = max(d.get(eng, -1), idx)
        return ins

    def emit(self, nc, final_wait_keys=()):
        with contextlib.ExitStack() as st:
            csem = {e: st.enter_context(nc.semaphore("c_" + e)) for e in self.ENGS if e != "sp"}
            dsem = {e: [st.enter_context(nc.semaphore(f"d_{e}{j}")) for j in range(self.NDS)]
                    for e in ("sp", "pool", "act")}
            for e in self.ENGS:
                cc = 0
                dc = 0
                for t in self.ins[e]:
                    if t.dma:
                        t.sem = dsem[e][dc % self.NDS]
                        t.val = 16 * (dc // self.NDS + 1)
                        dc += 1
                    elif t.signal:
                        cc += 1
                        t.sem = csem[e]
                        t.val = cc
            final_deps = set()
            for k in final_wait_keys:
                w = self.last_w.get(k)
                if w is not None:
                    final_deps.add(w)
            block = st.enter_context(nc.Block())

            def run(engh, e):
                known = {}

                def waits_raw(sem, v):
                    key = id(sem)
                    if known.get(key, 0) >= v:
                        return
                    engh.wait_ge(sem, v)
                    known[key] = v

                def waits(deps):
                    need = {}
                    for (de, di) in deps:
                        d = self.ins[de][di]
                        key = id(d.sem)
                        if known.get(key, 0) >= d.val:
                            continue
                        if key not in need or need[key][1] < d.val:
                            need[key] = (d.sem, d.val)
                    for key, (s, v) in need.items():
                        engh.wait_ge(s, v)
                        known[key] = v

                for t in self.ins[e]:
                    waits(t.deps)
                    if t.dma and t.val > 16:
                        waits_raw(t.sem, t.val - 16)
                    r = t.fn(engh)
                    try:
                        self.names[r.ins.name] = (t.eng, t.idx, t.tag, t.rw)
                    except Exception:
                        pass
                    if t.signal:
                        r.then_inc(t.sem, 16 if t.dma else 1)
                if e == "sp":
                    waits(final_deps)

            @block.tensor
            def _(h):
                run(h, "pe")

            @block.scalar
            def _(h):
                run(h, "act")

            @block.vector
            def _(h):
                run(h, "dve")

            @block.gpsimd
            def _(h):
                run(h, "pool")

            @block.sync
            def _(h):
                run(h, "sp")


class _Stop(Exception):
    pass


def build_program(debug=False, stage=99):
    nc = bass.Bass("TRN2", target_bir_lowering=False)
    S = Sched()

    def din(name, shape):
        return nc.dram_tensor(name, list(shape), F32, kind="ExternalInput").ap()

    x_d = din("x_loc", [SEQ, D])
    c_d = din("c_col", [128, 8])
    adaw_d = din("ada_w", [2, 12, 128, 8 * 256])
    adab_d = din("ada_b", [2, 3 * D])
    normw_d = din("norm_w", [2, D])
    fnorm_d = din("final_norm", [D])
    ident_d = din("ident", [128, 128])
    wk0_d = din("wk0", [128, 8 * 672])
    wA_d = din("wA", [4, 128, 8 * 256])
    wgb_d = din("wgb", [4, 128, 8 * 128])
    wuq_d = din("wuq", [128, 2 * 768])
    wuk_d = din("wuk", [128, 512])
    wuv_d = din("wuv", [128, 512])
    wout0_d = din("wout0", [128, 8 * D])
    aqn_d = din("aqn", [64])
    akn_d = din("akn", [64])
    bqn_d = din("bqn", [256])
    bkvn_d = din("bkvn", [128])
    tab_d = din("tab", [NT, 128, 192])
    tabqa_d = din("tabqa", [128, NQ * 128])
    tabqb_d = din("tabqb", [128, NQ * 64])
    wkv1_d = din("wkv1", [128, 8 * 512])
    wqg1_d = din("wqg1", [8, 128, 8 * 256])
    wout1_d = din("wout1", [128, 8 * D])
    sink_d = din("sink", [16])
    dtab_d = din("dtab", [128, 384])
    y_d = nc.dram_tensor("y", [TO, D], F32, kind="ExternalOutput").ap()
    dbg_d = None
    if debug:
        dbg_d = nc.dram_tensor("dbg", [TQ, D], F32, kind="ExternalOutput").ap()
        dbg2_d = nc.dram_tensor("dbg2", [128, 8 * TQ], BF16, kind="ExternalOutput").ap()
        dbg3_d = nc.dram_tensor("dbg3", [128, 3 * SEQ + NT * 192 + 2 * TQ], BF16, kind="ExternalOutput").ap()
        dbg4_d = nc.dram_tensor("dbg4", [128, 3 * D], F32, kind="ExternalOutput").ap()

    def mm(out, lhsT, rhs, start, stop, r, w, skip=False):
        if skip:
            S.add("pe", lambda e: e.matmul(out, lhsT=lhsT, rhs=rhs, start=start, stop=stop, skip_group_check=True), r, w)
        else:
            S.add("pe", lambda e: e.matmul(out, lhsT=lhsT, rhs=rhs, start=start, stop=stop), r, w)

    def tp(out, in_, r, w):
        S.add("pe", lambda e: e.transpose(out, in_, ident[:]), list(r) + ["ident"], w)

    def act(out, in_, func, r, w, bias=None, scale=None, accum=None):
        kw = {}
        if bias is not None:
            kw["bias"] = bias
        if scale is not None:
            kw["scale"] = scale
        if accum is not None:
            kw["accum_out"] = accum
        S.add("act", lambda e: e.activation(out=out, in_=in_, func=func, **kw), r, w)

    def tt(eng, out, in0, in1, op, r, w):
        S.add(eng, lambda e: e.tensor_tensor(out=out, in0=in0, in1=in1, op=op), r, w)

    def ts(eng, out, in0, s1, s2, op0, op1, r, w):
        if s2 is None:
            S.add(eng, lambda e: e.tensor_scalar(out=out, in0=in0, scalar1=s1, scalar2=None, op0=op0), r, w)
        else:
            S.add(eng, lambda e: e.tensor_scalar(out=out, in0=in0, scalar1=s1, scalar2=s2, op0=op0, op1=op1), r, w)

    def stt(eng, out, in0, scalar, in1, op0, op1, r, w):
        S.add(eng, lambda e: e.scalar_tensor_tensor(out=out, in0=in0, scalar=scalar, in1=in1, op0=op0, op1=op1), r, w)

    def cp(eng, out, in_, r, w):
        if eng == "act":
            S.add("act", lambda e: e.activation(out=out, in_=in_, func=AF.Copy), r, w)
        else:
            S.add(eng, lambda e: e.tensor_copy(out=out, in_=in_), r, w)

    def red(out, in_, r, w):
        S.add("dve", lambda e: e.tensor_reduce(out=out, in_=in_, axis=AX.X, op=ALU.add), r, w)

    def rcp(out, in_, r, w):
        S.add("dve", lambda e: e.reciprocal(out=out, in_=in_), r, w)

    def mset(eng, ap, val, w):
        S.add(eng, lambda e: e.memset(ap, val), (), w)

    def dma(q, out, in_, r, w):
        S.add(q, lambda e: e.dma_start(out=out, in_=in_), r, w, dma=True)

    def rstd(dst, ss, n, r, w):
        act(dst, ss, AF.Ln, r, w, bias=EPS, scale=1.0 / n)
        act(dst, dst, AF.Exp, w, w, scale=-0.5)

    with contextlib.ExitStack() as outer:
        ARENA_WORDS = 52992
        arena_t = outer.enter_context(nc.sbuf_tensor("arena", [128, ARENA_WORDS], F32))
        A = {"top": 0, "peak": 0}

        def sb(name, shape, dt, st=None):
            n = 1
            for d_ in shape[1:]:
                n *= int(d_)
            nbytes = n * (2 if dt == BF16 else 4)
            w = (nbytes + 31) // 32 * 8
            off = A["top"]
            A["top"] += w
            assert A["top"] <= ARENA_WORDS, ("SBUF arena overflow", name, A["top"])
            A["peak"] = max(A["peak"], A["top"])
            v = arena_t[:, off:off + w]
            if dt == BF16:
                v = v.bitcast(BF16)
            v = v[:, 0:n]
            if len(shape) == 3:
                v = v.rearrange("p (a b) -> p a b", a=int(shape[1]))
            elif len(shape) == 4:
                v = v.rearrange("p (a b c) -> p a b c", a=int(shape[1]), b=int(shape[2]))
            return v

        def mark():
            return A["top"]

        def release(m):
            A["top"] = m

        pp = [outer.enter_context(nc.psum_tensor(f"pp{i}", [128, 1024], F32)) for i in range(4)]

        def pk(i, h):
            return f"pp{i}{'ab'[h]}"

        okeys = []
        try:
            ident = sb("ident", [128, 128], BF16)
            c_sb = sb("c_sb", [128, 8], F32)
            gate_bc = sb("gate_bc", [128, D], F32)
            junk = sb("junk", [128, D], BF16)
            stat = sb("stat", [128, 64], F32)
            mixT = sb("mixT", [128, 8, TQ], BF16)

            dma("pool", ident[:], ident_d, (), ["ident"])
            dma("sp", c_sb[:], c_d, (), ["c_sb"])
            act(c_sb[:], c_sb[:], AF.Silu, ["c_sb"], ["c_sb"])

            def emit_mods(layer, g_bc, sh_bc, SW, nslots=2, queues=("sp",)):
                m = mark()
                ones_f = sb("ones_f", [128, 128], F32)
                crep = sb("crep", [128, 8, 128], F32)
                aw = [sb(f"aw{i}", [128, 8, SW], F32) for i in range(nslots)]
                bb = sb("adab", [128, SW], F32)
                nw = sb("nw", [128, D], F32)
                mset("dve", ones_f[:], 1.0, ["ones_f"])
                for k in range(8):
                    ts("dve", crep[:, k, :], ones_f[:], c_sb[:, k:k + 1], None, ALU.mult, None,
                       ["ones_f", "c_sb"], [f"crep{k}"])
                dma("sp", nw[:], normw_d[layer].partition_broadcast(128), (), ["nw"])
                for s in range(3 * D // SW):
                    sl = s % nslots
                    kind = (s * SW) // D
                    c0 = (s * SW) % D
                    dma(queues[s % len(queues)], aw[sl][:].rearrange("p k n -> p (k n)"), adaw_d[layer][s], (), [f"aw{sl}"])
                    dma("sp", bb[:], adab_d[layer][s * SW:(s + 1) * SW].partition_broadcast(128), (), ["adab"])
                    ps = pp[s % 2][:, 0:SW]
                    for k in range(8):
                        mm(ps, crep[:, k, :], aw[sl][:, k, :], k == 0, k == 7, [f"crep{k}", f"aw{sl}"], [pk(s % 2, 0)])
                    col = slice(c0, c0 + SW)
                    if kind == 0:
                        tt("dve", sh_bc[:, col], ps, bb[:], ALU.add, [pk(s % 2, 0), "adab"], [f"modw{s}"])
                    elif kind == 1:
                        tt("dve", g_bc[:, col], ps, bb[:], ALU.add, [pk(s % 2, 0), "adab"], [f"modw{s}"])
                        stt("dve", g_bc[:, col], g_bc[:, col], 1.0, nw[:, col], ALU.add, ALU.mult,
                            [f"modw{s}", "nw"], [f"modw{s}"])
                    else:
                        tt("dve", gate_bc[:, col], ps, bb[:], ALU.add, [pk(s % 2, 0), "adab"], [f"modw{s}"])
                S.barrier()
                release(m)

            def norm_rope64(src, nt, nh, gain_bc, cos3, sin3, dst, scr, r, w, kp, tk=(), e2="pool"):
                n = nt * nh
                W = n * 64
                sq, qn, t1, t2 = scr["a"], scr["b"], scr["c"], scr["d"]
                ss = scr["st"]
                tt(e2, sq[:, :W], src, src, ALU.mult, r, [kp + "sq"])
                red(ss[:, :n], sq[:, :W].rearrange("p (n d) -> p n d", d=64), [kp + "sq"], [kp + "ss"])
                rstd(ss[:, :n], ss[:, :n], 64.0, [kp + "ss"], [kp + "ss"])
                tt("dve", qn[:, :W].rearrange("p (n d) -> p n d", d=64), src.rearrange("p (n d) -> p n d", d=64),
                   ss[:, :n].unsqueeze(2).broadcast_to([128, n, 64]), ALU.mult, list(r) + [kp + "ss"], [kp + "qn"])
                tt(e2, qn[:, :W].rearrange("p (n d) -> p n d", d=64), qn[:, :W].rearrange("p (n d) -> p n d", d=64),
                   gain_bc[:].unsqueeze(1).broadcast_to([128, n, 64]), ALU.mult, [kp + "qn", "gains"], [kp + "qn"])
                q4 = qn[:, :W].rearrange("p (t h d) -> p t h d", t=nt, h=nh)
                tt(e2, t1[:, :W].rearrange("p (t h d) -> p t h d", t=nt, h=nh), q4,
                   cos3.unsqueeze(2).broadcast_to([128, nt, nh, 64]), ALU.mult, [kp + "qn", "tabs"] + list(tk), [kp + "t1"])
                q6 = qn[:, :W].rearrange("p (t h x r i) -> p t h x r i", t=nt, h=nh, x=2, r=2)
                o6 = t2[:, :W].rearrange("p (t h x r i) -> p t h x r i", t=nt, h=nh, x=2, r=2)
                s5 = sin3.rearrange("p t (x r i) -> p t x r i", x=2, r=2)
                k = 0
                for h in range(nh):
                    for rr in range(2):
                        eng = e2 if k % 2 == 0 else "dve"
                        k += 1
                        tt(eng, o6[:, :, h, :, rr, :], q6[:, :, h, :, 1 - rr, :], s5[:, :, :, rr, :], ALU.mult,
                           [kp + "qn", "tabs"] + list(tk), [kp + f"t2_{h}{rr}"])
                tt("dve", dst, t1[:, :W], t2[:, :W], ALU.add,
                   [kp + "t1"] + [kp + f"t2_{h}{rr}" for h in range(nh) for rr in range(2)], w)

            def rope32(src3, a, cb, sbn, dst3, bufs, r, w, kp, e2="pool"):
                u1 = bufs[0][:, :a * 32].rearrange("p (a d) -> p a d", d=32)
                u2 = bufs[1][:, :a * 32].rearrange("p (a d) -> p a d", d=32)
                tt(e2, u1, src3, cb, ALU.mult, list(r) + ["tabs"], [kp + "u1"])
                tt(e2, u2[:, :, 0:16], src3[:, :, 16:32], sbn[:, :, 0:16], ALU.mult, list(r) + ["tabs"], [kp + "u2a"])
                tt("dve", u2[:, :, 16:32], src3[:, :, 0:16], sbn[:, :, 16:32], ALU.mult, list(r) + ["tabs"], [kp + "u2b"])
                tt("dve", dst3, u1, u2, ALU.add, [kp + "u1", kp + "u2a", kp + "u2b"], w)

            MODK = []

            def hT_pre(src_ap, src_keys, s, xm_ap, xm_key, xs_ap, g_bc, sh_bc):
                ss = stat[:, s:s + 1]
                act(junk[:], src_ap, AF.Square, src_keys, ["junk", f"ss{s}"], accum=ss)
                rstd(ss, ss, float(D), [f"ss{s}"], [f"ss{s}"])
                stt("dve", xm_ap, src_ap, ss, g_bc[:], ALU.mult, ALU.mult, list(src_keys) + [f"ss{s}"], [xm_key])
                tt("dve", xs_ap, xm_ap, sh_bc[:], ALU.add, [xm_key], [f"xs{s}"])

            def hT_tr(s, xs_ap, dst, dkeys):
                pT = pp[2 * s][:, 0:512].bitcast(BF16)
                for c in range(8):
                    tp(pT[:, c * 128:(c + 1) * 128], xs_ap[:, c * 128:(c + 1) * 128], [f"xs{s}"], [pk(2 * s, 0)])
                cp("act", dst, pT.rearrange("p (c n) -> p c n", c=8), [pk(2 * s, 0)], dkeys)

            mL0 = mark()
            g_bc = sb("g_bc", [128, D], F32)
            sh_bc = sb("sh_bc", [128, D], F32)
            emit_mods(0, g_bc, sh_bc, 256, nslots=4, queues=("sp", "act"))
            if stage <= 0:
                raise _Stop()
            hTq = sb("hTq", [128, 8, TQ], BF16)
            tabB = sb("tabB", [128, NQ, 64], F32)
            aqn = sb("aqn", [128, 64], F32)
            akn = sb("akn", [128, 64], F32)
            bqn = sb("bqn", [128, 256], F32)
            bkvn = sb("bkvn", [128, 128], F32)
            ckvnT = sb("ckvnT", [128, SEQ], BF16)
            kropeT = sb("kropeT", [128, SEQ], BF16)
            cqnT = sb("cqnT", [128, 2, TQ], BF16)
            mLA = mark()
            KTA = sb("KTA", [128, SEQ], BF16)
            VA = sb("VA", [128, NT, 192], BF16)
            tabA = sb("tabA", [128, NQ, 128], F32)

            dma("sp", tabA[:].rearrange("p t d -> p (t d)"), tabqa_d, (), ["tabs"])
            dma("sp", tabB[:].rearrange("p t d -> p (t d)"), tabqb_d, (), ["tabs"])
            dma("sp", aqn[:], aqn_d.partition_broadcast(128), (), ["gains"])
            dma("sp", akn[:], akn_d.partition_broadcast(128), (), ["gains"])
            dma("sp", bqn[:], bqn_d.partition_broadcast(128), (), ["gains"])
            dma("sp", bkvn[:], bkvn_d.partition_broadcast(128), (), ["gains"])
            mset("pool", VA[:], 1.0, ["VA_init"])

            mP1 = mark()
            wk0 = sb("wk0", [128, 8, 672], BF16)
            xt = [sb(f"xt{i}", [128, D], F32) for i in range(2)]
            xs = [sb(f"xs{i}", [128, D], BF16) for i in range(2)]
            ks = [sb(f"ks{i}", [128, 672], F32) for i in range(3)]
            kbf = [sb(f"kbf{i}", [128, 544], BF16) for i in range(3)]
            hTt = sb("hTt", [128, 3, 8, 128], BF16)
            tabK = [sb(f"tabK{i}", [128, 192], F32) for i in range(3)]
            scr1 = [dict(a=sb("s1a", [128, 128], F32), b=sb("s1b", [128, 128], F32), c=sb("s1c", [128, 128], F32),
                         d=sb("s1d", [128, 128], F32), st=sb("s1s", [128, 8], F32), e=sb("s1e", [128, 32], F32),
                         f=sb("s1f", [128, 32], F32)) for i in range(3)]
            dma("pool", wk0[:].rearrange("p c n -> p (c n)"), wk0_d, (), ["wk0"])

            def p1_ctx(t):
                s = t % 2
                u = t % 3
                isq = t < NQ
                if isq:
                    hdst = hTq[:, :, t * 128:(t + 1) * 128]
                    hk = [f"hTq{t}"]
                    cA, sA = tabA[:, t:t + 1, 0:64], tabA[:, t:t + 1, 64:128]
                    cB, sB = tabB[:, t:t + 1, 0:32], tabB[:, t:t + 1, 32:64]
                    tk = []
                else:
                    hdst = hTt[:, u, :, :]
                    hk = [f"hTt{u}"]
                    cA, sA = tabK[u][:, 0:64].unsqueeze(1), tabK[u][:, 64:128].unsqueeze(1)
                    cB, sB = tabK[u][:, 128:160].unsqueeze(1), tabK[u][:, 160:192].unsqueeze(1)
                    tk = [f"tabK{u}"]
                return s, u, isq, hdst, hk, cA, sA, cB, sB, tk

            def p1_A1(t):
                S.tag = f'p1A1 t{t}'
                s, u, isq, hdst, hk, cA, sA, cB, sB, tk = p1_ctx(t)
                dma("sp", xt[s][:], x_d[t * 128:(t + 1) * 128, :], (), [f"xt{s}"])
                hT_pre(xt[s][:], [f"xt{s}"], s, xt[s][:], f"xt{s}", xs[s][:], g_bc, sh_bc)

            def p1_A2(t):
                S.tag = f'p1A2 t{t}'
                s, u, isq, hdst, hk, cA, sA, cB, sB, tk = p1_ctx(t)
                hT_tr(s, xs[s][:], hdst, hk)

            def p1_A3(t):
                S.tag = f'p1A3 t{t}'
                s, u, isq, hdst, hk, cA, sA, cB, sB, tk = p1_ctx(t)
                if not isq:
                    dma("sp", tabK[u][:], tab_d[t], (), [f"tabK{u}"])
                pj = pp[2 * s + 1]
                for c in range(8):
                    mm(pj[:, 0:416], hdst[:, c, :], wk0[:, c, 0:416], c == 0, c == 7, hk + ["wk0"], [pk(2 * s + 1, 0)])
                cp("act", ks[u][:, 0:416], pj[:, 0:416], [pk(2 * s + 1, 0)], [f"ksa{u}"])
                if isq:
                    for c in range(8):
                        mm(pj[:, 512:768], hdst[:, c, :], wk0[:, c, 416:672], c == 0, c == 7, hk + ["wk0"], [pk(2 * s + 1, 1)])
                    cp("act", ks[u][:, 416:672], pj[:, 512:768], [pk(2 * s + 1, 1)], [f"ksb{u}"])

            def p1_B(t):
                S.tag = f'p1B t{t}'
                s, u, isq, hdst, hk, cA, sA, cB, sB, tk = p1_ctx(t)
                sc = scr1[u]
                norm_rope64(ks[u][:, 0:128], 1, 2, akn, cA, sA, kbf[u][:, 0:128], sc,
                            [f"ksa{u}"] + tk, [f"kbfa{u}"], f"ka{u}", tk=tk)
                cp("pool", VA[:, t, :].rearrange("p (a b) -> p a b", b=64)[:, 0:3:2, :],
                   ks[u][:, 128:256].rearrange("p (a b) -> p a b", b=64), [f"ksa{u}", "VA_init"], [f"VA{t}"])
                ssc = stat[:, 4 + u:5 + u]
                act(junk[:, 0:128], ks[u][:, 256:384], AF.Square, [f"ksa{u}"], ["junk", f"ssc{u}"], accum=ssc)
                rstd(ssc, ssc, 128.0, [f"ssc{u}"], [f"ssc{u}"])
                stt("dve", kbf[u][:, 128:256], ks[u][:, 256:384], ssc, bkvn[:], ALU.mult, ALU.mult,
                    [f"ksa{u}", f"ssc{u}", "gains"], [f"kbfb{u}"])
                rope32(ks[u][:, 384:416].unsqueeze(1), 1, cB, sB, kbf[u][:, 256:288].unsqueeze(1), (sc["e"], sc["f"]),
                       [f"ksa{u}"] + tk, [f"kbfc{u}"], f"kr{u}")
                pS = pp[2 * s][:, 512:1024].bitcast(BF16)
                tp(pS[:, 0:128], kbf[u][:, 0:128], [f"kbfa{u}"], [pk(2 * s, 1)])
                tp(pS[:, 192:320], kbf[u][:, 128:256], [f"kbfb{u}"], [pk(2 * s, 1)])
                if isq:
                    ssq = stat[:, 8 + u:9 + u]
                    act(junk[:, 0:256], ks[u][:, 416:672], AF.Square, [f"ksb{u}"], ["junk", f"ssq{u}"], accum=ssq)
                    rstd(ssq, ssq, 256.0, [f"ssq{u}"], [f"ssq{u}"])
                    stt("dve", kbf[u][:, 288:544], ks[u][:, 416:672], ssq, bqn[:], ALU.mult, ALU.mult,
                        [f"ksb{u}", f"ssq{u}", "gains"], [f"kbfd{u}"])
                    tp(pS[:, 384:512], kbf[u][:, 288:416], [f"kbfd{u}"], [pk(2 * s, 1)])
                    tp(pS[:, 576:704], kbf[u][:, 416:544], [f"kbfd{u}"], [pk(2 * s, 1)])
                tp(pS[0:32, 768:896], kbf[u][:, 256:288], [f"kbfc{u}"], [pk(2 * s, 1)])
                cp("dve", KTA[:, t * 128:(t + 1) * 128], pS[:, 0:128], [pk(2 * s, 1)], [f"KTA{t}"])
                cp("dve", ckvnT[:, t * 128:(t + 1) * 128], pS[:, 192:320], [pk(2 * s, 1)], [f"ckvnT{t}"])
                cp("dve", kropeT[64:96, t * 128:(t + 1) * 128], pS[0:32, 768:896], [pk(2 * s, 1)], [f"kropeT{t}"])
                if isq:
                    cp("dve", cqnT[:, :, t * 128:(t + 1) * 128], pS[:, 384:768].rearrange("p (c n) -> p c n", c=2)[:, :, 0:128],
                       [pk(2 * s, 1)], [f"cqnT{t}"])

            for step in range(NT + 4):
                if step < NT:
                    p1_A1(step)
                if 0 <= step - 1 < NT:
                    p1_A2(step - 1)
                if 0 <= step - 2 < NT:
                    p1_A3(step - 2)
                if 0 <= step - 4 < NT:
                    p1_B(step - 4)
            S.barrier()
            if debug:
                dma("sp", dbg2_d, hTq[:].rearrange("p c n -> p (c n)"), (), ["dbg2"])
                dma("sp", dbg3_d[:, 0:SEQ], KTA[:], (), ["dbg3a"])
                dma("sp", dbg3_d[:, SEQ:2 * SEQ], ckvnT[:], (), ["dbg3b"])
                dma("sp", dbg3_d[64:96, 2 * SEQ:3 * SEQ], kropeT[64:96, :], (), ["dbg3c"])
                dma("sp", dbg3_d[:, 3 * SEQ:3 * SEQ + NT * 192], VA[:].rearrange("p t n -> p (t n)"), (), ["dbg3d"])
                dma("sp", dbg3_d[:, 3 * SEQ + NT * 192:], cqnT[:].rearrange("p c n -> p (c n)"), (), ["dbg3e"])
                dma("sp", dbg4_d[:, 0:D], g_bc[:], (), ["dbg4a"])
                dma("sp", dbg4_d[:, D:2 * D], sh_bc[:], (), ["dbg4b"])
                dma("sp", dbg4_d[:, 2 * D:3 * D], gate_bc[:], (), ["dbg4c"])
                S.barrier()
            if stage <= 1:
                raise _Stop()
            release(mP1)

            def attention_l0(qk_l, qk_r, v_l, scale, GT, chunk, kp, PT, rd, fint):
                its = [(qg, kb) for qg in range(5) for kb in range(NT)]
                n = len(its)

                QSZ = [448, 448, 448, 448, 384]
                QOFF = [0, 448, 896, 1344, 1792]

                def geo(i):
                    qg, kb = its[i]
                    return qg, kb, QOFF[qg], QSZ[qg], qg % 2, i % 2, i % 3

                def st_qk(i):
                    qg, kb, q0, nq, osl, ssl, psl = geo(i)
                    psS = pp[ssl]
                    mm(psS[:, 0:nq], qk_l(0, kb), qk_r(0, q0, nq), True, True, [kp + "K", kp + "Kr", kp + "Q"], [pk(ssl, 0)])
                    mm(psS[:, 512:512 + nq], qk_l(1, kb), qk_r(1, q0, nq), True, True, [kp + "K", kp + "Kr", kp + "Q"], [pk(ssl, 1)])

                def st_exp(i):
                    qg, kb, q0, nq, osl, ssl, psl = geo(i)
                    act(PT[psl][:].rearrange("p (h n) -> p h n", h=2)[:, :, 0:nq],
                        pp[ssl][:].rearrange("p (h n) -> p h n", h=2)[:, :, 0:nq], AF.Exp,
                        [pk(ssl, 0), pk(ssl, 1)], [f"PT{psl}"], scale=scale)

                def st_pv(i):
                    qg, kb, q0, nq, osl, ssl, psl = geo(i)
                    psO = pp[2 + osl]
                    mm(psO[:, 0:nq], v_l(0, kb), PT[psl][:, 0:nq], kb == 0, kb == NT - 1, [kp + "V", f"PT{psl}"], [pk(2 + osl, 0)])
                    mm(psO[:, 512:512 + nq], v_l(1, kb), PT[psl][:, 512:512 + nq], kb == 0, kb == NT - 1,
                       [kp + "V", f"PT{psl}"], [pk(2 + osl, 1)])
                    if kb != NT - 1:
                        return
                    r_ = rd[osl]
                    tmp = fint[osl]
                    rcp(r_[64:128, 0:nq], psO[64:128, 0:nq], [pk(2 + osl, 0)], [f"rd{osl}"])
                    rcp(r_[0:64, 0:nq], psO[0:64, 512:512 + nq], [pk(2 + osl, 1)], [f"rd{osl}"])
                    tt("dve", tmp[0:64, 0:nq], psO[0:64, 0:nq], r_[64:128, 0:nq], ALU.mult, [pk(2 + osl, 0), f"rd{osl}"], [f"fin{osl}"])
                    tt("dve", tmp[64:128, 0:nq], psO[64:128, 512:512 + nq], r_[0:64, 0:nq], ALU.mult,
                       [pk(2 + osl, 1), f"rd{osl}"], [f"fin{osl}"])
                    tt("pool", mixT[:, chunk, q0:q0 + nq], tmp[:, 0:nq], GT[:, q0:q0 + nq], ALU.mult,
                       [f"fin{osl}", kp + "G"], [f"mixT{chunk}_{qg}"])

                import os
                l1_, l2_ = [int(v) for v in os.environ.get("LAG_" + kp, "1,2").split(",")]
                for step in range(n + l2_):
                    if step < n:
                        st_qk(step)
                    if 0 <= step - l1_ < n:
                        st_exp(step - l1_)
                    if 0 <= step - l2_ < n:
                        st_pv(step - l2_)

            def gates_fm(wslice, GT, r, kp):
                for qg in range(5):
                    q0 = qg * 512
                    nq = min(512, TQ - q0)
                    i = qg % 2
                    for c in range(8):
                        mm(pp[i][:, 0:nq], wslice(c), hTq[:, c, q0:q0 + nq], c == 0, c == 7, list(r), [pk(i, 0)])
                    act(GT[:, q0:q0 + nq], pp[i][:, 0:nq], AF.Silu, [pk(i, 0)], [kp + "G"])

            mP2 = mark()
            wA = [sb(f"wA{i}", [128, 8, 256], BF16) for i in range(2)]
            QTA = sb("QTA", [128, TQ], BF16)
            GTA = sb("GTA", [128, TQ], BF16)
            qs = [sb(f"qs{i}", [128, 512], F32) for i in range(3)]
            qbf = [sb(f"qbf{i}", [128, 512], BF16) for i in range(3)]
            PT = [sb(f"PT{i}", [128, 1024], BF16) for i in range(3)]
            rd = [sb(f"rd{i}", [128, 512], F32) for i in range(2)]
            fint = [sb(f"fint{i}", [128, 512], F32) for i in range(2)]
            scr2 = [dict(a=sb(f"s2a{k}", [128, 512], F32), b=sb(f"s2b{k}", [128, 512], F32), c=sb(f"s2c{k}", [128, 512], F32),
                         d=sb(f"s2d{k}", [128, 512], F32), st=sb(f"s2s{k}", [128, 16], F32)) for k in range(2)]
            dma("pool", wA[0][:].rearrange("p c n -> p (c n)"), wA_d[0], (), ["wA0"])
            for g in range(4):
                S.tag = f'p2 g{g}'
                w = wA[g % 2]
                wk = f"wA{g % 2}"
                if g + 1 < 4:
                    dma("pool", wA[(g + 1) % 2][:].rearrange("p c n -> p (c n)"), wA_d[g + 1], (), [f"wA{(g + 1) % 2}"])
                def qa_A(bi, w=w, wk=wk):
                    t0 = bi * 4
                    nt = min(4, NQ - t0)
                    i = (bi + 1) % 2
                    pj = pp[2 + i]
                    for tl in range(nt):
                        t = t0 + tl
                        for c in range(8):
                            mm(pj[:, tl * 128:(tl + 1) * 128], hTq[:, c, t * 128:(t + 1) * 128], w[:, c, 0:128],
                               c == 0, c == 7, [wk], [pk(2 + i, 0)])
                    cp("act", qs[bi % 3][:, 0:nt * 128], pj[:, 0:nt * 128], [pk(2 + i, 0)], [f"qs{bi % 3}"])

                def qa_G(qg, w=w, wk=wk):
                    q0 = qg * 512
                    nq = min(512, TQ - q0)
                    i = qg % 2
                    for c in range(8):
                        mm(pp[i][:, 0:nq], w[:, c, 128:256], hTq[:, c, q0:q0 + nq], c == 0, c == 7, [wk], [pk(i, 0)])
                    act(GTA[:, q0:q0 + nq], pp[i][:, 0:nq], AF.Silu, [pk(i, 0)], ["AG"])

                def qa_B(bi):
                    t0 = bi * 4
                    nt = min(4, NQ - t0)
                    i = (bi + 1) % 2
                    u = bi % 3
                    norm_rope64(qs[u][:, 0:nt * 128], nt, 2, aqn, tabA[:, t0:t0 + nt, 0:64], tabA[:, t0:t0 + nt, 64:128],
                                qbf[u][:, 0:nt * 128], scr2[bi % 2], [f"qs{u}"], [f"qbf{u}"], f"qa{bi % 2}", e2="dve")
                    pS = pp[2 + i][:, 512:1024].bitcast(BF16)
                    for tl in range(nt):
                        tp(pS[:, tl * 128:(tl + 1) * 128], qbf[u][:, tl * 128:(tl + 1) * 128], [f"qbf{u}"], [pk(2 + i, 1)])
                    cp("dve", QTA[:, t0 * 128:(t0 + nt) * 128], pS[:, 0:nt * 128], [pk(2 + i, 1)], ["AQ"])

                for step in range(7):
                    if step < 5:
                        qa_A(step)
                        qa_G(step)
                    if 0 <= step - 2 < 5:
                        qa_B(step - 2)
                attention_l0(lambda h, kb: KTA[64 * h:64 * h + 64, kb * 128:(kb + 1) * 128],
                             lambda h, q0, nq: QTA[64 * h:64 * h + 64, q0:q0 + nq],
                             lambda h, kb: VA[:, kb, 64 * h:64 * h + 128],
                             SCALE_A, GTA, g, "A", PT, rd, fint)
            S.barrier()
            if stage <= 2:
                raise _Stop()
            release(mLA)

            wuq = sb("wuq", [128, 2, 768], BF16)
            wuk = sb("wuk", [128, 512], BF16)
            wuv = sb("wuv", [128, 512], BF16)
            wgb = [sb(f"wgb{i}", [128, 8, 128], BF16) for i in range(2)]
            KTB = [sb(f"KTB{i}", [128, SEQ], BF16) for i in range(2)]
            QTB = [sb(f"QTB{i}", [128, TQ], BF16) for i in range(2)]
            VB = sb("VB", [128, NT, 192], BF16)
            GTB = sb("GTB", [128, TQ], BF16)
            qs3 = [sb(f"qs3_{i}", [128, 384], F32) for i in range(3)]
            qb3 = [sb(f"qb3_{i}", [128, 384], BF16) for i in range(3)]
            PT = [sb(f"PTb{i}", [128, 1024], BF16) for i in range(3)]
            rd = [sb(f"rdb{i}", [128, 512], F32) for i in range(2)]
            fint = [sb(f"fintb{i}", [128, 512], F32) for i in range(2)]
            s3 = [[sb(f"s3_{i}{k}", [128, 64], F32) for k in range(4)] for i in range(3)]
            dma("pool", wuq[:].rearrange("p c n -> p (c n)"), wuq_d, (), ["wuq"])
            dma("pool", wuk[:], wuk_d, (), ["wuk"])
            dma("pool", wuv[:], wuv_d, (), ["wuv"])
            dma("pool", wgb[0][:].rearrange("p c n -> p (c n)"), wgb_d[0], (), ["wgb0"])
            mset("pool", VB[:], 1.0, ["VB_init"])
            for hh_ in range(2):
                dma("sp", KTB[hh_][64:96, :], kropeT[64:96, :], (), ["BKr"])
            for p in range(4):
                S.tag = f'p3 p{p}'
                wg = wgb[p % 2]
                wgk = f"wgb{p % 2}"
                if p + 1 < 4:
                    dma("pool", wgb[(p + 1) % 2][:].rearrange("p c n -> p (c n)"), wgb_d[p + 1], (), [f"wgb{(p + 1) % 2}"])
                def kb_unit(u_, p=p):
                    hh, kg = u_ // 8, u_ % 8
                    h = 2 * p + hh
                    i = (kg + 1) % 2
                    mm(pp[2 + i][0:64, 0:512], wuk[:, h * 64:(h + 1) * 64], ckvnT[:, kg * 512:(kg + 1) * 512],
                       True, True, ["wuk"], [pk(2 + i, 0)])
                    cp("act" if kg % 2 == 0 else "dve", KTB[hh][0:64, kg * 512:(kg + 1) * 512], pp[2 + i][0:64, 0:512],
                       [pk(2 + i, 0)], ["BK"])

                def vb_unit(kg, p=p):
                    i = (kg + 1) % 2
                    for tl in range(4):
                        kb = kg * 4 + tl
                        mm(pp[2 + i][:, 512 + tl * 128:512 + (tl + 1) * 128], ckvnT[:, kb * 128:(kb + 1) * 128],
                           wuv[:, p * 128:(p + 1) * 128], True, True, ["wuv"], [pk(2 + i, 1)])
                    cp("dve" if kg % 2 == 0 else "act",
                       VB[:, kg * 4:(kg + 1) * 4, :].rearrange("p t (a b) -> p t a b", b=64)[:, :, 0:3:2, :],
                       pp[2 + i][:, 512:1024].rearrange("p (t a b) -> p t a b", t=4, a=2), [pk(2 + i, 1), "VB_init"], ["BV"])

                def qb_A(bi, p=p):
                    t0 = bi * 2
                    nt = min(2, NQ - t0)
                    i = bi % 2
                    pj = pp[i]
                    for tl in range(nt):
                        t = t0 + tl
                        for cc in range(2):
                            mm(pj[:, tl * 192:(tl + 1) * 192], cqnT[:, cc, t * 128:(t + 1) * 128],
                               wuq[:, cc, p * 192:(p + 1) * 192], cc == 0, cc == 1, ["wuq"], [pk(i, 0)])
                    cp("act", qs3[bi % 3][:, 0:nt * 192], pj[:, 0:nt * 192], [pk(i, 0)], [f"qs3{bi % 3}"])

                def qb_B(bi):
                    t0 = bi * 2
                    nt = min(2, NQ - t0)
                    i = bi % 2
                    u = bi % 3
                    W = nt * 192
                    v_s = qs3[u][:, 0:W].rearrange("p (a d) -> p a d", d=96)
                    v_d = qb3[u][:, 0:W].rearrange("p (a d) -> p a d", d=96)
                    cp("dve", v_d[:, :, 0:64], v_s[:, :, 0:64], [f"qs3{u}"], [f"qb3n{u}"])
                    rk = [f"qb3n{u}"]
                    for tl in range(nt):
                        t = t0 + tl
                        rope32(v_s[:, 2 * tl:2 * tl + 2, 64:96], 2,
                               tabB[:, t:t + 1, 0:32].broadcast_to([128, 2, 32]), tabB[:, t:t + 1, 32:64].broadcast_to([128, 2, 32]),
                               v_d[:, 2 * tl:2 * tl + 2, 64:96], (s3[u][2 * tl], s3[u][2 * tl + 1]),
                               [f"qs3{u}"], [f"qb3r{u}_{tl}"], f"qr{u}_{tl}", e2="dve")
                        rk.append(f"qb3r{u}_{tl}")
                    pS = pp[i][:, 512:1024].bitcast(BF16)
                    for tl in range(nt):
                        for hh in range(2):
                            tp(pS[0:96, (tl * 2 + hh) * 192:(tl * 2 + hh) * 192 + 128],
                               qb3[u][:, tl * 192 + hh * 96:tl * 192 + (hh + 1) * 96], rk, [pk(i, 1)])
                    for hh in range(2):
                        cp("dve", QTB[hh][0:96, t0 * 128:(t0 + nt) * 128].rearrange("p (t n) -> p t n", t=nt),
                           pS[0:96, 0:nt * 384].rearrange("p (t h n) -> p t h n", t=nt, h=2)[:, :, hh, 0:128], [pk(i, 1)], ["BQ"])

                for step in range(11):
                    if step < 9:
                        qb_A(step)
                    for u_ in (2 * step, 2 * step + 1):
                        if u_ < 16:
                            kb_unit(u_)
                    if step < 8:
                        vb_unit(step)
                    if 0 <= step - 2 < 9:
                        qb_B(step - 2)
                gates_fm(lambda c, wg=wg: wg[:, c, :], GTB, [wgk], "B")
                attention_l0(lambda h, kb: KTB[h][0:96, kb * 128:(kb + 1) * 128],
                             lambda h, q0, nq: QTB[h][0:96, q0:q0 + nq],
                             lambda h, kb: VB[:, kb, 64 * h:64 * h + 128],
                             SCALE_B, GTB, 4 + p, "B", PT, rd, fint)
            S.barrier()
            if stage <= 3:
                raise _Stop()
            release(mL0)

            x1 = sb("x1", [128, NQ, D], F32)
            mP4 = mark()
            wo = sb("wo", [128, 8, D], BF16)
            xt4 = [sb(f"xt4_{i}", [128, D], F32) for i in range(2)]
            yg = [sb(f"yg{i}", [128, D], F32) for i in range(2)]
            dma("pool", wo[:].rearrange("p c n -> p (c n)"), wout0_d, (), ["wo"])
            for t in range(NQ):
                S.tag = f'p4 t{t}'
                s = t % 2
                dma("sp", xt4[s][:], x_d[t * 128:(t + 1) * 128, :], (), [f"xt4{s}"])
                for half in range(2):
                    for c in range(8):
                        mm(pp[2 * s + half][:, 0:512], mixT[:, c, t * 128:(t + 1) * 128], wo[:, c, half * 512:(half + 1) * 512],
                           c == 0, c == 7, ["wo"], [pk(2 * s + half, 0)])
                    tt("dve", yg[s][:, half * 512:(half + 1) * 512], pp[2 * s + half][:, 0:512], gate_bc[:, half * 512:(half + 1) * 512],
                       ALU.mult, [pk(2 * s + half, 0)], [f"yg{s}_{half}"])
                tt("dve", x1[:, t, :], yg[s][:], xt4[s][:], ALU.add, [f"yg{s}_0", f"yg{s}_1", f"xt4{s}"], [f"x1_{t}"])
                if debug:
                    dma("sp", dbg_d[t * 128:(t + 1) * 128, :], x1[:, t, :], [f"x1_{t}"], [f"dbg{t}"])
            S.barrier()
            if stage <= 4:
                raise _Stop()
            release(mP4)

            mL1 = mark()
            h1T = sb("h1T", [128, 8, TQ], BF16)
            K1T = sb("K1T", [128, 2, TQ], BF16)
            V1 = sb("V1", [128, NQ, 384], BF16)
            dtab = sb("dtab", [128, 384], F32)
            esink = sb("esink", [128, 16], F32)
            mL1a = mark()
            g_bc = sb("g_bc1", [128, D], F32)
            sh_bc = sb("sh_bc1", [128, D], F32)
            emit_mods(1, g_bc, sh_bc, 256)
            xs1 = [sb(f"xs1_{i}", [128, D], BF16) for i in range(2)]
            xm1 = [sb(f"xm1_{i}", [128, D], F32) for i in range(2)]
            wkv1 = sb("wkv1", [128, 8, 512], BF16)
            dma("pool", wkv1[:].rearrange("p c n -> p (c n)"), wkv1_d, (), ["wkv1"])
            dma("sp", dtab[:], dtab_d, (), ["dtab"])
            dma("sp", esink[:], sink_d.partition_broadcast(128), (), ["esink"])
            act(esink[:], esink[:], AF.Exp, ["esink"], ["esink"])
            mset("pool", V1[:], 1.0, ["V1_init"])

            def l1_pre(t):
                s = t % 2
                hT_pre(x1[:, t, :], [], s, xm1[s][:], f"xm1{s}", xs1[s][:], g_bc, sh_bc)

            def l1_tr(t):
                s = t % 2
                hT_tr(s, xs1[s][:], h1T[:, :, t * 128:(t + 1) * 128], [f"h1T{t}"])

            def l1_v(t):
                s = t % 2
                pj = pp[2 * s + 1]
                for c in range(8):
                    mm(pj[:, 0:256], h1T[:, c, t * 128:(t + 1) * 128], wkv1[:, c, 256:512], c == 0, c == 7,
                       [f"h1T{t}", "wkv1"], [pk(2 * s + 1, 0)])
                cp("dve", V1[:, t, :].rearrange("p (j a b) -> p j a b", j=2, a=3)[:, :, 0:3:2, :],
                   pj[:, 0:256].rearrange("p (j a b) -> p j a b", j=2, a=2), [pk(2 * s + 1, 0), "V1_init"], ["V1"])

            for step in range(NQ + 2):
                if step < NQ:
                    l1_pre(step)
                if 0 <= step - 1 < NQ:
                    l1_tr(step - 1)
                if 0 <= step - 2 < NQ:
                    l1_v(step - 2)
            for j in range(2):
                for qg in range(5):
                    q0 = qg * 512
                    nq = min(512, TQ - q0)
                    i = (j * 5 + qg) % 2
                    for c in range(8):
                        mm(pp[i][:, 0:nq], wkv1[:, c, j * 128:(j + 1) * 128], h1T[:, c, q0:q0 + nq], c == 0, c == 7,
                           ["wkv1"] + [f"h1T{t}" for t in range(q0 // 128, (q0 + nq) // 128)], [pk(i, 0)])
                    cp("act" if qg % 2 == 0 else "dve", K1T[:, j, q0:q0 + nq], pp[i][:, 0:nq], [pk(i, 0)], ["K1T"])
            S.barrier()
            if stage <= 5:
                raise _Stop()
            release(mL1a)

            wqg1 = [sb(f"wqg1_{i}", [128, 8, 256], BF16) for i in range(2)]
            Q1T = sb("Q1T", [128, TO], BF16)
            G1T = sb("G1T", [128, TO], BF16)
            Etab = sb("Etab", [128, 16, 384], BF16)
            esb = sb("esb", [128, 1], F32)
            PT1 = [sb(f"PT1_{i}", [128, 768], BF16) for i in range(4)]
            rd1 = [sb("rd1_0", [128, 512], F32)] * 2
            tm1 = [sb("tm1_0", [128, 512], F32)] * 2
            dma("pool", wqg1[0][:].rearrange("p c n -> p (c n)"), wqg1_d[0], (), ["wqg0"])

            slopes = [2.0 ** (-8.0 * (h + 1) / 16.0) for h in range(16)]
            for h_ in range(16):
                act(Etab[:, h_, :], dtab[:], AF.Exp, ["dtab"], ["Etab"], scale=-slopes[h_])
            for P_ in range(8):
                S.tag = f'L1 P{P_}'
                j, g = P_ // 4, P_ % 4
                heads = [(2 * j) * 4 + g, (2 * j + 1) * 4 + g]
                w = wqg1[P_ % 2]
                wk = f"wqg{P_ % 2}"
                if P_ + 1 < 8:
                    dma("pool", wqg1[(P_ + 1) % 2][:].rearrange("p c n -> p (c n)"), wqg1_d[P_ + 1], (), [f"wqg{(P_ + 1) % 2}"])
                cp("pool", esb[64:128, 0:1], esink[64:128, heads[0]:heads[0] + 1], ["esink"], ["esb"])
                cp("pool", esb[0:64, 0:1], esink[0:64, heads[1]:heads[1] + 1], ["esink"], ["esb"])
                for qg in range(4):
                    q0 = qg * 512
                    i = qg % 2
                    for c in range(8):
                        mm(pp[i][:, 0:512], w[:, c, 0:128], h1T[:, c, q0:q0 + 512], c == 0, c == 7, [wk], [pk(i, 0)])
                    cp("dve", Q1T[:, q0:q0 + 512], pp[i][:, 0:512], [pk(i, 0)], ["Q1T"])
                    for c in range(8):
                        mm(pp[i][:, 512:1024], w[:, c, 128:256], h1T[:, c, q0:q0 + 512], c == 0, c == 7, [wk], [pk(i, 1)])
                    act(G1T[:, q0:q0 + 512], pp[i][:, 512:1024], AF.Silu, [pk(i, 1)], ["G1T"])
                its1 = []
                for G in range(4):
                    jb0 = G * 4
                    kbs = list(range(max(jb0 - 1, 0), min(jb0 + 4, 16) + 1))
                    for idx, kb in enumerate(kbs):
                        qlo = max(kb - 1, jb0)
                        qhi = min(kb + 1, jb0 + 3)
                        its1.append(dict(G=G, kb=kb, first=(idx == 0), last=(idx == len(kbs) - 1), qlo=qlo,
                                         nq=(qhi - qlo + 1) * 128, d0=(qlo - (kb - 1)) * 128, lo=(qlo - jb0) * 128))
                n1 = len(its1)

                def l1_qk(i):
                    it = its1[i]
                    ssl = i % 2
                    nq, kb, qlo = it["nq"], it["kb"], it["qlo"]
                    for hh in range(2):
                        mm(pp[ssl][:, hh * 512:hh * 512 + nq], K1T[64 * hh:64 * hh + 64, j, kb * 128:(kb + 1) * 128],
                           Q1T[64 * hh:64 * hh + 64, qlo * 128:qlo * 128 + nq], True, True, ["Q1T"], [pk(ssl, hh)])

                def l1_exp(i):
                    it = its1[i]
                    ssl = i % 2
                    psl = i % 4
                    nq = it["nq"]
                    act(PT1[psl][:].rearrange("p (h n) -> p h n", h=2)[:, :, 0:nq],
                        pp[ssl][:].rearrange("p (h n) -> p h n", h=2)[:, :, 0:nq], AF.Exp,
                        [pk(ssl, 0), pk(ssl, 1)], [f"PT1_{psl}_0", f"PT1_{psl}_1"], scale=SCALE_C)

                def l1_bias(i):
                    it = its1[i]
                    ssl = i % 4
                    nq, d0 = it["nq"], it["d0"]
                    for hh in range(2):
                        tt("dve", PT1[ssl][:, hh * 384:hh * 384 + nq], PT1[ssl][:, hh * 384:hh * 384 + nq],
                           Etab[:, heads[hh], d0:d0 + nq], ALU.mult, [f"PT1_{ssl}_{hh}", "Etab"], [f"PT1_{ssl}_{hh}"])

                def l1_pv(i):
                    it = its1[i]
                    ssl = i % 4
                    osl = it["G"] % 2
                    psO = pp[2 + osl]
                    nq, kb, lo = it["nq"], it["kb"], it["lo"]
                    for hh in range(2):
                        mm(psO[:, hh * 512 + lo:hh * 512 + lo + nq], V1[:, kb, j * 192 + 64 * hh:j * 192 + 64 * hh + 128],
                           PT1[ssl][:, hh * 384:hh * 384 + nq], it["first"], it["last"], [f"PT1_{ssl}_{hh}"], [pk(2 + osl, hh)],
                           skip=True)
                    if not it["last"]:
                        return
                    r_ = rd1[osl]
                    tmp = tm1[osl]
                    q0 = it["G"] * 512
                    act(r_[64:128, :], psO[64:128, 0:512], AF.Ln, [pk(2 + osl, 0), "esb"], ["rd1a"], bias=esb[64:128, 0:1])
                    act(r_[0:64, :], psO[0:64, 512:1024], AF.Ln, [pk(2 + osl, 1), "esb"], ["rd1b"], bias=esb[0:64, 0:1])
                    act(r_[:, :], r_[:, :], AF.Exp, ["rd1a", "rd1b"], ["rd1"], scale=-1.0)
                    tt("dve", tmp[0:64, :], psO[0:64, 0:512], r_[64:128, :], ALU.mult, [pk(2 + osl, 0), "rd1"], ["tm1"])
                    tt("dve", tmp[64:128, :], psO[64:128, 512:1024], r_[0:64, :], ALU.mult, [pk(2 + osl, 1), "rd1"], ["tm1"])
                    tt("pool", mixT[:, P_, q0:q0 + 512], tmp[:, :], G1T[:, q0:q0 + 512], ALU.mult, ["tm1", "G1T"], ["mix1"])

                for step in range(n1 + 3):
                    if 0 <= step - 3 < n1:
                        l1_pv(step - 3)
                    if step < n1:
                        l1_qk(step)
                    if 0 <= step - 1 < n1:
                        l1_exp(step - 1)
                    if 0 <= step - 2 < n1:
                        l1_bias(step - 2)
            S.barrier()
            if stage <= 6:
                raise _Stop()
            release(mL1)

            wo1 = sb("wo1", [128, 8, D], BF16)
            fn_bc = sb("fn_bc", [128, D], F32)
            yg = [sb(f"yg1_{i}", [128, D], F32) for i in range(2)]
            x2 = [sb(f"x2_{i}", [128, D], F32) for i in range(2)]
            ot = [sb(f"ot{i}", [128, D], F32) for i in range(2)]
            dma("pool", wo1[:].rearrange("p c n -> p (c n)"), wout1_d, (), ["wo1"])
            dma("sp", fn_bc[:], fnorm_d.partition_broadcast(128), (), ["fn_bc"])
            for t in range(16):
                s = t % 2
                for half in range(2):
                    for c in range(8):
                        mm(pp[2 * s + half][:, 0:512], mixT[:, c, t * 128:(t + 1) * 128], wo1[:, c, half * 512:(half + 1) * 512],
                           c == 0, c == 7, ["wo1"], [pk(2 * s + half, 0)])
                    tt("dve", yg[s][:, half * 512:(half + 1) * 512], pp[2 * s + half][:, 0:512], gate_bc[:, half * 512:(half + 1) * 512],
                       ALU.mult, [pk(2 * s + half, 0)], [f"yg{s}_{half}"])
                tt("dve", x2[s][:], yg[s][:], x1[:, t, :], ALU.add, [f"yg{s}_0", f"yg{s}_1"], [f"x2_{s}"])
                ss = stat[:, 16 + s:17 + s]
                act(junk[:], x2[s][:], AF.Square, [f"x2_{s}"], ["junk", f"fss{s}"], accum=ss)
                rstd(ss, ss, float(D), [f"fss{s}"], [f"fss{s}"])
                stt("dve", ot[s][:], x2[s][:], ss, fn_bc[:], ALU.mult, ALU.mult, [f"x2_{s}", f"fss{s}", "fn_bc"], [f"ot{s}"])
                dma("sp", y_d[t * 128:(t + 1) * 128, :], ot[s][:], [f"ot{s}"], [f"y{t}"])
                okeys.append(f"y{t}")
        except _Stop:
            pass
        print("SBUF arena peak words:", A["peak"], "of", ARENA_WORDS)
        S.emit(nc, final_wait_keys=okeys)
    nc._sched = S
    return nc


def _rope_cs(pos, dim):
    inv = (np.float32(10000.0) ** (-(np.arange(0, dim, 2, dtype=np.float32)) / np.float32(dim))).astype(np.float32)
    ang = pos.astype(np.float32)[:, None] * inv[None, :]
    return np.cos(ang).astype(np.float32), np.sin(ang).astype(np.float32)


def _tables(tok):
    row = tok // 64
    col = tok % 64
    cr, sr = _rope_cs(row, 32)
    cc, sc_ = _rope_cs(col, 32)
    ct, st_ = _rope_cs(tok, 32)
    cosA = np.concatenate([cr, cr, cc, cc], axis=1)
    sinA = np.concatenate([-sr, sr, -sc_, sc_], axis=1)
    cosB = np.concatenate([ct, ct], axis=1)
    sinB = np.concatenate([-st_, st_], axis=1)

    tab = np.concatenate([cosA, sinA, cosB, sinB], axis=1).astype(np.float32)
    return np.ascontiguousarray(tab.reshape(NT, 128, 192))


def _dtab():
    s = np.arange(128)[:, None]
    q = np.arange(384)[None, :] - 128
    d = np.abs(q - s).astype(np.float32)
    return np.where(d <= 128, d, np.float32(BIGD)).astype(np.float32)


def _prep(inputs):
    f = lambda a: np.ascontiguousarray(np.asarray(a, dtype=np.float32))
    x = f(inputs["x"])
    c = f(inputs["c"])
    w_in = f(inputs["even_w_in"])[0]
    qa, ka, va, ga = w_in[:, 0:512], w_in[:, 512:640], w_in[:, 640:768], w_in[:, 768:1280]
    cq, ckv, kr, gb = w_in[:, 1280:1536], w_in[:, 1536:1664], w_in[:, 1664:1696], w_in[:, 1696:2208]
    wk0 = np.concatenate([ka, va, ckv, kr, cq], axis=1)
    permA = [np.concatenate([np.arange((0 * 4 + g) * 64, (0 * 4 + g) * 64 + 64), np.arange((1 * 4 + g) * 64, (1 * 4 + g) * 64 + 64)])
             for g in range(4)]
    wA = np.stack([np.concatenate([qa[:, permA[g]], ga[:, permA[g]]], axis=1) for g in range(4)])
    wgb = np.stack([gb[:, p * 128:(p + 1) * 128] for p in range(4)])
    wout = f(inputs["even_w_out"])[0]
    rows0 = np.concatenate([np.concatenate(permA), np.arange(512, 1024)])
    wout0 = wout[rows0, :]
    w1 = f(inputs["odd_w_in"])[0]
    qc, kc, vc, gc = w1[:, 0:1024], w1[:, 1024:1280], w1[:, 1280:1536], w1[:, 1536:2560]
    perm1 = []
    for P_ in range(8):
        j, g = P_ // 4, P_ % 4
        hA, hB = (2 * j) * 4 + g, (2 * j + 1) * 4 + g
        perm1.append(np.concatenate([np.arange(hA * 64, hA * 64 + 64), np.arange(hB * 64, hB * 64 + 64)]))
    wqg1 = np.stack([np.concatenate([qc[:, perm1[P_]], gc[:, perm1[P_]]], axis=1) for P_ in range(8)])
    wkv1 = np.concatenate([kc, vc], axis=1)
    wout1 = f(inputs["odd_w_out"])[0][np.concatenate(perm1), :]
    def pm(w):
        cN = w.shape[0] // 128
        return np.ascontiguousarray(w.reshape(cN, 128, w.shape[1]).transpose(1, 0, 2).reshape(128, cN * w.shape[1]))
    adaw = f(inputs["ada_w"]).reshape(2, 8, 128, 12, 256).transpose(0, 3, 2, 1, 4).reshape(2, 12, 128, 8 * 256)
    shared = {
        "ada_w": np.ascontiguousarray(adaw), "ada_b": f(inputs["ada_b"]), "norm_w": f(inputs["norm_w"]),
        "final_norm": f(inputs["final_norm"]), "ident": np.eye(128, dtype=np.float32),
        "wk0": pm(wk0), "wA": np.stack([pm(wA[i]) for i in range(4)]), "wgb": np.stack([pm(wgb[i]) for i in range(4)]),
        "wuq": pm(f(inputs["b_w_uq"])[0]), "wuk": np.ascontiguousarray(f(inputs["b_w_uk"])[0].reshape(128, 512)),
        "wuv": np.ascontiguousarray(f(inputs["b_w_uv"])[0].reshape(128, 512)),
        "wout0": pm(wout0),
        "aqn": f(inputs["a_q_norm"])[0], "akn": f(inputs["a_k_norm"])[0],
        "bqn": f(inputs["b_q_lora_norm"])[0], "bkvn": f(inputs["b_kv_lora_norm"])[0],
        "wkv1": pm(wkv1), "wqg1": np.stack([pm(wqg1[i]) for i in range(8)]), "wout1": pm(wout1),
        "sink": f(inputs["c_sink"])[0], "dtab": _dtab(),
    }
    tabs = [_tables(np.arange(SEQ)), _tables(SEQ - 1 - np.arange(SEQ))]
    in_maps = []
    for core in range(8):
        b, hf = core // 2, core % 2
        xl = x[b] if hf == 0 else x[b][::-1]
        m = dict(shared)
        m["x_loc"] = np.ascontiguousarray(xl)
        m["c_col"] = np.ascontiguousarray(c[b].reshape(8, 128).T)
        m["tab"] = tabs[hf]
        tq = tabs[hf][0:NQ]
        m["tabqa"] = np.ascontiguousarray(tq[:, :, 0:128].transpose(1, 0, 2).reshape(128, NQ * 128))
        m["tabqb"] = np.ascontiguousarray(tq[:, :, 128:192].transpose(1, 0, 2).reshape(128, NQ * 64))
        in_maps.append(m)
    return in_maps


_NC_CACHE = {}


def kernel(**inputs):
    debug = bool(inputs.pop("_debug", False))
    in_maps = _prep(inputs)
    if debug not in _NC_CACHE:
        _NC_CACHE[debug] = build_program(debug)
    nc = _NC_CACHE[debug]
    res = run_bass_kernel_spmd(nc, in_maps, core_ids=list(range(8)))
    out = np.empty((4, SEQ, D), dtype=np.float32)
    for core in range(8):
        b, hf = core // 2, core % 2
        y = np.asarray(res.results[core]["y"], dtype=np.float32)
        if hf == 0:
            out[b, 0:TO] = y
        else:
            out[b, TO:SEQ] = y[::-1]
    if debug:
        return out, [np.asarray(res.results[core]["dbg"]) for core in range(8)], [dict(h=np.asarray(res.results[core]["dbg2"]).astype(np.float32).reshape(128, 8, TQ), k=np.asarray(res.results[core]["dbg3"]).astype(np.float32), m=np.asarray(res.results[core]["dbg4"])) for core in range(8)]
    return out
```

```python
import contextlib
import numpy as np
import concourse.bass as bass
import concourse.mybir as mybir
from concourse.bass_utils import run_bass_kernel_spmd

F32 = mybir.dt.float32
BF16 = mybir.dt.bfloat16
AF = mybir.ActivationFunctionType
ALU = mybir.AluOpType
AX = mybir.AxisListType

D = 1024
SEQ = 4096
NT = 32
NQ = 17
TQ = NQ * 128
TO = 2048
EPS = 1e-6
SCALE_A = 64 ** -0.5
SCALE_B = 96 ** -0.5
SCALE_C = 64 ** -0.5
BIGD = 1.0e5


class _Ins:
    __slots__ = ("eng", "idx", "fn", "deps", "dma", "signal", "sem", "val", "tag", "rw")

    def __init__(self, eng, idx, fn, deps, dma):
        self.eng, self.idx, self.fn, self.deps, self.dma = eng, idx, fn, deps, dma
        self.signal = dma
        self.sem = None
        self.val = 0


class Sched:
    ENGS = ["pe", "act", "dve", "pool", "sp"]
    NDS = 8

    def __init__(self):
        self.ins = {e: [] for e in self.ENGS}
        self.last_w = {}
        self.readers = {}
        self.bar = set()
        self.bar_done = {e: True for e in self.ENGS}
        self.tag = ""
        self.names = {}

    def barrier(self):
        deps = set()
        for e in self.ENGS:
            lst = self.ins[e]
            last_c = None
            nd = 0
            for i in range(len(lst) - 1, -1, -1):
                t = lst[i]
                if t.dma:
                    if nd < self.NDS:
                        deps.add((e, i))
                        nd += 1
                elif last_c is None:
                    last_c = i
                    deps.add((e, i))
                if nd >= self.NDS and last_c is not None:
                    break
        for (e, i) in deps:
            self.ins[e][i].signal = True
        self.bar = deps
        self.bar_done = {e: False for e in self.ENGS}
        self.last_w = {}
        self.readers = {}

    def add(self, eng, fn, reads=(), writes=(), dma=False):
        lst = self.ins[eng]
        idx = len(lst)
        deps = set()
        for k in list(reads) + list(writes):
            w = self.last_w.get(k)
            if w is not None:
                deps.add(w)
        for k in writes:
            for rk, i in self.readers.get(k, {}).items():
                e = rk if isinstance(rk, str) else rk[1]
                deps.add((e, i))
        final = set()
        for (e, i) in deps:
            t = self.ins[e][i]
            if e == eng and eng == "pe" and not t.dma and not dma:
                continue
            t.signal = True
            final.add((e, i))
        if not self.bar_done[eng]:
            final |= {d for d in self.bar if not (d[0] == eng and eng == "pe" and not self.ins[d[0]][d[1]].dma)}
            self.bar_done[eng] = True
        ins = _Ins(eng, idx, fn, final, dma)
        ins.tag = self.tag
        ins.rw = (tuple(reads), tuple(writes))
        lst.append(ins)
        for k in writes:
            self.last_w[k] = (eng, idx)
            self.readers[k] = {}
        for k in reads:
            d = self.readers.setdefault(k, {})
            if dma:
                d[("dma", eng, idx)] = idx
            else:
                d[eng] = max(d.get(eng, -1), idx)
        return ins

    def emit(self, nc, final_wait_keys=()):
        with contextlib.ExitStack() as st:
            csem = {e: st.enter_context(nc.semaphore("c_" + e)) for e in self.ENGS if e != "sp"}
            dsem = {e: [st.enter_context(nc.semaphore(f"d_{e}{j}")) for j in range(self.NDS)]
                    for e in ("sp", "pool", "act")}
            for e in self.ENGS:
                cc = 0
                dc = 0
                for t in self.ins[e]:
                    if t.dma:
                        t.sem = dsem[e][dc % self.NDS]
                        t.val = 16 * (dc // self.NDS + 1)
                        dc += 1
                    elif t.signal:
                        cc += 1
                        t.sem = csem[e]
                        t.val = cc
            final_deps = set()
            for k in final_wait_keys:
                w = self.last_w.get(k)
                if w is not None:
                    final_deps.add(w)
            block = st.enter_context(nc.Block())

            def run(engh, e):
                known = {}

                def waits_raw(sem, v):
                    key = id(sem)
                    if known.get(key, 0) >= v:
                        return
                    engh.wait_ge(sem, v)
                    known[key] = v

                def waits(deps):
                    need = {}
                    for (de, di) in deps:
                        d = self.ins[de][di]
                        key = id(d.sem)
                        if known.get(key, 0) >= d.val:
                            continue
                        if key not in need or need[key][1] < d.val:
                            need[key] = (d.sem, d.val)
                    for key, (s, v) in need.items():
                        engh.wait_ge(s, v)
                        known[key] = v

                for t in self.ins[e]:
                    waits(t.deps)
                    if t.dma and t.val > 16:
                        waits_raw(t.sem, t.val - 16)
                    r = t.fn(engh)
                    try:
                        self.names[r.ins.name] = (t.eng, t.idx, t.tag, t.rw)
                    except Exception:
                        pass
                    if t.signal:
                        r.then_inc(t.sem, 16 if t.dma else 1)
                if e == "sp":
                    waits(final_deps)

            @block.tensor
            def _(h):
                run(h, "pe")

            @block.scalar
            def _(h):
                run(h, "act")

            @block.vector
            def _(h):
                run(h, "dve")

            @block.gpsimd
            def _(h):
                run(h, "pool")

            @block.sync
            def _(h):
                run(h, "sp")


class _Stop(Exception):
    pass


def build_program(debug=False, stage=99):
    nc = bass.Bass("TRN2", target_bir_lowering=False)
    S = Sched()

    def din(name, shape):
        return nc.dram_tensor(name, list(shape), F32, kind="ExternalInput").ap()

    x_d = din("x_loc", [SEQ, D])
    c_d = din("c_col", [128, 8])
    adaw_d = din("ada_w", [2, 12, 128, 8 * 256])
    adab_d = din("ada_b", [2, 3 * D])
    normw_d = din("norm_w", [2, D])
    fnorm_d = din("final_norm", [D])
    ident_d = din("ident", [128, 128])
    wk0_d = din("wk0", [128, 8 * 672])
    wA_d = din("wA", [4, 128, 8 * 256])
    wgb_d = din("wgb", [4, 128, 8 * 128])
    wuq_d = din("wuq", [128, 2 * 768])
    wuk_d = din("wuk", [128, 512])
    wuv_d = din("wuv", [128, 512])
    wout0_d = din("wout0", [128, 8 * D])
    aqn_d = din("aqn", [64])
    akn_d = din("akn", [64])
    bqn_d = din("bqn", [256])
    bkvn_d = din("bkvn", [128])
    tab_d = din("tab", [NT, 128, 192])
    tabqa_d = din("tabqa", [128, NQ * 128])
    tabqb_d = din("tabqb", [128, NQ * 64])
    wkv1_d = din("wkv1", [128, 8 * 512])
    wqg1_d = din("wqg1", [8, 128, 8 * 256])
    wout1_d = din("wout1", [128, 8 * D])
    sink_d = din("sink", [16])
    dtab_d = din("dtab", [128, 384])
    y_d = nc.dram_tensor("y", [TO, D], F32, kind="ExternalOutput").ap()
    dbg_d = None
    if debug:
        dbg_d = nc.dram_tensor("dbg", [TQ, D], F32, kind="ExternalOutput").ap()
        dbg2_d = nc.dram_tensor("dbg2", [128, 8 * TQ], BF16, kind="ExternalOutput").ap()
        dbg3_d = nc.dram_tensor("dbg3", [128, 3 * SEQ + NT * 192 + 2 * TQ], BF16, kind="ExternalOutput").ap()
        dbg4_d = nc.dram_tensor("dbg4", [128, 3 * D], F32, kind="ExternalOutput").ap()

    def mm(out, lhsT, rhs, start, stop, r, w, skip=False):
        if skip:
            S.add("pe", lambda e: e.matmul(out, lhsT=lhsT, rhs=rhs, start=start, stop=stop, skip_group_check=True), r, w)
        else:
            S.add("pe", lambda e: e.matmul(out, lhsT=lhsT, rhs=rhs, start=start, stop=stop), r, w)

    def tp(out, in_, r, w):
        S.add("pe", lambda e: e.transpose(out, in_, ident[:]), list(r) + ["ident"], w)

    def act(out, in_, func, r, w, bias=None, scale=None, accum=None):
        kw = {}
        if bias is not None:
            kw["bias"] = bias
        if scale is not None:
            kw["scale"] = scale
        if accum is not None:
            kw["accum_out"] = accum
        S.add("act", lambda e: e.activation(out=out, in_=in_, func=func, **kw), r, w)

    def tt(eng, out, in0, in1, op, r, w):
        S.add(eng, lambda e: e.tensor_tensor(out=out, in0=in0, in1=in1, op=op), r, w)

    def ts(eng, out, in0, s1, s2, op0, op1, r, w):
        if s2 is None:
            S.add(eng, lambda e: e.tensor_scalar(out=out, in0=in0, scalar1=s1, scalar2=None, op0=op0), r, w)
        else:
            S.add(eng, lambda e: e.tensor_scalar(out=out, in0=in0, scalar1=s1, scalar2=s2, op0=op0, op1=op1), r, w)

    def stt(eng, out, in0, scalar, in1, op0, op1, r, w):
        S.add(eng, lambda e: e.scalar_tensor_tensor(out=out, in0=in0, scalar=scalar, in1=in1, op0=op0, op1=op1), r, w)

    def cp(eng, out, in_, r, w):
        if eng == "act":
            S.add("act", lambda e: e.activation(out=out, in_=in_, func=AF.Copy), r, w)
        else:
            S.add(eng, lambda e: e.tensor_copy(out=out, in_=in_), r, w)

    def red(out, in_, r, w):
        S.add("dve", lambda e: e.tensor_reduce(out=out, in_=in_, axis=AX.X, op=ALU.add), r, w)

    def rcp(out, in_, r, w):
        S.add("dve", lambda e: e.reciprocal(out=out, in_=in_), r, w)

    def mset(eng, ap, val, w):
        S.add(eng, lambda e: e.memset(ap, val), (), w)

    def dma(q, out, in_, r, w):
        S.add(q, lambda e: e.dma_start(out=out, in_=in_), r, w, dma=True)

    def rstd(dst, ss, n, r, w):
        act(dst, ss, AF.Ln, r, w, bias=EPS, scale=1.0 / n)
        act(dst, dst, AF.Exp, w, w, scale=-0.5)

    with contextlib.ExitStack() as outer:
        ARENA_WORDS = 52992
        arena_t = outer.enter_context(nc.sbuf_tensor("arena", [128, ARENA_WORDS], F32))
        A = {"top": 0, "peak": 0}

        def sb(name, shape, dt, st=None):
            n = 1
            for d_ in shape[1:]:
                n *= int(d_)
            nbytes = n * (2 if dt == BF16 else 4)
            w = (nbytes + 31) // 32 * 8
            off = A["top"]
            A["top"] += w
            assert A["top"] <= ARENA_WORDS, ("SBUF arena overflow", name, A["top"])
            A["peak"] = max(A["peak"], A["top"])
            v = arena_t[:, off:off + w]
            if dt == BF16:
                v = v.bitcast(BF16)
            v = v[:, 0:n]
            if len(shape) == 3:
                v = v.rearrange("p (a b) -> p a b", a=int(shape[1]))
            elif len(shape) == 4:
                v = v.rearrange("p (a b c) -> p a b c", a=int(shape[1]), b=int(shape[2]))
            return v

        def mark():
            return A["top"]

        def release(m):
            A["top"] = m

        pp = [outer.enter_context(nc.psum_tensor(f"pp{i}", [128, 1024], F32)) for i in range(4)]

        def pk(i, h):
            return f"pp{i}{'ab'[h]}"

        okeys = []
        try:
            ident = sb("ident", [128, 128], BF16)
            c_sb = sb("c_sb", [128, 8], F32)
            gate_bc = sb("gate_bc", [128, D], F32)
            junk = sb("junk", [128, D], BF16)
            stat = sb("stat", [128, 64], F32)
            mixT = sb("mixT", [128, 8, TQ], BF16)

            dma("pool", ident[:], ident_d, (), ["ident"])
            dma("sp", c_sb[:], c_d, (), ["c_sb"])
            act(c_sb[:], c_sb[:], AF.Silu, ["c_sb"], ["c_sb"])

            def emit_mods(layer, g_bc, sh_bc, SW, nslots=2, queues=("sp",)):
                m = mark()
                ones_f = sb("ones_f", [128, 128], F32)
                crep = sb("crep", [128, 8, 128], F32)
                aw = [sb(f"aw{i}", [128, 8, SW], F32) for i in range(nslots)]
                bb = sb("adab", [128, SW], F32)
                nw = sb("nw", [128, D], F32)
                mset("dve", ones_f[:], 1.0, ["ones_f"])
                for k in range(8):
                    ts("dve", crep[:, k, :], ones_f[:], c_sb[:, k:k + 1], None, ALU.mult, None,
                       ["ones_f", "c_sb"], [f"crep{k}"])
                dma("sp", nw[:], normw_d[layer].partition_broadcast(128), (), ["nw"])
                for s in range(3 * D // SW):
                    sl = s % nslots
                    kind = (s * SW) // D
                    c0 = (s * SW) % D
                    dma(queues[s % len(queues)], aw[sl][:].rearrange("p k n -> p (k n)"), adaw_d[layer][s], (), [f"aw{sl}"])
                    dma("sp", bb[:], adab_d[layer][s * SW:(s + 1) * SW].partition_broadcast(128), (), ["adab"])
                    ps = pp[s % 2][:, 0:SW]
                    for k in range(8):
                        mm(ps, crep[:, k, :], aw[sl][:, k, :], k == 0, k == 7, [f"crep{k}", f"aw{sl}"], [pk(s % 2, 0)])
                    col = slice(c0, c0 + SW)
                    if kind == 0:
                        tt("dve", sh_bc[:, col], ps, bb[:], ALU.add, [pk(s % 2, 0), "adab"], [f"modw{s}"])
                    elif kind == 1:
                        tt("dve", g_bc[:, col], ps, bb[:], ALU.add, [pk(s % 2, 0), "adab"], [f"modw{s}"])
                        stt("dve", g_bc[:, col], g_bc[:, col], 1.0, nw[:, col], ALU.add, ALU.mult,
                            [f"modw{s}", "nw"], [f"modw{s}"])
                    else:
                        tt("dve", gate_bc[:, col], ps, bb[:], ALU.add, [pk(s % 2, 0), "adab"], [f"modw{s}"])
                S.barrier()
                release(m)

            def norm_rope64(src, nt, nh, gain_bc, cos3, sin3, dst, scr, r, w, kp, tk=(), e2="pool"):
                n = nt * nh
                W = n * 64
                sq, qn, t1, t2 = scr["a"], scr["b"], scr["c"], scr["d"]
                ss = scr["st"]
                tt(e2, sq[:, :W], src, src, ALU.mult, r, [kp + "sq"])
                red(ss[:, :n], sq[:, :W].rearrange("p (n d) -> p n d", d=64), [kp + "sq"], [kp + "ss"])
                rstd(ss[:, :n], ss[:, :n], 64.0, [kp + "ss"], [kp + "ss"])
                tt("dve", qn[:, :W].rearrange("p (n d) -> p n d", d=64), src.rearrange("p (n d) -> p n d", d=64),
                   ss[:, :n].unsqueeze(2).broadcast_to([128, n, 64]), ALU.mult, list(r) + [kp + "ss"], [kp + "qn"])
                tt(e2, qn[:, :W].rearrange("p (n d) -> p n d", d=64), qn[:, :W].rearrange("p (n d) -> p n d", d=64),
                   gain_bc[:].unsqueeze(1).broadcast_to([128, n, 64]), ALU.mult, [kp + "qn", "gains"], [kp + "qn"])
                q4 = qn[:, :W].rearrange("p (t h d) -> p t h d", t=nt, h=nh)
                tt(e2, t1[:, :W].rearrange("p (t h d) -> p t h d", t=nt, h=nh), q4,
                   cos3.unsqueeze(2).broadcast_to([128, nt, nh, 64]), ALU.mult, [kp + "qn", "tabs"] + list(tk), [kp + "t1"])
                q6 = qn[:, :W].rearrange("p (t h x r i) -> p t h x r i", t=nt, h=nh, x=2, r=2)
                o6 = t2[:, :W].rearrange("p (t h x r i) -> p t h x r i", t=nt, h=nh, x=2, r=2)
                s5 = sin3.rearrange("p t (x r i) -> p t x r i", x=2, r=2)
                k = 0
                for h in range(nh):
                    for rr in range(2):
                        eng = e2 if k % 2 == 0 else "dve"
                        k += 1
                        tt(eng, o6[:, :, h, :, rr, :], q6[:, :, h, :, 1 - rr, :], s5[:, :, :, rr, :], ALU.mult,
                           [kp + "qn", "tabs"] + list(tk), [kp + f"t2_{h}{rr}"])
                tt("dve", dst, t1[:, :W], t2[:, :W], ALU.add,
                   [kp + "t1"] + [kp + f"t2_{h}{rr}" for h in range(nh) for rr in range(2)], w)

            def rope32(src3, a, cb, sbn, dst3, bufs, r, w, kp):
                u1 = bufs[0][:, :a * 32].rearrange("p (a d) -> p a d", d=32)
                u2 = bufs[1][:, :a * 32].rearrange("p (a d) -> p a d", d=32)
                tt("pool", u1, src3, cb, ALU.mult, list(r) + ["tabs"], [kp + "u1"])
                tt("pool", u2[:, :, 0:16], src3[:, :, 16:32], sbn[:, :, 0:16], ALU.mult, list(r) + ["tabs"], [kp + "u2a"])
                tt("dve", u2[:, :, 16:32], src3[:, :, 0:16], sbn[:, :, 16:32], ALU.mult, list(r) + ["tabs"], [kp + "u2b"])
                tt("dve", dst3, u1, u2, ALU.add, [kp + "u1", kp + "u2a", kp + "u2b"], w)

            MODK = []

            def hT_pre(src_ap, src_keys, s, xm_ap, xm_key, xs_ap, g_bc, sh_bc):
                ss = stat[:, s:s + 1]
                act(junk[:], src_ap, AF.Square, src_keys, ["junk", f"ss{s}"], accum=ss)
                rstd(ss, ss, float(D), [f"ss{s}"], [f"ss{s}"])
                stt("dve", xm_ap, src_ap, ss, g_bc[:], ALU.mult, ALU.mult, list(src_keys) + [f"ss{s}"], [xm_key])
                tt("dve", xs_ap, xm_ap, sh_bc[:], ALU.add, [xm_key], [f"xs{s}"])

            def hT_tr(s, xs_ap, dst, dkeys):
                pT = pp[2 * s][:, 0:512].bitcast(BF16)
                for c in range(8):
                    tp(pT[:, c * 128:(c + 1) * 128], xs_ap[:, c * 128:(c + 1) * 128], [f"xs{s}"], [pk(2 * s, 0)])
                cp("act", dst, pT.rearrange("p (c n) -> p c n", c=8), [pk(2 * s, 0)], dkeys)

            mL0 = mark()
            g_bc = sb("g_bc", [128, D], F32)
            sh_bc = sb("sh_bc", [128, D], F32)
            emit_mods(0, g_bc, sh_bc, 256, nslots=4, queues=("sp", "act"))
            if stage <= 0:
                raise _Stop()
            hTq = sb("hTq", [128, 8, TQ], BF16)
            tabB = sb("tabB", [128, NQ, 64], F32)
            aqn = sb("aqn", [128, 64], F32)
            akn = sb("akn", [128, 64], F32)
            bqn = sb("bqn", [128, 256], F32)
            bkvn = sb("bkvn", [128, 128], F32)
            ckvnT = sb("ckvnT", [128, SEQ], BF16)
            kropeT = sb("kropeT", [128, SEQ], BF16)
            cqnT = sb("cqnT", [128, 2, TQ], BF16)
            mLA = mark()
            KTA = sb("KTA", [128, SEQ], BF16)
            VA = sb("VA", [128, NT, 192], BF16)
            tabA = sb("tabA", [128, NQ, 128], F32)

            dma("sp", tabA[:].rearrange("p t d -> p (t d)"), tabqa_d, (), ["tabs"])
            dma("sp", tabB[:].rearrange("p t d -> p (t d)"), tabqb_d, (), ["tabs"])
            dma("sp", aqn[:], aqn_d.partition_broadcast(128), (), ["gains"])
            dma("sp", akn[:], akn_d.partition_broadcast(128), (), ["gains"])
            dma("sp", bqn[:], bqn_d.partition_broadcast(128), (), ["gains"])
            dma("sp", bkvn[:], bkvn_d.partition_broadcast(128), (), ["gains"])
            mset("pool", VA[:], 1.0, ["VA_init"])

            mP1 = mark()
            wk0 = sb("wk0", [128, 8, 672], BF16)
            xt = [sb(f"xt{i}", [128, D], F32) for i in range(2)]
            xs = [sb(f"xs{i}", [128, D], BF16) for i in range(2)]
            ks = [sb(f"ks{i}", [128, 672], F32) for i in range(3)]
            kbf = [sb(f"kbf{i}", [128, 544], BF16) for i in range(3)]
            hTt = sb("hTt", [128, 3, 8, 128], BF16)
            tabK = [sb(f"tabK{i}", [128, 192], F32) for i in range(3)]
            scr1 = [dict(a=sb("s1a", [128, 128], F32), b=sb("s1b", [128, 128], F32), c=sb("s1c", [128, 128], F32),
                         d=sb("s1d", [128, 128], F32), st=sb("s1s", [128, 8], F32), e=sb("s1e", [128, 32], F32),
                         f=sb("s1f", [128, 32], F32)) for i in range(3)]
            dma("pool", wk0[:].rearrange("p c n -> p (c n)"), wk0_d, (), ["wk0"])

            def p1_ctx(t):
                s = t % 2
                u = t % 3
                isq = t < NQ
                if isq:
                    hdst = hTq[:, :, t * 128:(t + 1) * 128]
                    hk = [f"hTq{t}"]
                    cA, sA = tabA[:, t:t + 1, 0:64], tabA[:, t:t + 1, 64:128]
                    cB, sB = tabB[:, t:t + 1, 0:32], tabB[:, t:t + 1, 32:64]
                    tk = []
                else:
                    hdst = hTt[:, u, :, :]
                    hk = [f"hTt{u}"]
                    cA, sA = tabK[u][:, 0:64].unsqueeze(1), tabK[u][:, 64:128].unsqueeze(1)
                    cB, sB = tabK[u][:, 128:160].unsqueeze(1), tabK[u][:, 160:192].unsqueeze(1)
                    tk = [f"tabK{u}"]
                return s, u, isq, hdst, hk, cA, sA, cB, sB, tk

            def p1_A1(t):
                S.tag = f'p1A1 t{t}'
                s, u, isq, hdst, hk, cA, sA, cB, sB, tk = p1_ctx(t)
                dma("sp", xt[s][:], x_d[t * 128:(t + 1) * 128, :], (), [f"xt{s}"])
                hT_pre(xt[s][:], [f"xt{s}"], s, xt[s][:], f"xt{s}", xs[s][:], g_bc, sh_bc)

            def p1_A2(t):
                S.tag = f'p1A2 t{t}'
                s, u, isq, hdst, hk, cA, sA, cB, sB, tk = p1_ctx(t)
                hT_tr(s, xs[s][:], hdst, hk)

            def p1_A3(t):
                S.tag = f'p1A3 t{t}'
                s, u, isq, hdst, hk, cA, sA, cB, sB, tk = p1_ctx(t)
                if not isq:
                    dma("sp", tabK[u][:], tab_d[t], (), [f"tabK{u}"])
                pj = pp[2 * s + 1]
                for c in range(8):
                    mm(pj[:, 0:416], hdst[:, c, :], wk0[:, c, 0:416], c == 0, c == 7, hk + ["wk0"], [pk(2 * s + 1, 0)])
                cp("act", ks[u][:, 0:416], pj[:, 0:416], [pk(2 * s + 1, 0)], [f"ksa{u}"])
                if isq:
                    for c in range(8):
                        mm(pj[:, 512:768], hdst[:, c, :], wk0[:, c, 416:672], c == 0, c == 7, hk + ["wk0"], [pk(2 * s + 1, 1)])
                    cp("act", ks[u][:, 416:672], pj[:, 512:768], [pk(2 * s + 1, 1)], [f"ksb{u}"])

            def p1_B(t):
                S.tag = f'p1B t{t}'
                s, u, isq, hdst, hk, cA, sA, cB, sB, tk = p1_ctx(t)
                sc = scr1[u]
                norm_rope64(ks[u][:, 0:128], 1, 2, akn, cA, sA, kbf[u][:, 0:128], sc,
                            [f"ksa{u}"] + tk, [f"kbfa{u}"], f"ka{u}", tk=tk)
                cp("pool", VA[:, t, :].rearrange("p (a b) -> p a b", b=64)[:, 0:3:2, :],
                   ks[u][:, 128:256].rearrange("p (a b) -> p a b", b=64), [f"ksa{u}", "VA_init"], [f"VA{t}"])
                ssc = stat[:, 4 + u:5 + u]
                act(junk[:, 0:128], ks[u][:, 256:384], AF.Square, [f"ksa{u}"], ["junk", f"ssc{u}"], accum=ssc)
                rstd(ssc, ssc, 128.0, [f"ssc{u}"], [f"ssc{u}"])
                stt("dve", kbf[u][:, 128:256], ks[u][:, 256:384], ssc, bkvn[:], ALU.mult, ALU.mult,
                    [f"ksa{u}", f"ssc{u}", "gains"], [f"kbfb{u}"])
                rope32(ks[u][:, 384:416].unsqueeze(1), 1, cB, sB, kbf[u][:, 256:288].unsqueeze(1), (sc["e"], sc["f"]),
                       [f"ksa{u}"] + tk, [f"kbfc{u}"], f"kr{u}")
                pS = pp[2 * s][:, 512:1024].bitcast(BF16)
                tp(pS[:, 0:128], kbf[u][:, 0:128], [f"kbfa{u}"], [pk(2 * s, 1)])
                tp(pS[:, 192:320], kbf[u][:, 128:256], [f"kbfb{u}"], [pk(2 * s, 1)])
                if isq:
                    ssq = stat[:, 8 + u:9 + u]
                    act(junk[:, 0:256], ks[u][:, 416:672], AF.Square, [f"ksb{u}"], ["junk", f"ssq{u}"], accum=ssq)
                    rstd(ssq, ssq, 256.0, [f"ssq{u}"], [f"ssq{u}"])
                    stt("dve", kbf[u][:, 288:544], ks[u][:, 416:672], ssq, bqn[:], ALU.mult, ALU.mult,
                        [f"ksb{u}", f"ssq{u}", "gains"], [f"kbfd{u}"])
                    tp(pS[:, 384:512], kbf[u][:, 288:416], [f"kbfd{u}"], [pk(2 * s, 1)])
                    tp(pS[:, 576:704], kbf[u][:, 416:544], [f"kbfd{u}"], [pk(2 * s, 1)])
                tp(pS[0:32, 768:896], kbf[u][:, 256:288], [f"kbfc{u}"], [pk(2 * s, 1)])
                cp("dve", KTA[:, t * 128:(t + 1) * 128], pS[:, 0:128], [pk(2 * s, 1)], [f"KTA{t}"])
                cp("dve", ckvnT[:, t * 128:(t + 1) * 128], pS[:, 192:320], [pk(2 * s, 1)], [f"ckvnT{t}"])
                cp("dve", kropeT[64:96, t * 128:(t + 1) * 128], pS[0:32, 768:896], [pk(2 * s, 1)], [f"kropeT{t}"])
                if isq:
                    cp("dve", cqnT[:, :, t * 128:(t + 1) * 128], pS[:, 384:768].rearrange("p (c n) -> p c n", c=2)[:, :, 0:128],
                       [pk(2 * s, 1)], [f"cqnT{t}"])

            for step in range(NT + 4):
                if step < NT:
                    p1_A1(step)
                if 0 <= step - 1 < NT:
                    p1_A2(step - 1)
                if 0 <= step - 2 < NT:
                    p1_A3(step - 2)
                if 0 <= step - 4 < NT:
                    p1_B(step - 4)
            S.barrier()
            if debug:
                dma("sp", dbg2_d, hTq[:].rearrange("p c n -> p (c n)"), (), ["dbg2"])
                dma("sp", dbg3_d[:, 0:SEQ], KTA[:], (), ["dbg3a"])
                dma("sp", dbg3_d[:, SEQ:2 * SEQ], ckvnT[:], (), ["dbg3b"])
                dma("sp", dbg3_d[64:96, 2 * SEQ:3 * SEQ], kropeT[64:96, :], (), ["dbg3c"])
                dma("sp", dbg3_d[:, 3 * SEQ:3 * SEQ + NT * 192], VA[:].rearrange("p t n -> p (t n)"), (), ["dbg3d"])
                dma("sp", dbg3_d[:, 3 * SEQ + NT * 192:], cqnT[:].rearrange("p c n -> p (c n)"), (), ["dbg3e"])
                dma("sp", dbg4_d[:, 0:D], g_bc[:], (), ["dbg4a"])
                dma("sp", dbg4_d[:, D:2 * D], sh_bc[:], (), ["dbg4b"])
                dma("sp", dbg4_d[:, 2 * D:3 * D], gate_bc[:], (), ["dbg4c"])
                S.barrier()
            if stage <= 1:
                raise _Stop()
            release(mP1)

            def attention_l0(qk_l, qk_r, v_l, scale, GT, chunk, kp, PT, rd, fint):
                its = [(qg, kb) for qg in range(5) for kb in range(NT)]
                n = len(its)

                QSZ = [448, 448, 448, 448, 384]
                QOFF = [0, 448, 896, 1344, 1792]

                def geo(i):
                    qg, kb = its[i]
                    return qg, kb, QOFF[qg], QSZ[qg], qg % 2, i % 2, i % 3

                def st_qk(i):
                    qg, kb, q0, nq, osl, ssl, psl = geo(i)
                    psS = pp[ssl]
                    mm(psS[:, 0:nq], qk_l(0, kb), qk_r(0, q0, nq), True, True, [kp + "K", kp + "Kr", kp + "Q"], [pk(ssl, 0)])
                    mm(psS[:, 512:512 + nq], qk_l(1, kb), qk_r(1, q0, nq), True, True, [kp + "K", kp + "Kr", kp + "Q"], [pk(ssl, 1)])

                def st_exp(i):
                    qg, kb, q0, nq, osl, ssl, psl = geo(i)
                    act(PT[psl][:].rearrange("p (h n) -> p h n", h=2)[:, :, 0:nq],
                        pp[ssl][:].rearrange("p (h n) -> p h n", h=2)[:, :, 0:nq], AF.Exp,
                        [pk(ssl, 0), pk(ssl, 1)], [f"PT{psl}"], scale=scale)

                def st_pv(i):
                    qg, kb, q0, nq, osl, ssl, psl = geo(i)
                    psO = pp[2 + osl]
                    mm(psO[:, 0:nq], v_l(0, kb), PT[psl][:, 0:nq], kb == 0, kb == NT - 1, [kp + "V", f"PT{psl}"], [pk(2 + osl, 0)])
                    mm(psO[:, 512:512 + nq], v_l(1, kb), PT[psl][:, 512:512 + nq], kb == 0, kb == NT - 1,
                       [kp + "V", f"PT{psl}"], [pk(2 + osl, 1)])
                    if kb != NT - 1:
                        return
                    r_ = rd[osl]
                    tmp = fint[osl]
                    rcp(r_[64:128, 0:nq], psO[64:128, 0:nq], [pk(2 + osl, 0)], [f"rd{osl}"])
                    rcp(r_[0:64, 0:nq], psO[0:64, 512:512 + nq], [pk(2 + osl, 1)], [f"rd{osl}"])
                    tt("dve", tmp[0:64, 0:nq], psO[0:64, 0:nq], r_[64:128, 0:nq], ALU.mult, [pk(2 + osl, 0), f"rd{osl}"], [f"fin{osl}"])
                    tt("dve", tmp[64:128, 0:nq], psO[64:128, 512:512 + nq], r_[0:64, 0:nq], ALU.mult,
                       [pk(2 + osl, 1), f"rd{osl}"], [f"fin{osl}"])
                    tt("pool", mixT[:, chunk, q0:q0 + nq], tmp[:, 0:nq], GT[:, q0:q0 + nq], ALU.mult,
                       [f"fin{osl}", kp + "G"], [f"mixT{chunk}_{qg}"])

                import os
                l1_, l2_ = [int(v) for v in os.environ.get("LAG_" + kp, "1,2").split(",")]
                for step in range(n + l2_):
                    if step < n:
                        st_qk(step)
                    if 0 <= step - l1_ < n:
                        st_exp(step - l1_)
                    if 0 <= step - l2_ < n:
                        st_pv(step - l2_)

            def gates_fm(wslice, GT, r, kp):
                for qg in range(5):
                    q0 = qg * 512
                    nq = min(512, TQ - q0)
                    i = qg % 2
                    for c in range(8):
                        mm(pp[i][:, 0:nq], wslice(c), hTq[:, c, q0:q0 + nq], c == 0, c == 7, list(r), [pk(i, 0)])
                    act(GT[:, q0:q0 + nq], pp[i][:, 0:nq], AF.Silu, [pk(i, 0)], [kp + "G"])

            mP2 = mark()
            wA = [sb(f"wA{i}", [128, 8, 256], BF16) for i in range(2)]
            QTA = sb("QTA", [128, TQ], BF16)
            GTA = sb("GTA", [128, TQ], BF16)
            qs = [sb(f"qs{i}", [128, 512], F32) for i in range(3)]
            qbf = [sb(f"qbf{i}", [128, 512], BF16) for i in range(3)]
            PT = [sb(f"PT{i}", [128, 1024], BF16) for i in range(3)]
            rd = [sb(f"rd{i}", [128, 512], F32) for i in range(2)]
            fint = [sb(f"fint{i}", [128, 512], F32) for i in range(2)]
            scr2 = [dict(a=sb(f"s2a{k}", [128, 512], F32), b=sb(f"s2b{k}", [128, 512], F32), c=sb(f"s2c{k}", [128, 512], F32),
                         d=sb(f"s2d{k}", [128, 512], F32), st=sb(f"s2s{k}", [128, 16], F32)) for k in range(2)]
            dma("pool", wA[0][:].rearrange("p c n -> p (c n)"), wA_d[0], (), ["wA0"])
            for g in range(4):
                S.tag = f'p2 g{g}'
                w = wA[g % 2]
                wk = f"wA{g % 2}"
                if g + 1 < 4:
                    dma("pool", wA[(g + 1) % 2][:].rearrange("p c n -> p (c n)"), wA_d[g + 1], (), [f"wA{(g + 1) % 2}"])
                def qa_A(bi, w=w, wk=wk):
                    t0 = bi * 4
                    nt = min(4, NQ - t0)
                    i = (bi + 1) % 2
                    pj = pp[2 + i]
                    for tl in range(nt):
                        t = t0 + tl
                        for c in range(8):
                            mm(pj[:, tl * 128:(tl + 1) * 128], hTq[:, c, t * 128:(t + 1) * 128], w[:, c, 0:128],
                               c == 0, c == 7, [wk], [pk(2 + i, 0)])
                    cp("act", qs[bi % 3][:, 0:nt * 128], pj[:, 0:nt * 128], [pk(2 + i, 0)], [f"qs{bi % 3}"])

                def qa_G(qg, w=w, wk=wk):
                    q0 = qg * 512
                    nq = min(512, TQ - q0)
                    i = qg % 2
                    for c in range(8):
                        mm(pp[i][:, 0:nq], w[:, c, 128:256], hTq[:, c, q0:q0 + nq], c == 0, c == 7, [wk], [pk(i, 0)])
                    act(GTA[:, q0:q0 + nq], pp[i][:, 0:nq], AF.Silu, [pk(i, 0)], ["AG"])

                def qa_B(bi):
                    t0 = bi * 4
                    nt = min(4, NQ - t0)
                    i = (bi + 1) % 2
                    u = bi % 3
                    norm_rope64(qs[u][:, 0:nt * 128], nt, 2, aqn, tabA[:, t0:t0 + nt, 0:64], tabA[:, t0:t0 + nt, 64:128],
                                qbf[u][:, 0:nt * 128], scr2[bi % 2], [f"qs{u}"], [f"qbf{u}"], f"qa{bi % 2}", e2="dve")
                    pS = pp[2 + i][:, 512:1024].bitcast(BF16)
                    for tl in range(nt):
                        tp(pS[:, tl * 128:(tl + 1) * 128], qbf[u][:, tl * 128:(tl + 1) * 128], [f"qbf{u}"], [pk(2 + i, 1)])
                    cp("dve", QTA[:, t0 * 128:(t0 + nt) * 128], pS[:, 0:nt * 128], [pk(2 + i, 1)], ["AQ"])

                for step in range(7):
                    if step < 5:
                        qa_A(step)
                        qa_G(step)
                    if 0 <= step - 2 < 5:
                        qa_B(step - 2)
                attention_l0(lambda h, kb: KTA[64 * h:64 * h + 64, kb * 128:(kb + 1) * 128],
                             lambda h, q0, nq: QTA[64 * h:64 * h + 64, q0:q0 + nq],
                             lambda h, kb: VA[:, kb, 64 * h:64 * h + 128],
                             SCALE_A, GTA, g, "A", PT, rd, fint)
            S.barrier()
            if stage <= 2:
                raise _Stop()
            release(mLA)

            wuq = sb("wuq", [128, 2, 768], BF16)
            wuk = sb("wuk", [128, 512], BF16)
            wuv = sb("wuv", [128, 512], BF16)
            wgb = [sb(f"wgb{i}", [128, 8, 128], BF16) for i in range(2)]
            KTB = [sb(f"KTB{i}", [128, SEQ], BF16) for i in range(2)]
            QTB = [sb(f"QTB{i}", [128, TQ], BF16) for i in range(2)]
            VB = sb("VB", [128, NT, 192], BF16)
            GTB = sb("GTB", [128, TQ], BF16)
            qs3 = [sb(f"qs3_{i}", [128, 384], F32) for i in range(3)]
            qb3 = [sb(f"qb3_{i}", [128, 384], BF16) for i in range(3)]
            PT = [sb(f"PTb{i}", [128, 1024], BF16) for i in range(3)]
            rd = [sb(f"rdb{i}", [128, 512], F32) for i in range(2)]
            fint = [sb(f"fintb{i}", [128, 512], F32) for i in range(2)]
            s3 = [[sb(f"s3_{i}{k}", [128, 64], F32) for k in range(4)] for i in range(3)]
            dma("pool", wuq[:].rearrange("p c n -> p (c n)"), wuq_d, (), ["wuq"])
            dma("pool", wuk[:], wuk_d, (), ["wuk"])
            dma("pool", wuv[:], wuv_d, (), ["wuv"])
            dma("pool", wgb[0][:].rearrange("p c n -> p (c n)"), wgb_d[0], (), ["wgb0"])
            mset("pool", VB[:], 1.0, ["VB_init"])
            for hh_ in range(2):
                dma("sp", KTB[hh_][64:96, :], kropeT[64:96, :], (), ["BKr"])
            for p in range(4):
                S.tag = f'p3 p{p}'
                wg = wgb[p % 2]
                wgk = f"wgb{p % 2}"
                if p + 1 < 4:
                    dma("pool", wgb[(p + 1) % 2][:].rearrange("p c n -> p (c n)"), wgb_d[p + 1], (), [f"wgb{(p + 1) % 2}"])
                def kb_unit(u_, p=p):
                    hh, kg = u_ // 8, u_ % 8
                    h = 2 * p + hh
                    i = (kg + 1) % 2
                    mm(pp[2 + i][0:64, 0:512], wuk[:, h * 64:(h + 1) * 64], ckvnT[:, kg * 512:(kg + 1) * 512],
                       True, True, ["wuk"], [pk(2 + i, 0)])
                    cp("act" if kg % 2 == 0 else "dve", KTB[hh][0:64, kg * 512:(kg + 1) * 512], pp[2 + i][0:64, 0:512],
                       [pk(2 + i, 0)], ["BK"])

                def vb_unit(kg, p=p):
                    i = (kg + 1) % 2
                    for tl in range(4):
                        kb = kg * 4 + tl
                        mm(pp[2 + i][:, 512 + tl * 128:512 + (tl + 1) * 128], ckvnT[:, kb * 128:(kb + 1) * 128],
                           wuv[:, p * 128:(p + 1) * 128], True, True, ["wuv"], [pk(2 + i, 1)])
                    cp("dve" if kg % 2 == 0 else "act",
                       VB[:, kg * 4:(kg + 1) * 4, :].rearrange("p t (a b) -> p t a b", b=64)[:, :, 0:3:2, :],
                       pp[2 + i][:, 512:1024].rearrange("p (t a b) -> p t a b", t=4, a=2), [pk(2 + i, 1), "VB_init"], ["BV"])

                def qb_A(bi, p=p):
                    t0 = bi * 2
                    nt = min(2, NQ - t0)
                    i = bi % 2
                    pj = pp[i]
                    for tl in range(nt):
                        t = t0 + tl
                        for cc in range(2):
                            mm(pj[:, tl * 192:(tl + 1) * 192], cqnT[:, cc, t * 128:(t + 1) * 128],
                               wuq[:, cc, p * 192:(p + 1) * 192], cc == 0, cc == 1, ["wuq"], [pk(i, 0)])
                    cp("act", qs3[bi % 3][:, 0:nt * 192], pj[:, 0:nt * 192], [pk(i, 0)], [f"qs3{bi % 3}"])

                def qb_B(bi):
                    t0 = bi * 2
                    nt = min(2, NQ - t0)
                    i = bi % 2
                    u = bi % 3
                    W = nt * 192
                    v_s = qs3[u][:, 0:W].rearrange("p (a d) -> p a d", d=96)
                    v_d = qb3[u][:, 0:W].rearrange("p (a d) -> p a d", d=96)
                    cp("pool", v_d[:, :, 0:64], v_s[:, :, 0:64], [f"qs3{u}"], [f"qb3n{u}"])
                    rk = [f"qb3n{u}"]
                    for tl in range(nt):
                        t = t0 + tl
                        rope32(v_s[:, 2 * tl:2 * tl + 2, 64:96], 2,
                               tabB[:, t:t + 1, 0:32].broadcast_to([128, 2, 32]), tabB[:, t:t + 1, 32:64].broadcast_to([128, 2, 32]),
                               v_d[:, 2 * tl:2 * tl + 2, 64:96], (s3[u][2 * tl], s3[u][2 * tl + 1]),
                               [f"qs3{u}"], [f"qb3r{u}_{tl}"], f"qr{u}_{tl}")
                        rk.append(f"qb3r{u}_{tl}")
                    pS = pp[i][:, 512:1024].bitcast(BF16)
                    for tl in range(nt):
                        for hh in range(2):
                            tp(pS[0:96, (tl * 2 + hh) * 192:(tl * 2 + hh) * 192 + 128],
                               qb3[u][:, tl * 192 + hh * 96:tl * 192 + (hh + 1) * 96], rk, [pk(i, 1)])
                    for hh in range(2):
                        cp("dve", QTB[hh][0:96, t0 * 128:(t0 + nt) * 128].rearrange("p (t n) -> p t n", t=nt),
                           pS[0:96, 0:nt * 384].rearrange("p (t h n) -> p t h n", t=nt, h=2)[:, :, hh, 0:128], [pk(i, 1)], ["BQ"])

                for step in range(11):
                    if step < 9:
                        qb_A(step)
                    for u_ in (2 * step, 2 * step + 1):
                        if u_ < 16:
                            kb_unit(u_)
                    if step < 8:
                        vb_unit(step)
                    if 0 <= step - 2 < 9:
                        qb_B(step - 2)
                gates_fm(lambda c, wg=wg: wg[:, c, :], GTB, [wgk], "B")
                attention_l0(lambda h, kb: KTB[h][0:96, kb * 128:(kb + 1) * 128],
                             lambda h, q0, nq: QTB[h][0:96, q0:q0 + nq],
                             lambda h, kb: VB[:, kb, 64 * h:64 * h + 128],
                             SCALE_B, GTB, 4 + p, "B", PT, rd, fint)
            S.barrier()
            if stage <= 3:
                raise _Stop()
            release(mL0)

            x1 = sb("x1", [128, NQ, D], F32)
            mP4 = mark()
            wo = sb("wo", [128, 8, D], BF16)
            xt4 = [sb(f"xt4_{i}", [128, D], F32) for i in range(2)]
            yg = [sb(f"yg{i}", [128, D], F32) for i in range(2)]
            dma("pool", wo[:].rearrange("p c n -> p (c n)"), wout0_d, (), ["wo"])
            for t in range(NQ):
                S.tag = f'p4 t{t}'
                s = t % 2
                dma("sp", xt4[s][:], x_d[t * 128:(t + 1) * 128, :], (), [f"xt4{s}"])
                for half in range(2):
                    for c in range(8):
                        mm(pp[2 * s + half][:, 0:512], mixT[:, c, t * 128:(t + 1) * 128], wo[:, c, half * 512:(half + 1) * 512],
                           c == 0, c == 7, ["wo"], [pk(2 * s + half, 0)])
                    tt("dve", yg[s][:, half * 512:(half + 1) * 512], pp[2 * s + half][:, 0:512], gate_bc[:, half * 512:(half + 1) * 512],
                       ALU.mult, [pk(2 * s + half, 0)], [f"yg{s}_{half}"])
                tt("dve", x1[:, t, :], yg[s][:], xt4[s][:], ALU.add, [f"yg{s}_0", f"yg{s}_1", f"xt4{s}"], [f"x1_{t}"])
                if debug:
                    dma("sp", dbg_d[t * 128:(t + 1) * 128, :], x1[:, t, :], [f"x1_{t}"], [f"dbg{t}"])
            S.barrier()
            if stage <= 4:
                raise _Stop()
            release(mP4)

            mL1 = mark()
            h1T = sb("h1T", [128, 8, TQ], BF16)
            K1T = sb("K1T", [128, 2, TQ], BF16)
            V1 = sb("V1", [128, NQ, 384], BF16)
            dtab = sb("dtab", [128, 384], F32)
            esink = sb("esink", [128, 16], F32)
            mL1a = mark()
            g_bc = sb("g_bc1", [128, D], F32)
            sh_bc = sb("sh_bc1", [128, D], F32)
            emit_mods(1, g_bc, sh_bc, 256)
            xs1 = [sb(f"xs1_{i}", [128, D], BF16) for i in range(2)]
            xm1 = [sb(f"xm1_{i}", [128, D], F32) for i in range(2)]
            wkv1 = sb("wkv1", [128, 8, 512], BF16)
            dma("pool", wkv1[:].rearrange("p c n -> p (c n)"), wkv1_d, (), ["wkv1"])
            dma("sp", dtab[:], dtab_d, (), ["dtab"])
            dma("sp", esink[:], sink_d.partition_broadcast(128), (), ["esink"])
            act(esink[:], esink[:], AF.Exp, ["esink"], ["esink"])
            mset("pool", V1[:], 1.0, ["V1_init"])

            def l1_pre(t):
                s = t % 2
                hT_pre(x1[:, t, :], [], s, xm1[s][:], f"xm1{s}", xs1[s][:], g_bc, sh_bc)

            def l1_tr(t):
                s = t % 2
                hT_tr(s, xs1[s][:], h1T[:, :, t * 128:(t + 1) * 128], [f"h1T{t}"])

            def l1_v(t):
                s = t % 2
                pj = pp[2 * s + 1]
                for c in range(8):
                    mm(pj[:, 0:256], h1T[:, c, t * 128:(t + 1) * 128], wkv1[:, c, 256:512], c == 0, c == 7,
                       [f"h1T{t}", "wkv1"], [pk(2 * s + 1, 0)])
                cp("dve", V1[:, t, :].rearrange("p (j a b) -> p j a b", j=2, a=3)[:, :, 0:3:2, :],
                   pj[:, 0:256].rearrange("p (j a b) -> p j a b", j=2, a=2), [pk(2 * s + 1, 0), "V1_init"], ["V1"])

            for step in range(NQ + 2):
                if step < NQ:
                    l1_pre(step)
                if 0 <= step - 1 < NQ:
                    l1_tr(step - 1)
                if 0 <= step - 2 < NQ:
                    l1_v(step - 2)
            for j in range(2):
                for qg in range(5):
                    q0 = qg * 512
                    nq = min(512, TQ - q0)
                    i = (j * 5 + qg) % 2
                    for c in range(8):
                        mm(pp[i][:, 0:nq], wkv1[:, c, j * 128:(j + 1) * 128], h1T[:, c, q0:q0 + nq], c == 0, c == 7,
                           ["wkv1"] + [f"h1T{t}" for t in range(q0 // 128, (q0 + nq) // 128)], [pk(i, 0)])
                    cp("act" if qg % 2 == 0 else "dve", K1T[:, j, q0:q0 + nq], pp[i][:, 0:nq], [pk(i, 0)], ["K1T"])
            S.barrier()
            if stage <= 5:
                raise _Stop()
            release(mL1a)

            wqg1 = [sb(f"wqg1_{i}", [128, 8, 256], BF16) for i in range(2)]
            Q1T = sb("Q1T", [128, TO], BF16)
            G1T = sb("G1T", [128, TO], BF16)
            Etab = sb("Etab", [128, 16, 384], BF16)
            esb = sb("esb", [128, 1], F32)
            PT1 = [sb(f"PT1_{i}", [128, 768], BF16) for i in range(4)]
            rd1 = [sb("rd1_0", [128, 512], F32)] * 2
            tm1 = [sb("tm1_0", [128, 512], F32)] * 2
            dma("pool", wqg1[0][:].rearrange("p c n -> p (c n)"), wqg1_d[0], (), ["wqg0"])

            slopes = [2.0 ** (-8.0 * (h + 1) / 16.0) for h in range(16)]
            for h_ in range(16):
                act(Etab[:, h_, :], dtab[:], AF.Exp, ["dtab"], ["Etab"], scale=-slopes[h_])
            for P_ in range(8):
                S.tag = f'L1 P{P_}'
                j, g = P_ // 4, P_ % 4
                heads = [(2 * j) * 4 + g, (2 * j + 1) * 4 + g]
                w = wqg1[P_ % 2]
                wk = f"wqg{P_ % 2}"
                if P_ + 1 < 8:
                    dma("pool", wqg1[(P_ + 1) % 2][:].rearrange("p c n -> p (c n)"), wqg1_d[P_ + 1], (), [f"wqg{(P_ + 1) % 2}"])
                cp("pool", esb[64:128, 0:1], esink[64:128, heads[0]:heads[0] + 1], ["esink"], ["esb"])
                cp("pool", esb[0:64, 0:1], esink[0:64, heads[1]:heads[1] + 1], ["esink"], ["esb"])
                for qg in range(4):
                    q0 = qg * 512
                    i = qg % 2
                    for c in range(8):
                        mm(pp[i][:, 0:512], w[:, c, 0:128], h1T[:, c, q0:q0 + 512], c == 0, c == 7, [wk], [pk(i, 0)])
                    cp("dve", Q1T[:, q0:q0 + 512], pp[i][:, 0:512], [pk(i, 0)], ["Q1T"])
                    for c in range(8):
                        mm(pp[i][:, 512:1024], w[:, c, 128:256], h1T[:, c, q0:q0 + 512], c == 0, c == 7, [wk], [pk(i, 1)])
                    act(G1T[:, q0:q0 + 512], pp[i][:, 512:1024], AF.Silu, [pk(i, 1)], ["G1T"])
                its1 = []
                for G in range(4):
                    jb0 = G * 4
                    kbs = list(range(max(jb0 - 1, 0), min(jb0 + 4, 16) + 1))
                    for idx, kb in enumerate(kbs):
                        qlo = max(kb - 1, jb0)
                        qhi = min(kb + 1, jb0 + 3)
                        its1.append(dict(G=G, kb=kb, first=(idx == 0), last=(idx == len(kbs) - 1), qlo=qlo,
                                         nq=(qhi - qlo + 1) * 128, d0=(qlo - (kb - 1)) * 128, lo=(qlo - jb0) * 128))
                n1 = len(its1)

                def l1_qk(i):
                    it = its1[i]
                    ssl = i % 2
                    nq, kb, qlo = it["nq"], it["kb"], it["qlo"]
                    for hh in range(2):
                        mm(pp[ssl][:, hh * 512:hh * 512 + nq], K1T[64 * hh:64 * hh + 64, j, kb * 128:(kb + 1) * 128],
                           Q1T[64 * hh:64 * hh + 64, qlo * 128:qlo * 128 + nq], True, True, ["Q1T"], [pk(ssl, hh)])

                def l1_exp(i):
                    it = its1[i]
                    ssl = i % 2
                    psl = i % 4
                    nq = it["nq"]
                    act(PT1[psl][:].rearrange("p (h n) -> p h n", h=2)[:, :, 0:nq],
                        pp[ssl][:].rearrange("p (h n) -> p h n", h=2)[:, :, 0:nq], AF.Exp,
                        [pk(ssl, 0), pk(ssl, 1)], [f"PT1_{psl}_0", f"PT1_{psl}_1"], scale=SCALE_C)

                def l1_bias(i):
                    it = its1[i]
                    ssl = i % 4
                    nq, d0 = it["nq"], it["d0"]
                    for hh in range(2):
                        tt("dve", PT1[ssl][:, hh * 384:hh * 384 + nq], PT1[ssl][:, hh * 384:hh * 384 + nq],
                           Etab[:, heads[hh], d0:d0 + nq], ALU.mult, [f"PT1_{ssl}_{hh}", "Etab"], [f"PT1_{ssl}_{hh}"])

                def l1_pv(i):
                    it = its1[i]
                    ssl = i % 4
                    osl = it["G"] % 2
                    psO = pp[2 + osl]
                    nq, kb, lo = it["nq"], it["kb"], it["lo"]
                    for hh in range(2):
                        mm(psO[:, hh * 512 + lo:hh * 512 + lo + nq], V1[:, kb, j * 192 + 64 * hh:j * 192 + 64 * hh + 128],
                           PT1[ssl][:, hh * 384:hh * 384 + nq], it["first"], it["last"], [f"PT1_{ssl}_{hh}"], [pk(2 + osl, hh)],
                           skip=True)
                    if not it["last"]:
                        return
                    r_ = rd1[osl]
                    tmp = tm1[osl]
                    q0 = it["G"] * 512
                    act(r_[64:128, :], psO[64:128, 0:512], AF.Ln, [pk(2 + osl, 0), "esb"], ["rd1a"], bias=esb[64:128, 0:1])
                    act(r_[0:64, :], psO[0:64, 512:1024], AF.Ln, [pk(2 + osl, 1), "esb"], ["rd1b"], bias=esb[0:64, 0:1])
                    act(r_[:, :], r_[:, :], AF.Exp, ["rd1a", "rd1b"], ["rd1"], scale=-1.0)
                    tt("dve", tmp[0:64, :], psO[0:64, 0:512], r_[64:128, :], ALU.mult, [pk(2 + osl, 0), "rd1"], ["tm1"])
                    tt("dve", tmp[64:128, :], psO[64:128, 512:1024], r_[0:64, :], ALU.mult, [pk(2 + osl, 1), "rd1"], ["tm1"])
                    tt("pool", mixT[:, P_, q0:q0 + 512], tmp[:, :], G1T[:, q0:q0 + 512], ALU.mult, ["tm1", "G1T"], ["mix1"])

                for step in range(n1 + 3):
                    if 0 <= step - 3 < n1:
                        l1_pv(step - 3)
                    if step < n1:
                        l1_qk(step)
                    if 0 <= step - 1 < n1:
                        l1_exp(step - 1)
                    if 0 <= step - 2 < n1:
                        l1_bias(step - 2)
            S.barrier()
            if stage <= 6:
                raise _Stop()
            release(mL1)

            wo1 = sb("wo1", [128, 8, D], BF16)
            fn_bc = sb("fn_bc", [128, D], F32)
            yg = [sb(f"yg1_{i}", [128, D], F32) for i in range(2)]
            x2 = [sb(f"x2_{i}", [128, D], F32) for i in range(2)]
            ot = [sb(f"ot{i}", [128, D], F32) for i in range(2)]
            dma("pool", wo1[:].rearrange("p c n -> p (c n)"), wout1_d, (), ["wo1"])
            dma("sp", fn_bc[:], fnorm_d.partition_broadcast(128), (), ["fn_bc"])
            def fin_A(t):
                s = t % 2
                for half in range(2):
                    for c in range(8):
                        mm(pp[2 * s + half][:, 0:512], mixT[:, c, t * 128:(t + 1) * 128], wo1[:, c, half * 512:(half + 1) * 512],
                           c == 0, c == 7, ["wo1"], [pk(2 * s + half, 0)])
                    tt("dve", yg[s][:, half * 512:(half + 1) * 512], pp[2 * s + half][:, 0:512], gate_bc[:, half * 512:(half + 1) * 512],
                       ALU.mult, [pk(2 * s + half, 0)], [f"yg{s}_{half}"])
                tt("dve", x2[s][:], yg[s][:], x1[:, t, :], ALU.add, [f"yg{s}_0", f"yg{s}_1"], [f"x2_{s}"])
                ss = stat[:, 16 + s:17 + s]
                act(junk[:], x2[s][:], AF.Square, [f"x2_{s}"], ["junk", f"fss{s}"], accum=ss)
                rstd(ss, ss, float(D), [f"fss{s}"], [f"fss{s}"])

            def fin_B(t):
                s = t % 2
                ss = stat[:, 16 + s:17 + s]
                stt("dve", ot[s][:], x2[s][:], ss, fn_bc[:], ALU.mult, ALU.mult, [f"x2_{s}", f"fss{s}", "fn_bc"], [f"ot{s}"])
                dma("sp", y_d[t * 128:(t + 1) * 128, :], ot[s][:], [f"ot{s}"], [f"y{t}"])
                okeys.append(f"y{t}")

            for step in range(17):
                if step < 16:
                    fin_A(step)
                if step >= 1:
                    fin_B(step - 1)
        except _Stop:
            pass
        print("SBUF arena peak words:", A["peak"], "of", ARENA_WORDS)
        S.emit(nc, final_wait_keys=okeys)
    nc._sched = S
    return nc


def _rope_cs(pos, dim):
    inv = (np.float32(10000.0) ** (-(np.arange(0, dim, 2, dtype=np.float32)) / np.float32(dim))).astype(np.float32)
    ang = pos.astype(np.float32)[:, None] * inv[None, :]
    return np.cos(ang).astype(np.float32), np.sin(ang).astype(np.float32)


def _tables(tok):
    row = tok // 64
    col = tok % 64
    cr, sr = _rope_cs(row, 32)
    cc, sc_ = _rope_cs(col, 32)
    ct, st_ = _rope_cs(tok, 32)
    cosA = np.concatenate([cr, cr, cc, cc], axis=1)
    sinA = np.concatenate([-sr, sr, -sc_, sc_], axis=1)
    cosB = np.concatenate([ct, ct], axis=1)
    sinB = np.concatenate([-st_, st_], axis=1)

    tab = np.concatenate([cosA, sinA, cosB, sinB], axis=1).astype(np.float32)
    return np.ascontiguousarray(tab.reshape(NT, 128, 192))


def _dtab():
    s = np.arange(128)[:, None]
    q = np.arange(384)[None, :] - 128
    d = np.abs(q - s).astype(np.float32)
    return np.where(d <= 128, d, np.float32(BIGD)).astype(np.float32)


def _prep(inputs):
    f = lambda a: np.ascontiguousarray(np.asarray(a, dtype=np.float32))
    x = f(inputs["x"])
    c = f(inputs["c"])
    w_in = f(inputs["even_w_in"])[0]
    qa, ka, va, ga = w_in[:, 0:512], w_in[:, 512:640], w_in[:, 640:768], w_in[:, 768:1280]
    cq, ckv, kr, gb = w_in[:, 1280:1536], w_in[:, 1536:1664], w_in[:, 1664:1696], w_in[:, 1696:2208]
    wk0 = np.concatenate([ka, va, ckv, kr, cq], axis=1)
    permA = [np.concatenate([np.arange((0 * 4 + g) * 64, (0 * 4 + g) * 64 + 64), np.arange((1 * 4 + g) * 64, (1 * 4 + g) * 64 + 64)])
             for g in range(4)]
    wA = np.stack([np.concatenate([qa[:, permA[g]], ga[:, permA[g]]], axis=1) for g in range(4)])
    wgb = np.stack([gb[:, p * 128:(p + 1) * 128] for p in range(4)])
    wout = f(inputs["even_w_out"])[0]
    rows0 = np.concatenate([np.concatenate(permA), np.arange(512, 1024)])
    wout0 = wout[rows0, :]
    w1 = f(inputs["odd_w_in"])[0]
    qc, kc, vc, gc = w1[:, 0:1024], w1[:, 1024:1280], w1[:, 1280:1536], w1[:, 1536:2560]
    perm1 = []
    for P_ in range(8):
        j, g = P_ // 4, P_ % 4
        hA, hB = (2 * j) * 4 + g, (2 * j + 1) * 4 + g
        perm1.append(np.concatenate([np.arange(hA * 64, hA * 64 + 64), np.arange(hB * 64, hB * 64 + 64)]))
    wqg1 = np.stack([np.concatenate([qc[:, perm1[P_]], gc[:, perm1[P_]]], axis=1) for P_ in range(8)])
    wkv1 = np.concatenate([kc, vc], axis=1)
    wout1 = f(inputs["odd_w_out"])[0][np.concatenate(perm1), :]
    def pm(w):
        cN = w.shape[0] // 128
        return np.ascontiguousarray(w.reshape(cN, 128, w.shape[1]).transpose(1, 0, 2).reshape(128, cN * w.shape[1]))
    adaw = f(inputs["ada_w"]).reshape(2, 8, 128, 12, 256).transpose(0, 3, 2, 1, 4).reshape(2, 12, 128, 8 * 256)
    shared = {
        "ada_w": np.ascontiguousarray(adaw), "ada_b": f(inputs["ada_b"]), "norm_w": f(inputs["norm_w"]),
        "final_norm": f(inputs["final_norm"]), "ident": np.eye(128, dtype=np.float32),
        "wk0": pm(wk0), "wA": np.stack([pm(wA[i]) for i in range(4)]), "wgb": np.stack([pm(wgb[i]) for i in range(4)]),
        "wuq": pm(f(inputs["b_w_uq"])[0]), "wuk": np.ascontiguousarray(f(inputs["b_w_uk"])[0].reshape(128, 512)),
        "wuv": np.ascontiguousarray(f(inputs["b_w_uv"])[0].reshape(128, 512)),
        "wout0": pm(wout0),
        "aqn": f(inputs["a_q_norm"])[0], "akn": f(inputs["a_k_norm"])[0],
        "bqn": f(inputs["b_q_lora_norm"])[0], "bkvn": f(inputs["b_kv_lora_norm"])[0],
        "wkv1": pm(wkv1), "wqg1": np.stack([pm(wqg1[i]) for i in range(8)]), "wout1": pm(wout1),
        "sink": f(inputs["c_sink"])[0], "dtab": _dtab(),
    }
    tabs = [_tables(np.arange(SEQ)), _tables(SEQ - 1 - np.arange(SEQ))]
    in_maps = []
    for core in range(8):
        b, hf = core // 2, core % 2
        xl = x[b] if hf == 0 else x[b][::-1]
        m = dict(shared)
        m["x_loc"] = np.ascontiguousarray(xl)
        m["c_col"] = np.ascontiguousarray(c[b].reshape(8, 128).T)
        m["tab"] = tabs[hf]
        tq = tabs[hf][0:NQ]
        m["tabqa"] = np.ascontiguousarray(tq[:, :, 0:128].transpose(1, 0, 2).reshape(128, NQ * 128))
        m["tabqb"] = np.ascontiguousarray(tq[:, :, 128:192].transpose(1, 0, 2).reshape(128, NQ * 64))
        in_maps.append(m)
    return in_maps


_NC_CACHE = {}


def kernel(**inputs):
    debug = bool(inputs.pop("_debug", False))
    in_maps = _prep(inputs)
    if debug not in _NC_CACHE:
        _NC_CACHE[debug] = build_program(debug)
    nc = _NC_CACHE[debug]
    res = run_bass_kernel_spmd(nc, in_maps, core_ids=list(range(8)))
    out = np.empty((4, SEQ, D), dtype=np.float32)
    for core in range(8):
        b, hf = core // 2, core % 2
        y = np.asarray(res.results[core]["y"], dtype=np.float32)
        if hf == 0:
            out[b, 0:TO] = y
        else:
            out[b, TO:SEQ] = y[::-1]
    if debug:
        return out, [np.asarray(res.results[core]["dbg"]) for core in range(8)], [dict(h=np.asarray(res.results[core]["dbg2"]).astype(np.float32).reshape(128, 8, TQ), k=np.asarray(res.results[core]["dbg3"]).astype(np.float32), m=np.asarray(res.results[core]["dbg4"])) for core in range(8)]
    return out
```
